# Optimizing a Trainium2 kernel written in Bass

```python
import math
import jax, jax.numpy as jnp
from jax import lax
import numpy as np

D_MODEL = 1024
BATCH = 8
SEQ = 8192
DEPTH = 2

SSM_WIDTH = 512
SSM_GROUP = 16
SSM_GROUPS = SSM_WIDTH // SSM_GROUP
SSM_STATE = 64
DT_MIN = 1e-3
DT_MAX = 1e-1
HEAD_DIM = 128
N_HEADS = D_MODEL // HEAD_DIM
N_KV_HEADS = 2
ATTN_WIDTH = N_HEADS * HEAD_DIM
KV_WIDTH = N_KV_HEADS * HEAD_DIM
IDX_HEADS = 8
IDX_DIM = 64
IDX_SCALE = (IDX_HEADS * IDX_DIM) ** -0.5
TOPK_MAX = 256
Q_BLOCK = 128
ROPE_THETA = 10000.0
D_FF = -(-8 * D_MODEL // (3 * 256)) * 256
DEEPNORM_ALPHA = (2 * DEPTH) ** 0.25
DEEPNORM_BETA = (8 * DEPTH) ** -0.25
LN_EPS = 1e-5
IN_SIZES = (SSM_WIDTH, ATTN_WIDTH, KV_WIDTH, KV_WIDTH, IDX_HEADS * IDX_DIM, IDX_DIM, IDX_HEADS, D_MODEL, D_MODEL)
IN_OFFSETS = [int(o) for o in np.cumsum(IN_SIZES)[:-1]]
D_IN = int(sum(IN_SIZES))

kernel_name = 'hybrid_s5_dsa_deepnorm_block'


def layer_norm(x, g, b):
    xf = x.astype(jnp.float32)
    mu = jnp.mean(xf, axis=-1, keepdims=True)
    var = jnp.mean(jnp.square(xf - mu), axis=-1, keepdims=True)
    y = (xf - mu) * lax.rsqrt(var + LN_EPS) * g.astype(jnp.float32) + b.astype(jnp.float32)
    return y.astype(x.dtype)


def rope(x, pos):
    half = x.shape[-1] // 2
    inv = ROPE_THETA ** (-jnp.arange(half, dtype=jnp.float32) / half)
    ang = pos.astype(jnp.float32)[:, None] * inv[None, :]
    cos = jnp.cos(ang)[:, None, :]
    sin = jnp.sin(ang)[:, None, :]
    xf = x.astype(jnp.float32)
    x1, x2 = xf[..., :half], xf[..., half:]
    out = jnp.concatenate([x1 * cos - x2 * sin, x2 * cos + x1 * sin], axis=-1)
    return out.astype(x.dtype)


def s5_branch(u, lam_re, lam_im, log_dt, b_re, b_im, c_re, c_im, d_skip, w_glu, b_glu):
    bsz, L, _ = u.shape
    f32 = jnp.float32
    uf = u.astype(f32).reshape(bsz, L, SSM_GROUPS, SSM_GROUP)
    dt = jnp.exp(log_dt.astype(f32))[:, None]
    lr, li = lam_re.astype(f32), lam_im.astype(f32)
    mag = jnp.exp(lr * dt)
    ar = mag * jnp.cos(li * dt)
    ai = mag * jnp.sin(li * dt)
    den = lr * lr + li * li
    nr = ar - 1.0
    fr = (nr * lr + ai * li) / den
    fi = (ai * lr - nr * li) / den
    br, bi = b_re.astype(f32), b_im.astype(f32)
    bbr = fr[..., None] * br - fi[..., None] * bi
    bbi = fr[..., None] * bi + fi[..., None] * br
    xr = jnp.einsum('bsgi,gpi->bsgp', uf, bbr)
    xi = jnp.einsum('bsgi,gpi->bsgp', uf, bbi)
    a_r = jnp.broadcast_to(ar, (1, L) + ar.shape)
    a_i = jnp.broadcast_to(ai, (1, L) + ai.shape)

    def combine(e1, e2):
        a1r, a1i, b1r, b1i = e1
        a2r, a2i, b2r, b2i = e2
        return (a1r * a2r - a1i * a2i,
                a1r * a2i + a1i * a2r,
                a2r * b1r - a2i * b1i + b2r,
                a2r * b1i + a2i * b1r + b2i)

    _, _, hr, hi = lax.associative_scan(combine, (a_r, a_i, xr, xi), axis=1)
    y = (jnp.einsum('bsgp,gip->bsgi', hr, c_re.astype(f32))
         - jnp.einsum('bsgp,gip->bsgi', hi, c_im.astype(f32))
         + d_skip.astype(f32).reshape(SSM_GROUPS, SSM_GROUP) * uf)
    y = jax.nn.gelu(y.reshape(bsz, L, SSM_WIDTH))
    y = y * jax.nn.sigmoid(y @ w_glu.astype(f32) + b_glu.astype(f32))
    return y.astype(u.dtype)


def dsa_branch(q, k, v, q_idx, k_idx, w_idx):
    bsz, L = q.shape[0], q.shape[1]
    n_sel = min(TOPK_MAX, L // 4)
    nblk = L // Q_BLOCK
    grp = N_HEADS // N_KV_HEADS
    key_pos = jnp.arange(L)
    bidx = jnp.arange(bsz)[:, None, None]

    def to_blocks(t):
        return jnp.swapaxes(t.reshape((bsz, nblk, Q_BLOCK) + t.shape[2:]), 0, 1)

    def block_fn(args):
        qb, qib, wb, start = args
        q_pos = start + jnp.arange(Q_BLOCK)
        causal = key_pos[None, :] <= q_pos[:, None]
        rel = jax.nn.relu(jnp.einsum('bqhd,bsd->bqhs', qib, k_idx).astype(jnp.float32))
        score = jnp.einsum('bqhs,bqh->bqs', rel, wb.astype(jnp.float32) * IDX_SCALE)
        score = jnp.where(causal[None], score, -jnp.inf)
        _, sel = lax.top_k(score, n_sel)
        valid = sel <= q_pos[None, :, None]
        ks = k[bidx, sel]
        vs = v[bidx, sel]
        qg = qb.reshape(bsz, Q_BLOCK, N_KV_HEADS, grp, HEAD_DIM)
        logits = jnp.einsum('bqhgd,bqnhd->bqhgn', qg, ks).astype(jnp.float32) * (HEAD_DIM ** -0.5)
        logits = jnp.where(valid[:, :, None, None, :], logits, -jnp.inf)
        p = jax.nn.softmax(logits, axis=-1).astype(vs.dtype)
        o = jnp.einsum('bqhgn,bqnhd->bqhgd', p, vs)
        return o.reshape(bsz, Q_BLOCK, ATTN_WIDTH)

    starts = jnp.arange(nblk) * Q_BLOCK
    out = lax.map(block_fn, (to_blocks(q), to_blocks(q_idx), to_blocks(w_idx), starts))
    return jnp.swapaxes(out, 0, 1).reshape(bsz, L, ATTN_WIDTH)


def hybrid_mixer(u, pos, w_in, lam_re, lam_im, log_dt, b_re, b_im, c_re, c_im, d_skip,
                 w_glu, b_glu, p_ssm, p_attn, w_out):
    bsz, L, _ = u.shape
    z = u @ w_in
    u_ssm, q, k, v, q_idx, k_idx, w_idx, g_ssm, g_attn = jnp.split(z, IN_OFFSETS, axis=-1)
    y_ssm = s5_branch(u_ssm, lam_re, lam_im, log_dt, b_re, b_im, c_re, c_im, d_skip, w_glu, b_glu)
    q = rope(q.reshape(bsz, L, N_HEADS, HEAD_DIM), pos)
    k = rope(k.reshape(bsz, L, N_KV_HEADS, HEAD_DIM), pos)
    v = v.reshape(bsz, L, N_KV_HEADS, HEAD_DIM)
    q_idx = rope(q_idx.reshape(bsz, L, IDX_HEADS, IDX_DIM), pos)
    k_idx = rope(k_idx.reshape(bsz, L, 1, IDX_DIM), pos)[:, :, 0]
    y_attn = dsa_branch(q, k, v, q_idx, k_idx, w_idx)
    merged = jax.nn.sigmoid(g_ssm) * (y_ssm @ p_ssm) + jax.nn.sigmoid(g_attn) * (y_attn @ p_attn)
    return merged @ w_out


def swiglu_ffn(u, w_gate_up, w_down):
    a, b = jnp.split(u @ w_gate_up, 2, axis=-1)
    return (jax.nn.silu(a) * b) @ w_down


def setup_inputs(seed: int = 0) -> dict:
    key = jax.random.key(seed)
    ks = jax.random.split(key, 24)
    f32 = jnp.float32

    def nrm(k, shape, std):
        return std * jax.random.normal(k, shape, f32)

    n = jnp.arange(SSM_STATE, dtype=f32)
    return {
        'x': nrm(ks[0], (BATCH, SEQ, D_MODEL), 1.0),
        'c': nrm(ks[1], (BATCH, D_MODEL), 1.0),
        'w_cond': nrm(ks[2], (DEPTH, D_MODEL, 6 * D_MODEL), 0.5 * D_MODEL ** -0.5),
        'b_cond': nrm(ks[3], (DEPTH, 6 * D_MODEL), 0.02),
        'w_in': nrm(ks[4], (DEPTH, D_MODEL, D_IN), D_MODEL ** -0.5),
        'ssm_lam_re': -0.5 + nrm(ks[5], (DEPTH, SSM_GROUPS, SSM_STATE), 0.01),
        'ssm_lam_im': math.pi * n + nrm(ks[6], (DEPTH, SSM_GROUPS, SSM_STATE), 0.01),
        'ssm_log_dt': jax.random.uniform(ks[7], (DEPTH, SSM_GROUPS), f32, math.log(DT_MIN), math.log(DT_MAX)),
        'ssm_b_re': nrm(ks[8], (DEPTH, SSM_GROUPS, SSM_STATE, SSM_GROUP), (2 * SSM_GROUP) ** -0.5),
        'ssm_b_im': nrm(ks[9], (DEPTH, SSM_GROUPS, SSM_STATE, SSM_GROUP), (2 * SSM_GROUP) ** -0.5),
        'ssm_c_re': nrm(ks[10], (DEPTH, SSM_GROUPS, SSM_GROUP, SSM_STATE), SSM_STATE ** -0.5),
        'ssm_c_im': nrm(ks[11], (DEPTH, SSM_GROUPS, SSM_GROUP, SSM_STATE), SSM_STATE ** -0.5),
        'ssm_d': nrm(ks[12], (DEPTH, SSM_WIDTH), 1.0),
        'ssm_w_glu': nrm(ks[13], (DEPTH, SSM_WIDTH, SSM_WIDTH), SSM_WIDTH ** -0.5),
        'ssm_b_glu': nrm(ks[14], (DEPTH, SSM_WIDTH), 0.02),
        'p_ssm': nrm(ks[15], (DEPTH, SSM_WIDTH, D_MODEL), SSM_WIDTH ** -0.5),
        'p_attn': nrm(ks[16], (DEPTH, ATTN_WIDTH, D_MODEL), ATTN_WIDTH ** -0.5),
        'w_out': nrm(ks[17], (DEPTH, D_MODEL, D_MODEL), DEEPNORM_BETA * D_MODEL ** -0.5),
        'ln1_g': 1.0 + nrm(ks[18], (DEPTH, D_MODEL), 0.02),
        'ln1_b': nrm(ks[19], (DEPTH, D_MODEL), 0.02),
        'w_gate_up': nrm(ks[20], (DEPTH, D_MODEL, 2 * D_FF), D_MODEL ** -0.5),
        'w_down': nrm(ks[21], (DEPTH, D_FF, D_MODEL), DEEPNORM_BETA * D_FF ** -0.5),
        'ln2_g': 1.0 + nrm(ks[22], (DEPTH, D_MODEL), 0.02),
        'ln2_b': nrm(ks[23], (DEPTH, D_MODEL), 0.02),
    }


def reference(x, c, w_cond, b_cond, w_in, ssm_lam_re, ssm_lam_im, ssm_log_dt, ssm_b_re, ssm_b_im,
              ssm_c_re, ssm_c_im, ssm_d, ssm_w_glu, ssm_b_glu, p_ssm, p_attn, w_out,
              ln1_g, ln1_b, w_gate_up, w_down, ln2_g, ln2_b):
    L = x.shape[1]
    pos = jnp.arange(L)
    cond_in = jax.nn.silu(c)
    for l in range(DEPTH):
        mod = cond_in @ w_cond[l] + b_cond[l]
        sh1, sc1, gt1, sh2, sc2, gt2 = [m[:, None, :] for m in jnp.split(mod, 6, axis=-1)]
        u = x * (1.0 + sc1) + sh1
        h = hybrid_mixer(u, pos, w_in[l], ssm_lam_re[l], ssm_lam_im[l], ssm_log_dt[l], ssm_b_re[l],
                         ssm_b_im[l], ssm_c_re[l], ssm_c_im[l], ssm_d[l], ssm_w_glu[l], ssm_b_glu[l],
                         p_ssm[l], p_attn[l], w_out[l])
        x = layer_norm(DEEPNORM_ALPHA * x + (1.0 + gt1) * h, ln1_g[l], ln1_b[l])
        u = x * (1.0 + sc2) + sh2
        f = swiglu_ffn(u, w_gate_up[l], w_down[l])
        x = layer_norm(DEEPNORM_ALPHA * x + (1.0 + gt2) * f, ln2_g[l], ln2_b[l])
    return x
```

```python
import math
import numpy as np
from contextlib import ExitStack
import concourse.bass as bass
import concourse.mybir as mybir
from concourse.bass_utils import run_bass_kernel_spmd

F32 = mybir.dt.float32
BF16 = mybir.dt.bfloat16
AF = mybir.ActivationFunctionType
ALU = mybir.AluOpType
AX = mybir.AxisListType

D = 1024
DEPTH = 2
DIN = 4680
DFF = 2816
OFF_SSM, OFF_Q, OFF_K, OFF_V, OFF_QI, OFF_KI, OFF_W, OFF_GS, OFF_GA = 0, 512, 1536, 1792, 2048, 2560, 2624, 2632, 3656
ALPHA = (2 * DEPTH) ** 0.25
LN_EPS = 1e-5
NBIS = 12
NEG = -1.0e30


class Prog:
    def __init__(self, nc, es):
        self.nc = nc
        self.es = es
        self.eng = {"pe": nc.tensor, "dve": nc.vector, "act": nc.scalar, "pool": nc.gpsimd, "sp": nc.sync}
        self.sems = []
        self.cur = {}
        self.waited = {e: {} for e in self.eng}
        self.lastw = {}
        self.readers = {}
        self.dpool = {}
        self.dnext = {}
        self.ninst = 0
        for e in ("pe", "dve", "act", "pool"):
            self.cur[e] = [self._newsem(), 0]
        for q, n in (("sp", 12), ("pool", 8)):
            self.dpool[q] = [[self._newsem(), 0] for _ in range(n)]
            self.dnext[q] = 0

    def _newsem(self):
        s = self.es.enter_context(self.nc.semaphore("s%d" % len(self.sems)))
        self.sems.append(s)
        return len(self.sems) - 1

    def _wait(self, e, dep):
        si, val, src = dep
        if src == e and e == "pe":
            return
        w = self.waited[e]
        if w.get(si, 0) >= val:
            return
        self.eng[e].wait_ge(self.sems[si], val)
        w[si] = val
        self.ninst += 1

    def _deps(self, e, reads, writes):
        for t in reads:
            d = self.lastw.get(t)
            if d:
                self._wait(e, d)
        for t in writes:
            d = self.lastw.get(t)
            if d:
                self._wait(e, d)
            for r in self.readers.get(t, {}).values():
                self._wait(e, r)

    def _book(self, h, reads, writes):
        key = h[2] if h[2] != "dma" else ("dma", h[0])
        for t in reads:
            self.readers.setdefault(t, {})[key] = h
        for t in writes:
            self.lastw[t] = h
            self.readers[t] = {}

    def op(self, e, fn, reads=(), writes=()):
        self._deps(e, reads, writes)
        inst = fn(self.eng[e])
        s = self.cur[e]
        if s[1] >= 30000:
            s = self.cur[e] = [self._newsem(), 0]
        s[1] += 1
        inst.then_inc(self.sems[s[0]], 1)
        self.ninst += 1
        self._book((s[0], s[1], e), reads, writes)

    def dma(self, q, out, in_, reads=(), writes=(), **kw):
        self._deps(q, reads, writes)
        pool = self.dpool[q]
        i = self.dnext[q]
        self.dnext[q] = (i + 1) % len(pool)
        ent = pool[i]
        if ent[1] > 0:
            self._wait(q, (ent[0], ent[1], "dma"))
        if ent[1] > 48000:
            ent[0] = self._newsem()
            ent[1] = 0
        inst = self.eng[q].dma_start(out=out, in_=in_, **kw)
        ent[1] += 16
        inst.then_inc(self.sems[ent[0]], 16)
        self.ninst += 1
        self._book((ent[0], ent[1], "dma"), reads, writes)

    def barrier(self, engines=("pe", "dve", "act", "pool", "sp")):
        hs = []
        for e in ("pe", "dve", "act", "pool"):
            s = self.cur[e]
            if s[1] > 0:
                hs.append((s[0], s[1], e))
        for q in self.dpool:
            for ent in self.dpool[q]:
                if ent[1] > 0:
                    hs.append((ent[0], ent[1], "dma"))
        for e in engines:
            for h in hs:
                if h[2] == e:
                    continue
                self._wait(e, h)


def build(L, dbg=(), nlayers=DEPTH, phases="PABCDE", inj=(), ssm_stop=9):
    NT = L // 128
    NS = L // 512
    NC8 = L // 8
    NCH = max(1, NC8 // 512)
    CW = NC8 // NCH
    NLEV = int(round(math.log2(NC8)))
    nc = bass.Bass("TRN2", target_bir_lowering=False)

    def din(name, shape, dt=F32):
        return nc.dram_tensor(name, list(shape), dt, kind="ExternalInput").ap()

    def dscr(name, shape, dt):
        kind = "ExternalOutput" if name in dbg else ("ExternalInput" if name in inj else "Internal")
        return nc.dram_tensor(name, list(shape), dt, kind=kind).ap()

    x_in = din("x", [L, D])
    ccol = din("ccol", [128, 8])
    w_cond = din("w_cond", [DEPTH, D, 6 * D])
    b_cond = din("b_cond", [DEPTH, 6 * D])
    w_in = din("w_in", [DEPTH, D, DIN])
    w_glu = din("ssm_w_glu", [DEPTH, 512, 512])
    bglu = din("bglu", [128, DEPTH * 4])
    p_ssm = din("p_ssm", [DEPTH, 512, D])
    p_attn = din("p_attn", [DEPTH, D, D])
    w_out = din("w_out", [DEPTH, D, D])
    lnp = din("lnp", [DEPTH, 4, D])
    w_gu = din("w_gate_up", [DEPTH, D, 2 * DFF])
    w_dn = din("w_down", [DEPTH, DFF, D])
    lamR_re = din("lamR_re", [DEPTH, 128, 256])
    lamR_im = din("lamR_im", [DEPTH, 128, 256])
    ldtR = din("ldtR", [DEPTH, 128, 256])
    bR_re = din("bR_re", [DEPTH, 128, 256])
    bR_im = din("bR_im", [DEPTH, 128, 256])
    dcol = din("dcol", [128, DEPTH * 4])
    lamS_re = din("lamS_re", [DEPTH, 128, 16])
    lamS_im = din("lamS_im", [DEPTH, 128, 16])
    ldtS = din("ldtS", [DEPTH, 128, 16])
    cS_re = din("cS_re", [DEPTH, 128, 256])
    cS_im = din("cS_im", [DEPTH, 128, 256])
    bS_re = din("bS_re", [DEPTH, 128, 256])
    bS_im = din("bS_im", [DEPTH, 128, 256])
    ident_d = din("ident", [128, 128])
    causal_d = din("causal", [128, 128])
    rowmask_d = din("rowmask", [128, 8])
    ropeA_c = din("ropeA_c", [128, L])
    ropeA_s = din("ropeA_s", [128, L])
    ropeB_c = din("ropeB_c", [128, L])
    ropeB_s = din("ropeB_s", [128, L])
    y_out = nc.dram_tensor("y", [L, D], F32, kind="ExternalOutput").ap()

    xres = dscr("xres", [L, D], F32)
    x1d = dscr("x1d", [L, D], F32)
    uTs = dscr("uTs", [512, L], BF16)
    qT = dscr("qT", [1024, L], BF16)
    kT = dscr("kT", [256, L], BF16)
    qiT = dscr("qiT", [512, L], BF16)
    kiT = dscr("kiT", [64, L], BF16)
    gsT = dscr("gsT", [1024, L], BF16)
    gaT = dscr("gaT", [1024, L], BF16)
    vd = dscr("vd", [L, 256], BF16)
    wid = dscr("wid", [L, 8], F32)
    ysT = dscr("ysT", [512, L], BF16)
    yaT = dscr("yaT", [1024, L], BF16)
    fpart = dscr("fpart", [L, D], F32)
    dbg_mod = dscr("dbg_mod", [128, 6 * D], F32)
    dbg_sc = dscr("dbg_sc", [128, L], F32)
    dbg_thr = dscr("dbg_thr", [128, 16], F32)

    with ExitStack() as es:
        P = Prog(nc, es)

        uid = [0]

        def SB(st, name, shape, dt):
            uid[0] += 1
            return st.enter_context(nc.sbuf_tensor("sb%d_%s" % (uid[0], name), list(shape), dt))

        def PSU(st, name, shape, dt=F32):
            uid[0] += 1
            return st.enter_context(nc.psum_tensor("ps%d_%s" % (uid[0], name), list(shape), dt))

        ident = SB(es, "ident", [128, 128], F32)
        identb = SB(es, "identb", [128, 128], BF16)
        ident4b = SB(es, "ident4b", [128, 4, 128], BF16)
        causal = SB(es, "causal", [128, 128], F32)
        rowmask = SB(es, "rowmask", [128, 8], F32)
        epsc = SB(es, "epsc", [128, 1], F32)
        modT = SB(es, "modT", [128, 48], F32)
        G12 = SB(es, "G12", [128, 2, D], F32)
        P.dma("sp", ident[:], ident_d, writes=["ident"])
        P.dma("sp", causal[:], causal_d, writes=["causal"])
        P.dma("sp", rowmask[:], rowmask_d, writes=["rowmask"])
        P.op("dve", lambda e: e.tensor_copy(out=identb[:], in_=ident[:]), reads=["ident"], writes=["identb"])
        P.op("dve", lambda e: e.memset(epsc[:], LN_EPS), writes=["epsc"])
        for h in range(4):
            P.op("dve", lambda e: e.tensor_copy(out=ident4b[:, h, :], in_=ident[:]), reads=["ident"], writes=["ident4b"])


        def layer_norm_tile(vt, dst, lnb, stats, mv, rstd):
            for hf in range(2):
                P.op("dve", lambda e: e.bn_stats(out=stats[:, hf, :], in_=vt[:, hf * 512:(hf + 1) * 512]), reads=["vt"], writes=["stats"])
            P.op("dve", lambda e: e.bn_aggr(out=mv[:], in_=stats[:].rearrange("p a b -> p (a b)")), reads=["stats"], writes=["mv"])
            P.op("act", lambda e: e.activation(out=rstd[:], in_=mv[:, 1:2], func=AF.Sqrt, bias=epsc[:, 0:1], scale=1.0), reads=["mv"], writes=["rstd"])
            P.op("dve", lambda e: e.reciprocal(out=rstd[:], in_=rstd[:]), reads=["rstd"], writes=["rstd"])
            P.op("dve", lambda e: e.tensor_scalar(out=dst, in0=vt[:], scalar1=mv[:, 0:1], scalar2=rstd[:, 0:1], op0=ALU.subtract, op1=ALU.mult),
                 reads=["vt", "mv", "rstd"], writes=["xs"])
            P.op("dve", lambda e: e.tensor_tensor(out=dst, in0=dst, in1=lnb[:, 0, :], op=ALU.mult), reads=["xs", "lnb"], writes=["xs"])
            P.op("dve", lambda e: e.tensor_tensor(out=dst, in0=dst, in1=lnb[:, 1, :], op=ALU.add), reads=["xs", "lnb"], writes=["xs"])

        def phase_d(l, xsrc):
            with ExitStack() as ph:
                wglu_sb = SB(ph, "wglu", [128, 4, 512], BF16)
                pssm_sb = SB(ph, "pssm", [128, 4, D], BF16)
                pattn_sb = SB(ph, "pattn", [128, 8, D], BF16)
                wout_sb = SB(ph, "wout", [128, 8, D], BF16)
                wstg = [SB(ph, "wstg%d" % i, [128, D], F32) for i in range(2)]
                bglu_sb = SB(ph, "bglu", [128, DEPTH * 4], F32)
                ys_sb = SB(ph, "ys", [128, 4, 512], BF16)
                ya_sb = SB(ph, "ya", [128, 8, 512], BF16)
                gs_sb = SB(ph, "gs", [128, 8, 512], BF16)
                ga_sb = SB(ph, "ga", [128, 8, 512], BF16)
                sg = SB(ph, "sg", [128, 512], F32)
                yssm = SB(ph, "yssm", [128, 4, 512], BF16)
                tA = SB(ph, "tA", [128, 512], F32)
                tB = SB(ph, "tB", [128, 512], F32)
                merged = SB(ph, "merged", [128, 8, 512], BF16)
                xs = SB(ph, "xs", [128, 4, D], F32)
                vt = SB(ph, "vt", [128, D], F32)
                stats = SB(ph, "stats", [128, 2, 6], F32)
                mv = SB(ph, "mv", [128, 2], F32)
                rstd = SB(ph, "rstd", [128, 1], F32)
                pg = [PSU(ph, "pg%d" % i, [128, 512]) for i in range(2)]
                pA = [PSU(ph, "pA%d" % i, [128, 512]) for i in range(2)]
                pB = [PSU(ph, "pB%d" % i, [128, 512]) for i in range(2)]
                phh = PSU(ph, "phh", [128, D])
                P.dma("sp", bglu_sb[:], bglu, writes=["bglu"])
                lnb = SB(ph, "lnb", [128, 2, D], F32)
                for i_ in range(2):
                    P.dma("sp", lnb[:, i_, :], lnp[l, i_, :].partition_broadcast(128), writes=["lnb"])
                P.dma("pool", wglu_sb[:], w_glu[l].rearrange("(k p) n -> p k n", p=128), writes=["wglu"])
                P.dma("pool", pssm_sb[:], p_ssm[l].rearrange("(k p) n -> p k n", p=128), writes=["pssm"])
                for k in range(8):
                    P.dma("pool", pattn_sb[:, k, :], p_attn[l, k * 128:(k + 1) * 128, :], writes=["pattn"])
                for k in range(8):
                    b = k % 2
                    P.dma("sp", wstg[b][:], w_out[l, k * 128:(k + 1) * 128, :], writes=[("wstg", b)])
                    P.op("dve", lambda e: e.tensor_tensor(out=wout_sb[:, k, :], in0=wstg[b][:], in1=G12[:, 0, :], op=ALU.mult),
                         reads=[("wstg", b), "G12"], writes=["wout"])
                cnt = 0
                for T in range(NS):
                    tsl = slice(T * 512, (T + 1) * 512)
                    P.dma("sp", ys_sb[:], ysT[:, tsl].rearrange("(k p) t -> p k t", p=128), reads=[("ysT", T)], writes=["ys"])
                    P.dma("sp", ya_sb[:], yaT[:, tsl].rearrange("(k p) t -> p k t", p=128), reads=[("yaT", T)], writes=["ya"])
                    P.dma("sp", gs_sb[:], gsT[:, tsl].rearrange("(k p) t -> p k t", p=128), reads=[("gsT", T)], writes=["gs"])
                    P.dma("sp", ga_sb[:], gaT[:, tsl].rearrange("(k p) t -> p k t", p=128), reads=[("gaT", T)], writes=["ga"])
                    P.dma("sp", xs[:], xsrc[tsl, :].rearrange("(s p) d -> p s d", p=128), writes=["xs"])
                    for m in range(4):
                        b = m % 2
                        for k in range(4):
                            P.op("pe", lambda e: e.matmul(pg[b][:], lhsT=wglu_sb[:, k, m * 128:(m + 1) * 128], rhs=ys_sb[:, k, :], start=(k == 0), stop=(k == 3)),
                                 reads=["wglu", "ys"], writes=[("pg", b)])
                        P.op("act", lambda e: e.activation(out=sg[:], in_=pg[b][:], func=AF.Sigmoid, bias=bglu_sb[:, l * 4 + m:l * 4 + m + 1], scale=1.0),
                             reads=[("pg", b), "bglu"], writes=["sg"])
                        P.op("dve", lambda e: e.tensor_tensor(out=yssm[:, m, :], in0=ys_sb[:, m, :], in1=sg[:], op=ALU.mult),
                             reads=["ys", "sg"], writes=["yssm"])
                    for n in range(8):
                        b = n % 2
                        for k in range(4):
                            P.op("pe", lambda e: e.matmul(pA[b][:], lhsT=pssm_sb[:, k, n * 128:(n + 1) * 128], rhs=yssm[:, k, :], start=(k == 0), stop=(k == 3)),
                                 reads=["pssm", "yssm"], writes=[("pA", b)])
                        for k in range(8):
                            P.op("pe", lambda e: e.matmul(pB[b][:], lhsT=pattn_sb[:, k, n * 128:(n + 1) * 128], rhs=ya_sb[:, k, :], start=(k == 0), stop=(k == 7)),
                                 reads=["pattn", "ya"], writes=[("pB", b)])
                        P.op("dve", lambda e: e.tensor_tensor(out=tA[:], in0=pA[b][:], in1=gs_sb[:, n, :], op=ALU.mult), reads=[("pA", b), "gs"], writes=["tA"])
                        P.op("dve", lambda e: e.tensor_tensor(out=tB[:], in0=pB[b][:], in1=ga_sb[:, n, :], op=ALU.mult), reads=[("pB", b), "ga"], writes=["tB"])
                        P.op("pool", lambda e: e.tensor_tensor(out=merged[:, n, :], in0=tA[:], in1=tB[:], op=ALU.add), reads=["tA", "tB"], writes=["merged"])
                    for s in range(4):
                        for hf in range(2):
                            for k in range(8):
                                P.op("pe", lambda e: e.matmul(phh[:, hf * 512:(hf + 1) * 512], lhsT=merged[:, k, s * 128:(s + 1) * 128],
                                                              rhs=wout_sb[:, k, hf * 512:(hf + 1) * 512], start=(k == 0), stop=(k == 7)),
                                     reads=["merged", "wout"], writes=["phh"])
                        for hf in range(2):
                            P.op("dve", lambda e: e.scalar_tensor_tensor(out=vt[:, hf * 512:(hf + 1) * 512], in0=xs[:, s, hf * 512:(hf + 1) * 512], scalar=ALPHA,
                                                                         in1=phh[:, hf * 512:(hf + 1) * 512], op0=ALU.mult, op1=ALU.add),
                                 reads=["xs", "phh"], writes=["vt"])
                        layer_norm_tile(vt, xs[:, s, :], lnb, stats, mv, rstd)
                    P.dma("sp", x1d[tsl, :].rearrange("(s p) d -> p s d", p=128), xs[:], reads=["xs"], writes=[("x1d", T)])
                P.barrier()

        def phase_e(l, xdst):
            NT2 = L // 256
            HM = 11
            for hp in range(2):
              with ExitStack() as ph:
                wgu_sb = SB(ph, "wgu", [128, 8, 2 * HM * 128], BF16)
                wdn_sb = SB(ph, "wdn", [128, HM, D], BF16)
                wstg = [SB(ph, "wstg%d" % i, [128, D], F32) for i in range(2)]
                xs = SB(ph, "xs", [128, 2, D], F32)
                fp_sb = SB(ph, "fp", [128, 2, D], F32)
                u2T = SB(ph, "u2T", [128, 8, 256], BF16)
                sa = SB(ph, "sa", [128, 256], F32)
                hT = SB(ph, "hT", [128, HM, 256], BF16)
                vt = SB(ph, "vt", [128, D], F32)
                stats = SB(ph, "stats", [128, 2, 6], F32)
                mv = SB(ph, "mv", [128, 2], F32)
                rstd = SB(ph, "rstd", [128, 1], F32)
                ptp = [PSU(ph, "ptp%d" % i, [128, 512]) for i in range(2)]
                pa = [PSU(ph, "pa%d" % i, [128, 512]) for i in range(2)]
                pb = [PSU(ph, "pb%d" % i, [128, 512]) for i in range(2)]
                pf = PSU(ph, "pf", [128, D])
                W = HM * 128
                lnb = SB(ph, "lnb", [128, 2, D], F32)
                for i_ in range(2):
                    P.dma("sp", lnb[:, i_, :], lnp[l, 2 + i_, :].partition_broadcast(128), writes=["lnb"])
                for k in range(8):
                    P.dma("pool", wgu_sb[:, k, 0:W], w_gu[l, k * 128:(k + 1) * 128, hp * W:(hp + 1) * W], writes=["wgu"])
                    P.dma("pool", wgu_sb[:, k, W:2 * W], w_gu[l, k * 128:(k + 1) * 128, DFF + hp * W:DFF + (hp + 1) * W], writes=["wgu"])
                for k in range(HM):
                    b = k % 2
                    r0 = (hp * HM + k) * 128
                    P.dma("sp", wstg[b][:], w_dn[l, r0:r0 + 128, :], writes=[("wstg", b)])
                    P.op("dve", lambda e: e.tensor_tensor(out=wdn_sb[:, k, :], in0=wstg[b][:], in1=G12[:, 1, :], op=ALU.mult),
                         reads=[("wstg", b), "G12"], writes=["wdn"])
                for T in range(NT2):
                    tsl = slice(T * 256, (T + 1) * 256)
                    P.dma("sp", xs[:], x1d[tsl, :].rearrange("(s p) d -> p s d", p=128), reads=[("x1d", T // 2)], writes=["xs"])
                    if hp == 1:
                        P.dma("sp", fp_sb[:], fpart[tsl, :].rearrange("(s p) d -> p s d", p=128), reads=[("fpart", T)], writes=["fp"])
                    for k in range(8):
                        b = k % 2
                        for s in range(2):
                            P.op("pe", lambda e: e.transpose(ptp[b][:, s * 128:(s + 1) * 128], xs[:, s, k * 128:(k + 1) * 128], ident[:]),
                                 reads=["xs", "ident"], writes=[("ptp", b)])
                        P.op("act", lambda e: e.activation(out=u2T[:, k, :], in_=ptp[b][:, 0:256], func=AF.Identity,
                                                           scale=modT[:, 32 + k:33 + k], bias=modT[:, 24 + k:25 + k]),
                             reads=[("ptp", b), "modT"], writes=["u2T"])
                    for m in range(HM):
                        b = m % 2
                        for k in range(8):
                            P.op("pe", lambda e: e.matmul(pa[b][:, 0:256], lhsT=wgu_sb[:, k, m * 128:(m + 1) * 128], rhs=u2T[:, k, :], start=(k == 0), stop=(k == 7)),
                                 reads=["wgu", "u2T"], writes=[("pa", b)])
                        for k in range(8):
                            P.op("pe", lambda e: e.matmul(pb[b][:, 0:256], lhsT=wgu_sb[:, k, W + m * 128:W + (m + 1) * 128], rhs=u2T[:, k, :], start=(k == 0), stop=(k == 7)),
                                 reads=["wgu", "u2T"], writes=[("pb", b)])
                        P.op("act", lambda e: e.activation(out=sa[:], in_=pa[b][:, 0:256], func=AF.Silu), reads=[("pa", b)], writes=["sa"])
                        P.op("dve", lambda e: e.tensor_tensor(out=hT[:, m, :], in0=sa[:], in1=pb[b][:, 0:256], op=ALU.mult), reads=["sa", ("pb", b)], writes=["hT"])
                    for s in range(2):
                        for hf in range(2):
                            for k in range(HM):
                                P.op("pe", lambda e: e.matmul(pf[:, hf * 512:(hf + 1) * 512], lhsT=hT[:, k, s * 128:(s + 1) * 128],
                                                              rhs=wdn_sb[:, k, hf * 512:(hf + 1) * 512], start=(k == 0), stop=(k == HM - 1)),
                                     reads=["hT", "wdn"], writes=["pf"])
                        if hp == 0:
                            for hf in range(2):
                                P.op("dve", lambda e: e.tensor_copy(out=fp_sb[:, s, hf * 512:(hf + 1) * 512], in_=pf[:, hf * 512:(hf + 1) * 512]),
                                     reads=["pf"], writes=["fp"])
                        else:
                            for hf in range(2):
                                P.op("dve", lambda e: e.tensor_tensor(out=vt[:, hf * 512:(hf + 1) * 512], in0=fp_sb[:, s, hf * 512:(hf + 1) * 512],
                                                                      in1=pf[:, hf * 512:(hf + 1) * 512], op=ALU.add), reads=["fp", "pf"], writes=["vt"])
                            P.op("dve", lambda e: e.scalar_tensor_tensor(out=vt[:], in0=xs[:, s, :], scalar=ALPHA, in1=vt[:], op0=ALU.mult, op1=ALU.add),
                                 reads=["xs", "vt"], writes=["vt"])
                            layer_norm_tile(vt, xs[:, s, :], lnb, stats, mv, rstd)
                    if hp == 0:
                        P.dma("sp", fpart[tsl, :].rearrange("(s p) d -> p s d", p=128), fp_sb[:], reads=["fp"], writes=[("fpart", T)])
                    else:
                        P.dma("sp", xdst[tsl, :].rearrange("(s p) d -> p s d", p=128), xs[:], reads=["xs"], writes=[("xdst", T)])
                P.barrier()


        def phase_dsa(l):
            SC = 1.0 / math.sqrt(128.0)
            with ExitStack() as ph:
                kT_sb = SB(ph, "kTsb", [128, 2, L], BF16)
                v_sb = SB(ph, "vsb", [128, NT, 2, 132], BF16)
                ki2 = SB(ph, "ki2", [128, L], BF16)
                score = SB(ph, "score", [128, L], F32)
                mbs = [SB(ph, "mb%d" % i, [128, L], BF16) for i in range(2)]
                junk = SB(ph, "junk", [128, L], mybir.dt.uint8)
                R = SB(ph, "R", [128, 8, 512], BF16)
                qT_b = [SB(ph, "qTi%d" % i, [128, 8, 128], BF16) for i in range(3)]
                qiT_b = [SB(ph, "qiTi%d" % i, [128, 4, 128], BF16) for i in range(2)]
                w_b = [SB(ph, "wi_%d" % i, [128, 8], F32) for i in range(2)]
                diag_b = [SB(ph, "diag%d" % i, [128, 8, 128], BF16) for i in range(2)]
                PT = [SB(ph, "PT%d" % i, [128, 512], BF16) for i in range(3)]
                o_b = [SB(ph, "osb%d" % i, [128, D], BF16) for i in range(2)]
                oT_b = [SB(ph, "oTsb%d" % i, [128, 8, 128], BF16) for i in range(2)]
                sm = SB(ph, "sm", [128, 16], F32)
                nrm = SB(ph, "nrm", [128, 8], F32)
                tneg = SB(ph, "tneg", [128, 1], F32)
                psx = [PSU(ph, "psx%d" % i, [128, 512]) for i in range(2)]
                pss = PSU(ph, "pss", [128, 512])
                psST = [PSU(ph, "psST%d" % i, [128, 512]) for i in range(2)]
                psO = [PSU(ph, "psO%d" % i, [128, 512]) for i in range(2)]
                psT = PSU(ph, "psT", [128, 8, 128], BF16)
                P.dma("sp", kT_sb[:], kT.rearrange("(g p) t -> p g t", p=128), writes=["kTsb"])
                for g in range(2):
                    P.dma("sp", v_sb[:, :, g, 0:128], vd[:, g * 128:(g + 1) * 128].rearrange("(j p) d -> p j d", p=128), writes=["vsb"])
                P.op("pool", lambda e: e.memset(v_sb[:, :, :, 128:129], 1.0), writes=["vsb"])
                P.dma("sp", ki2[0:64, :], kiT, writes=["ki2"])
                P.dma("sp", ki2[64:128, :], kiT, writes=["ki2"])
                P.op("pool", lambda e: e.memset(tneg[:], -1.0e29), writes=["tneg"])
                cnts = {"pt": 0, "st": 0}

                def emit_load(i):
                    pb = i % 2
                    qs = slice(i * 128, (i + 1) * 128)
                    qiT_i, w_i, diag = qiT_b[pb], w_b[pb], diag_b[pb]
                    P.dma("sp", qiT_i[:], qiT[:, qs].rearrange("(m p) t -> p m t", p=128), writes=[("qiTi", pb)])
                    P.dma("sp", w_i[:], wid[qs, :], writes=[("wi_", pb)])
                    P.dma("sp", qT_b[i % 3][:], qT[:, qs].rearrange("(h p) t -> p h t", p=128), writes=[("qTi", i % 3)])
                    for h in range(8):
                        P.op("pool", lambda e: e.tensor_scalar(out=diag[:, h, :], in0=identb[:], scalar1=w_i[:, h:h + 1], scalar2=None, op0=ALU.mult),
                             reads=["identb", ("wi_", pb)], writes=[("diag", pb)])

                def emit_idx(i):
                    pb = i % 2
                    n = 128 * (i + 1)
                    nch = (n + 511) // 512
                    qiT_i, diag = qiT_b[pb], diag_b[pb]
                    for c in range(nch):
                        wc = min(512, n - c * 512)
                        ks = slice(c * 512, c * 512 + wc)

                        def xmm(h):
                            base = 64 * (h % 2)
                            P.op("pe", lambda e: e.matmul(psx[h % 2][:, 0:wc], lhsT=qiT_i[base:base + 64, h // 2, :], rhs=ki2[base:base + 64, ks],
                                                          start=True, stop=True), reads=[("qiTi", pb), "ki2"], writes=[("psx", h % 2)])
                        xmm(0)
                        xmm(1)
                        for h in range(8):
                            P.op("act", lambda e: e.activation(out=R[:, h, 0:wc], in_=psx[h % 2][:, 0:wc], func=AF.Relu),
                                 reads=[("psx", h % 2)], writes=[("R", h)])
                            P.op("pe", lambda e: e.matmul(pss[:, 0:wc], lhsT=diag[:, h, :], rhs=R[:, h, 0:wc], start=(h == 0), stop=(h == 7)),
                                 reads=[("diag", pb), ("R", h)], writes=["pss"])
                            if h + 2 < 8:
                                xmm(h + 2)
                        last = (c == nch - 1)
                        wcopy = wc - 128 if last else wc
                        if wcopy > 0:
                            P.op("act", lambda e: e.activation(out=score[:, c * 512:c * 512 + wcopy], in_=pss[:, 0:wcopy], func=AF.Copy),
                                 reads=["pss"], writes=["score"])
                        if last:
                            P.op("dve", lambda e: e.tensor_tensor(out=score[:, n - 128:n], in0=pss[:, wc - 128:wc], in1=causal[:], op=ALU.add),
                                 reads=["pss", "causal"], writes=["score"])

                def emit_thr(i):
                    pb = i % 2
                    n = 128 * (i + 1)
                    mb = mbs[pb]
                    if i >= 2:
                        P.op("dve", lambda e: e.tensor_reduce(out=sm[:, 0:1], in_=score[:, 0:n], axis=AX.X, op=ALU.max), reads=["score"], writes=["sm"])
                        P.op("dve", lambda e: e.tensor_reduce(out=sm[:, 1:2], in_=score[:, 0:n - 128], axis=AX.X, op=ALU.min), reads=["score"], writes=["sm"])
                        P.op("dve", lambda e: e.tensor_scalar(out=sm[:, 2:3], in0=sm[:, 1:2], scalar1=-1.0, scalar2=None, op0=ALU.add), reads=["sm"], writes=["sm"])
                        P.op("dve", lambda e: e.tensor_tensor(out=sm[:, 3:4], in0=sm[:, 0:1], in1=sm[:, 2:3], op=ALU.subtract), reads=["sm"], writes=["sm"])
                        for k in range(NBIS):
                            ck = 2.0 ** (-(k + 1))
                            P.op("dve", lambda e: e.scalar_tensor_tensor(out=sm[:, 4:5], in0=sm[:, 3:4], scalar=ck, in1=sm[:, 2:3], op0=ALU.mult, op1=ALU.add),
                                 reads=["sm"], writes=["sm"])
                            P.op("dve", lambda e: e.tensor_scalar(out=junk[:, 0:n], in0=score[:, 0:n], scalar1=sm[:, 4:5], scalar2=0.0, op0=ALU.is_gt, op1=ALU.add,
                                                                  accum_out=sm[:, 5:6]), reads=["score", "sm"], writes=["sm", "junk"])
                            P.op("dve", lambda e: e.tensor_scalar(out=sm[:, 6:7], in0=sm[:, 5:6], scalar1=255.5, scalar2=sm[:, 3:4], op0=ALU.is_ge, op1=ALU.mult),
                                 reads=["sm"], writes=["sm"])
                            P.op("dve", lambda e: e.scalar_tensor_tensor(out=sm[:, 2:3], in0=sm[:, 6:7], scalar=ck, in1=sm[:, 2:3], op0=ALU.mult, op1=ALU.add),
                                 reads=["sm"], writes=["sm"])
                        thr = sm[:, 2:3]
                    else:
                        thr = tneg[:, 0:1]
                    P.op("dve", lambda e: e.tensor_scalar(out=mb[:, 0:n], in0=score[:, 0:n], scalar1=thr, scalar2=-30000.0, op0=ALU.is_le, op1=ALU.mult),
                         reads=["score", "sm", "tneg"], writes=[("mb", pb)])

                def emit_att(i):
                    pb = i % 2
                    qs = slice(i * 128, (i + 1) * 128)
                    qT_i, mb, o_sb, oT_sb = qT_b[i % 3], mbs[pb], o_b[pb], oT_b[pb]
                    for g in range(2):
                        def qk(j):
                            r = j % 2
                            P.op("pe", lambda e: e.matmul(psST[r][:], lhsT=kT_sb[:, g, j * 128:(j + 1) * 128], rhs=qT_i[:, 4 * g:4 * g + 4, :],
                                                          start=True, stop=False), reads=["kTsb", ("qTi", i % 3)], writes=[("psST", r)])
                            P.op("pe", lambda e: e.matmul(psST[r][:], lhsT=mb[:, j * 128:(j + 1) * 128], rhs=ident4b[:],
                                                          start=False, stop=True), reads=[("mb", pb), "ident4b"], writes=[("psST", r)])
                        qk(0)
                        for j in range(i + 1):
                            r = j % 2
                            r3 = cnts["pt"] % 3
                            cnts["pt"] += 1
                            P.op("act", lambda e: e.activation(out=PT[r3][:], in_=psST[r][:], func=AF.Exp, scale=SC),
                                 reads=[("psST", r)], writes=[("PT", r3)])
                            if j + 1 <= i:
                                qk(j + 1)
                            for hh in range(4):
                                off = (hh % 2) * 256
                                P.op("pe", lambda e: e.matmul(psO[hh // 2][:, off:off + 129], lhsT=PT[r3][:, hh * 128:(hh + 1) * 128], rhs=v_sb[:, j, g, 0:129],
                                                              start=(j == 0 and hh % 2 == 0), stop=(j == i and hh % 2 == 1), skip_group_check=True),
                                     reads=[("PT", r3), "vsb"], writes=[("psO", hh // 2)])
                        for hh in range(4):
                            off = (hh % 2) * 256
                            P.op("act", lambda e: e.activation(out=nrm[:, hh:hh + 1], in_=psO[hh // 2][:, off + 128:off + 129], func=AF.Ln),
                                 reads=[("psO", hh // 2)], writes=["nrm"])
                            P.op("act", lambda e: e.activation(out=nrm[:, 4 + hh:5 + hh], in_=nrm[:, hh:hh + 1], func=AF.Exp, scale=-1.0),
                                 reads=["nrm"], writes=["nrm"])
                            P.op("act", lambda e: e.activation(out=o_sb[:, (4 * g + hh) * 128:(4 * g + hh + 1) * 128], in_=psO[hh // 2][:, off:off + 128],
                                                               func=AF.Identity, scale=nrm[:, 4 + hh:5 + hh]),
                                 reads=[("psO", hh // 2), "nrm"], writes=[("osb", pb)])
                    for h in range(8):
                        P.op("pe", lambda e: e.transpose(psT[:, h, :], o_sb[:, h * 128:(h + 1) * 128], identb[:]), reads=[("osb", pb), "identb"], writes=["psT"])
                    P.op("act", lambda e: e.activation(out=oT_sb[:], in_=psT[:], func=AF.Copy), reads=["psT"], writes=[("oTsb", pb)])
                    P.dma("pool", yaT[:, qs].rearrange("(h p) t -> p h t", p=128), oT_sb[:], reads=[("oTsb", pb)], writes=[("yaT", i)])

                emit_load(0)
                for i in range(NT + 1):
                    if i + 1 < NT:
                        emit_load(i + 1)
                    if i < NT:
                        emit_idx(i)
                        emit_thr(i)
                    if i >= 1:
                        emit_att(i - 1)
                P.barrier()

        def phase_ssm(l):
            TK = "ssmprep"

            def TT(o, a, b, op, eng="dve"):
                P.op(eng, lambda e: e.tensor_tensor(out=o, in0=a, in1=b, op=op), reads=[TK], writes=[TK])

            def TS(o, a, s1, op0, s2=None, op1=None):
                if op1 is None:
                    P.op("dve", lambda e: e.tensor_scalar(out=o, in0=a, scalar1=s1, scalar2=None, op0=op0), reads=[TK], writes=[TK])
                else:
                    P.op("dve", lambda e: e.tensor_scalar(out=o, in0=a, scalar1=s1, scalar2=s2, op0=op0, op1=op1), reads=[TK], writes=[TK])

            def ACT(o, a, func, scale=1.0):
                P.op("act", lambda e: e.activation(out=o, in_=a, func=func, scale=scale), reads=[TK], writes=[TK])

            def zoh(st, tmp, nm, lam_re_d, lam_im_d, ldt_d, F):
                pw_re = SB(st, nm + "pwre", [128, 9, F], F32)
                pw_im = SB(st, nm + "pwim", [128, 9, F], F32)
                fr = SB(st, nm + "fr", [128, F], F32)
                fi = SB(st, nm + "fi", [128, F], F32)
                t = [SB(tmp, nm + "t%d" % i, [128, F], F32) for i in range(10)]
                lr, li, dt, mag, er, ei, a, b, c_, d_ = t
                P.dma("sp", lr[:], lam_re_d, writes=[TK])
                P.dma("sp", li[:], lam_im_d, writes=[TK])
                P.dma("sp", dt[:], ldt_d, writes=[TK])
                ACT(dt[:], dt[:], AF.Exp)
                TT(a[:], lr[:], dt[:], ALU.mult)
                ACT(mag[:], a[:], AF.Exp)
                TT(a[:], li[:], dt[:], ALU.mult)
                ACT(ei[:], a[:], AF.Sin, scale=1.0 / 16.0)
                ACT(b[:], a[:], AF.Sin, scale=1.0 / 32.0)
                TT(b[:], b[:], b[:], ALU.mult)
                TS(er[:], b[:], -2.0, ALU.mult, 1.0, ALU.add)
                for _ in range(4):
                    TT(a[:], er[:], er[:], ALU.mult)
                    TT(b[:], ei[:], ei[:], ALU.mult)
                    TT(c_[:], er[:], ei[:], ALU.mult)
                    TT(er[:], a[:], b[:], ALU.subtract)
                    TS(ei[:], c_[:], 2.0, ALU.mult)
                ar, ai = pw_re[:, 1, :], pw_im[:, 1, :]
                TT(ar, mag[:], er[:], ALU.mult)
                TT(ai, mag[:], ei[:], ALU.mult)
                P.op("dve", lambda e: e.memset(pw_re[:, 0, :], 1.0), reads=[TK], writes=[TK])
                P.op("dve", lambda e: e.memset(pw_im[:, 0, :], 0.0), reads=[TK], writes=[TK])
                TT(a[:], lr[:], lr[:], ALU.mult)
                TT(b[:], li[:], li[:], ALU.mult)
                TT(a[:], a[:], b[:], ALU.add)
                P.op("dve", lambda e: e.reciprocal(out=a[:], in_=a[:]), reads=[TK], writes=[TK])
                TS(b[:], ar, -1.0, ALU.add)
                TT(c_[:], b[:], lr[:], ALU.mult)
                TT(d_[:], ai, li[:], ALU.mult)
                TT(c_[:], c_[:], d_[:], ALU.add)
                TT(fr[:], c_[:], a[:], ALU.mult)
                TT(c_[:], ai, lr[:], ALU.mult)
                TT(d_[:], b[:], li[:], ALU.mult)
                TT(c_[:], c_[:], d_[:], ALU.subtract)
                TT(fi[:], c_[:], a[:], ALU.mult)
                for k in range(2, 9):
                    cmul(pw_re[:, k, :], pw_im[:, k, :], pw_re[:, k - 1, :], pw_im[:, k - 1, :], ar, ai, a[:], b[:])
                return pw_re, pw_im, fr, fi

            def cmul(o_re, o_im, x_re, x_im, y_re, y_im, t1, t2):
                TT(t1, x_re, y_re, ALU.mult)
                TT(t2, x_im, y_im, ALU.mult)
                TT(o_re, t1, t2, ALU.subtract)
                TT(t1, x_re, y_im, ALU.mult)
                TT(t2, x_im, y_re, ALU.mult)
                TT(o_im, t1, t2, ALU.add)

            with ExitStack() as ph:
                B8 = SB(ph, "B8", [128, 4, 8, 2, 64], F32)
                Wr = SB(ph, "Wr", [128, 9, 16, 16], F32)
                Wi = SB(ph, "Wi", [128, 9, 16, 16], F32)
                BbS_re = SB(ph, "BbSre", [128, 16, 16], F32)
                BbS_im = SB(ph, "BbSim", [128, 16, 16], F32)
                lev_re = SB(ph, "levre", [128, NLEV, 16], F32)
                lev_im = SB(ph, "levim", [128, NLEV, 16], F32)
                lev_nim = SB(ph, "levnim", [128, NLEV, 16], F32)
                dcol_sb = SB(ph, "dcol", [128, DEPTH * 4], F32)
                P.dma("sp", dcol_sb[:], dcol, writes=[TK])
                with ExitStack() as tmp:
                    pwR_re, pwR_im, frR, fiR = zoh(tmp, tmp, "R", lamR_re[l], lamR_im[l], ldtR[l], 256)
                    bre = SB(tmp, "bre", [128, 256], F32)
                    bim = SB(tmp, "bim", [128, 256], F32)
                    Bb_re = SB(tmp, "Bbre", [128, 256], F32)
                    Bb_im = SB(tmp, "Bbim", [128, 256], F32)
                    u1 = SB(tmp, "u1", [128, 256], F32)
                    u2 = SB(tmp, "u2", [128, 256], F32)
                    P.dma("sp", bre[:], bR_re[l], writes=[TK])
                    P.dma("sp", bim[:], bR_im[l], writes=[TK])
                    cmul(Bb_re[:], Bb_im[:], frR[:], fiR[:], bre[:], bim[:], u1[:], u2[:])
                    v4 = lambda ap: ap.rearrange("p (f q) -> p f q", q=64)
                    for s_ in range(8):
                        k = 7 - s_
                        cmul(B8[:, :, s_, 0, :], B8[:, :, s_, 1, :], v4(pwR_re[:, k, :]), v4(pwR_im[:, k, :]), v4(Bb_re[:]), v4(Bb_im[:]), v4(u1[:]), v4(u2[:]))
                    pwS_re, pwS_im, frS, fiS = zoh(tmp, tmp, "S", lamS_re[l], lamS_im[l], ldtS[l], 16)
                    cre = SB(tmp, "cre", [128, 16, 16], F32)
                    cim = SB(tmp, "cim", [128, 16, 16], F32)
                    bsr = SB(tmp, "bsr", [128, 16, 16], F32)
                    bsi = SB(tmp, "bsi", [128, 16, 16], F32)
                    w1 = SB(tmp, "w1", [128, 16, 16], F32)
                    w2 = SB(tmp, "w2", [128, 16, 16], F32)
                    P.dma("sp", cre[:], cS_re[l].rearrange("p (q i) -> p q i", i=16), writes=[TK])
                    P.dma("sp", cim[:], cS_im[l].rearrange("p (q i) -> p q i", i=16), writes=[TK])
                    P.dma("sp", bsr[:], bS_re[l].rearrange("p (q i) -> p q i", i=16), writes=[TK])
                    P.dma("sp", bsi[:], bS_im[l].rearrange("p (q i) -> p q i", i=16), writes=[TK])
                    bc = lambda ap: ap.unsqueeze(2).to_broadcast([128, 16, 16])
                    cmul(BbS_re[:], BbS_im[:], bc(frS[:]), bc(fiS[:]), bsr[:], bsi[:], w1[:], w2[:])
                    for k in range(9):
                        er_k, ei_k = bc(pwS_re[:, k, :]), bc(pwS_im[:, k, :])
                        TT(w1[:], cre[:], er_k, ALU.mult)
                        TT(w2[:], cim[:], ei_k, ALU.mult)
                        TT(Wr[:, k, :, :], w1[:], w2[:], ALU.subtract)
                        TT(w1[:], cre[:], ei_k, ALU.mult)
                        TT(w2[:], cim[:], er_k, ALU.mult)
                        TT(w1[:], w1[:], w2[:], ALU.add)
                        TS(Wi[:, k, :, :], w1[:], -1.0, ALU.mult)
                    P.op("dve", lambda e: e.tensor_copy(out=lev_re[:, 0, :], in_=pwS_re[:, 8, :]), reads=[TK], writes=[TK])
                    P.op("dve", lambda e: e.tensor_copy(out=lev_im[:, 0, :], in_=pwS_im[:, 8, :]), reads=[TK], writes=[TK])
                    for d_ in range(1, NLEV):
                        cmul(lev_re[:, d_, :], lev_im[:, d_, :], lev_re[:, d_ - 1, :], lev_im[:, d_ - 1, :], lev_re[:, d_ - 1, :], lev_im[:, d_ - 1, :],
                             w1[:, 0, :], w2[:, 0, :])
                    TS(lev_nim[:], lev_im[:], -1.0, ALU.mult)
                    P.barrier()
                if ssm_stop < 2:
                    return
                uT_ft = SB(ph, "uTft", [128, L], BF16)
                ys_t = SB(ph, "yst", [128, L], BF16)
                B8pad = SB(ph, "B8pad", [128, 4, 8, 2, 2, 64], BF16)
                Cpad = SB(ph, "Cpad", [128, 4, 2, 9, 128], BF16)
                BbSpad = SB(ph, "BbSpad", [128, 4, 2, 128], BF16)
                BD_sb = SB(ph, "BDsb", [128, 8, 128], BF16)
                sc_ = [[SB(ph, "scan%d%d" % (a_, b_), [128, NC8], F32) for b_ in range(2)] for a_ in range(2)]
                Hb = [[SB(ph, "Hb%d%d" % (q_, r_), [128, NC8], BF16) for r_ in range(2)] for q_ in range(4)]
                g1 = SB(ph, "g1", [128, CW], F32)
                g2_ = SB(ph, "g2", [128, CW], F32)
                psBD = PSU(ph, "psBD", [128, 1024])
                psX = [PSU(ph, "psX%d" % i, [128, 512]) for i in range(2)]
                psY = [PSU(ph, "psY%d" % i, [128, 512]) for i in range(2)]
                P.op("pool", lambda e: e.memset(Cpad[:], 0.0), writes=["Cpad"])
                P.op("pool", lambda e: e.memset(BbSpad[:], 0.0), writes=["BbSpad"])
                uTv = uT_ft[:].rearrange("p (c s) -> p c s", s=8)
                ysv = ys_t[:].rearrange("p (c s) -> p c s", s=8)
                xc = 0
                yc = 0
                for ft in range(4):
                    P.dma("sp", uT_ft[:], uTs[ft * 128:(ft + 1) * 128, :], writes=["uTft"])
                    for gl in range(8):
                        P.op("dve", lambda e: e.tensor_scalar(out=B8pad[:, gl // 2, :, :, gl % 2, :], in0=B8[:, ft],
                                                              scalar1=rowmask[:, gl:gl + 1], scalar2=None, op0=ALU.mult),
                             reads=[TK, "rowmask"], writes=["B8pad"])
                    for ql in range(4):
                        qq = ft * 4 + ql
                        for g2 in range(2):
                            rows = slice(g2 * 64, (g2 + 1) * 64)
                            cs = slice((2 * ql + g2) * 16, (2 * ql + g2) * 16 + 16)
                            for ri, Wsrc, Bsrc in ((0, Wr, BbS_re), (1, Wi, BbS_im)):
                                P.op("pool", lambda e: e.tensor_copy(out=Cpad[rows, ql, ri, :, cs], in_=Wsrc[rows, :, qq, :]), reads=[TK], writes=["Cpad"])
                                P.op("pool", lambda e: e.tensor_copy(out=BbSpad[rows, ql, ri, cs], in_=Bsrc[rows, qq, :]), reads=[TK], writes=["BbSpad"])
                    if ssm_stop < 3:
                        continue
                    for hf in range(2):
                        n_ = 0
                        for ql in range(4):
                            for ri in range(2):
                                P.op("pe", lambda e: e.matmul(psBD[:, hf * 512:(hf + 1) * 512], lhsT=BbSpad[:, ql, ri, :], rhs=Cpad[:, ql, ri, 4 * hf:4 * hf + 4, :],
                                                              start=(n_ == 0), stop=(n_ == 7)), reads=["BbSpad", "Cpad"], writes=["psBD"])
                                n_ += 1
                    P.op("dve", lambda e: e.scalar_tensor_tensor(out=BD_sb[:, 0, :], in0=ident[:], scalar=dcol_sb[:, l * 4 + ft:l * 4 + ft + 1], in1=psBD[:, 0:128],
                                                                 op0=ALU.mult, op1=ALU.add), reads=["psBD", "ident", TK], writes=["BDsb"])
                    P.op("act", lambda e: e.activation(out=BD_sb[:, 1:8, :], in_=psBD[:, 128:1024].rearrange("p (t q) -> p t q", q=128), func=AF.Copy),
                         reads=["psBD"], writes=["BDsb"])
                    if ssm_stop < 4:
                        continue
                    for ql in range(4):
                        qq = ft * 4 + ql
                        for ri in range(2):
                            for cb in range(NCH):
                                b = xc % 2
                                xc += 1
                                for s_ in range(8):
                                    P.op("pe", lambda e: e.matmul(psX[b][:, 0:CW], lhsT=B8pad[:, ql, s_, ri, :, :], rhs=uTv[:, cb * CW:(cb + 1) * CW, s_],
                                                                  start=(s_ == 0), stop=(s_ == 7)), reads=["B8pad", "uTft"], writes=[("psX", b)])
                                P.op("act", lambda e: e.activation(out=sc_[0][ri][:, cb * CW:(cb + 1) * CW], in_=psX[b][:, 0:CW], func=AF.Copy),
                                     reads=[("psX", b)], writes=[("scan", 0)])
                        cur = 0
                        for d_ in range(NLEV):
                            sh = 1 << d_
                            src, dst = sc_[cur], sc_[1 - cur]
                            lre = lev_re[:, d_, qq:qq + 1]
                            lim = lev_im[:, d_, qq:qq + 1]
                            lnim = lev_nim[:, d_, qq:qq + 1]
                            tk_s, tk_d = ("scan", cur), ("scan", 1 - cur)
                            m_ = NC8 - sh
                            P.op("dve", lambda e: e.scalar_tensor_tensor(out=dst[0][:, sh:], in0=src[0][:, 0:m_], scalar=lre, in1=src[0][:, sh:], op0=ALU.mult, op1=ALU.add),
                                 reads=[tk_s, TK], writes=[tk_d])
                            P.op("dve", lambda e: e.scalar_tensor_tensor(out=dst[0][:, sh:], in0=src[1][:, 0:m_], scalar=lnim, in1=dst[0][:, sh:], op0=ALU.mult, op1=ALU.add),
                                 reads=[tk_s, TK], writes=[tk_d])
                            P.op("dve", lambda e: e.scalar_tensor_tensor(out=dst[1][:, sh:], in0=src[1][:, 0:m_], scalar=lre, in1=src[1][:, sh:], op0=ALU.mult, op1=ALU.add),
                                 reads=[tk_s, TK], writes=[tk_d])
                            P.op("dve", lambda e: e.scalar_tensor_tensor(out=dst[1][:, sh:], in0=src[0][:, 0:m_], scalar=lim, in1=dst[1][:, sh:], op0=ALU.mult, op1=ALU.add),
                                 reads=[tk_s, TK], writes=[tk_d])
                            for ri in range(2):
                                P.op("pool", lambda e: e.tensor_copy(out=dst[ri][:, 0:sh], in_=src[ri][:, 0:sh]), reads=[tk_s], writes=[tk_d])
                            cur = 1 - cur
                        for ri in range(2):
                            P.op("pool", lambda e: e.memset(Hb[ql][ri][:, 0:1], 0.0), writes=[("Hb", ql)])
                            P.op("act", lambda e: e.activation(out=Hb[ql][ri][:, 1:NC8], in_=sc_[cur][ri][:, 0:NC8 - 1], func=AF.Copy),
                                 reads=[("scan", cur)], writes=[("Hb", ql)])
                    if ssm_stop < 5:
                        continue
                    for t_ in range(8):
                        for cb in range(NCH):
                            b = yc % 2
                            yc += 1
                            cs = slice(cb * CW, (cb + 1) * CW)
                            nmm = (t_ + 1) + 8
                            n_ = 0
                            for tau in range(t_ + 1):
                                P.op("pe", lambda e: e.matmul(psY[b][:, 0:CW], lhsT=BD_sb[:, tau, :], rhs=uTv[:, cs, t_ - tau], start=(n_ == 0), stop=(n_ == nmm - 1)),
                                     reads=["BDsb", "uTft"], writes=[("psY", b)])
                                n_ += 1
                            for ql in range(4):
                                for ri in range(2):
                                    P.op("pe", lambda e: e.matmul(psY[b][:, 0:CW], lhsT=Cpad[:, ql, ri, t_ + 1, :], rhs=Hb[ql][ri][:, cs], start=(n_ == 0), stop=(n_ == nmm - 1)),
                                         reads=["Cpad", ("Hb", ql)], writes=[("psY", b)])
                                    n_ += 1
                            y_ = psY[b][:, 0:CW]
                            P.op("act", lambda e: e.activation(out=g1[:], in_=y_, func=AF.Square), reads=[("psY", b)], writes=["g1"])
                            P.op("dve", lambda e: e.tensor_scalar(out=g1[:], in0=g1[:], scalar1=0.044715, scalar2=1.0, op0=ALU.mult, op1=ALU.add), reads=["g1"], writes=["g1"])
                            P.op("dve", lambda e: e.tensor_tensor(out=g1[:], in0=g1[:], in1=y_, op=ALU.mult), reads=["g1", ("psY", b)], writes=["g1"])
                            P.op("act", lambda e: e.activation(out=g2_[:], in_=g1[:], func=AF.Sigmoid, scale=1.5957691216057308), reads=["g1"], writes=["g2"])
                            P.op("dve", lambda e: e.tensor_tensor(out=ysv[:, cs, t_], in0=g2_[:], in1=y_, op=ALU.mult), reads=["g2", ("psY", b)], writes=["yst"])
                    P.dma("sp", ysT[ft * 128:(ft + 1) * 128, :], ys_t[:], reads=["yst"], writes=[("ysT", ft)])
                P.barrier()

        for l in range(nlayers):
            xsrc = x_in if l == 0 else xres
            xdst = y_out if l == nlayers - 1 else xres

            if "P" in phases:
              with ExitStack() as ph:
                condcol = SB(ph, "condcol", [128, 8], F32)
                condrep = SB(ph, "condrep", [128, 8, 128], F32)
                wc = [SB(ph, "wc%d" % i, [128, 8, 512], F32) for i in range(2)]
                bcb = [SB(ph, "bcb%d" % i, [128, 512], F32) for i in range(2)]
                modB = SB(ph, "modB", [128, 6 * D], F32)
                psm = [PSU(ph, "psm%d" % i, [128, 512]) for i in range(2)]
                pst = PSU(ph, "pst", [128, 48])
                P.dma("sp", condcol[:], ccol, writes=["condcol"])
                P.op("act", lambda e: e.activation(out=condcol[:], in_=condcol[:], func=AF.Silu), reads=["condcol"], writes=["condcol"])
                for k in range(8):
                    P.op("dve", lambda e: e.tensor_copy(out=condrep[:, k, :], in_=condcol[:, k:k + 1].to_broadcast([128, 128])),
                         reads=["condcol"], writes=["condrep"])
                for j in range(12):
                    b = j % 2
                    P.dma("sp", wc[b][:], w_cond[l, :, j * 512:(j + 1) * 512].rearrange("(k p) n -> p k n", p=128), writes=[("wc", b)])
                    P.dma("sp", bcb[b][:], b_cond[l, j * 512:(j + 1) * 512].partition_broadcast(128), writes=[("bcb", b)])
                    for k in range(8):
                        P.op("pe", lambda e: e.matmul(psm[b][:], lhsT=condrep[:, k, :], rhs=wc[b][:, k, :], start=(k == 0), stop=(k == 7)),
                             reads=["condrep", ("wc", b)], writes=[("psm", b)])
                    P.op("dve", lambda e: e.tensor_tensor(out=modB[:, j * 512:(j + 1) * 512], in0=psm[b][:], in1=bcb[b][:], op=ALU.add),
                         reads=[("psm", b), ("bcb", b)], writes=["modB"])
                for j in range(48):
                    P.op("pe", lambda e: e.matmul(pst[:, j:j + 1], lhsT=modB[:, j * 128:(j + 1) * 128], rhs=ident[:, 0:1], start=True, stop=True),
                         reads=["modB", "ident"], writes=["pst"])
                P.op("dve", lambda e: e.tensor_copy(out=modT[:], in_=pst[:]), reads=["pst"], writes=["modT"])
                for a in (8, 32):
                    P.op("dve", lambda e: e.tensor_scalar(out=modT[:, a:a + 8], in0=modT[:, a:a + 8], scalar1=1.0, scalar2=None, op0=ALU.add),
                         reads=["modT"], writes=["modT"])
                P.op("dve", lambda e: e.tensor_scalar(out=G12[:, 0, :], in0=modB[:, 2048:3072], scalar1=1.0, scalar2=None, op0=ALU.add),
                     reads=["modB"], writes=["G12"])
                P.op("dve", lambda e: e.tensor_scalar(out=G12[:, 1, :], in0=modB[:, 5120:6144], scalar1=1.0, scalar2=None, op0=ALU.add),
                     reads=["modB"], writes=["G12"])
                if "dbg_mod" in dbg and l == 0:
                    P.dma("sp", dbg_mod, modB[:], reads=["modB"])
                P.barrier()

            if "A" in phases:
              with ExitStack() as ph:
                wi = SB(ph, "wi", [128, 8, DIN], BF16)
                wr = SB(ph, "wr", [128, 8, 1856], BF16)
                xs = SB(ph, "xs", [128, 4, D], F32)
                uT = SB(ph, "uT", [128, 8, 512], BF16)
                rc = [SB(ph, "rc%d" % i, [128, 512], F32) for i in range(4)]
                t1 = SB(ph, "t1", [128, 512], F32)
                t2 = SB(ph, "t2", [128, 512], F32)
                stg = {}
                for nm, nt_ in (("ssm", 4), ("q", 8), ("k", 2), ("qi", 4), ("ki", 1), ("gs", 8), ("ga", 8)):
                    stg[nm] = SB(ph, "stg_" + nm, [128, nt_, 512], BF16)
                vst = SB(ph, "vst", [128, 4, 256], BF16)
                wst = SB(ph, "wst", [128, 4, 8], F32)
                ptp = [PSU(ph, "ptp%d" % i, [128, 512]) for i in range(2)]
                pz = [PSU(ph, "pz%d" % i, [128, 512]) for i in range(2)]
                pzr = [PSU(ph, "pzr%d" % i, [128, 512]) for i in range(2)]
                pv = PSU(ph, "pv", [128, 512])
                for k in range(8):
                    P.dma("pool", wi[:, k, :], w_in[l, k * 128:(k + 1) * 128, :], writes=["wi"])
                ro = 0
                rinfo = {}
                for nm, off, ncol, half in (("q", OFF_Q, 1024, 64), ("k", OFF_K, 256, 64), ("qi", OFF_QI, 512, 32), ("ki", OFF_KI, 64, 32)):
                    rinfo[nm] = ro
                    for k in range(8):
                        src = wi[:, k, off:off + ncol].rearrange("p (h two f) -> p h two f", two=2, f=half)
                        dst = wr[:, k, ro:ro + ncol].rearrange("p (h two f) -> p h two f", two=2, f=half)
                        P.op("pool", lambda e: e.tensor_copy(out=dst[:, :, 0, :], in_=src[:, :, 1, :]), reads=["wi"], writes=["wr"])
                        P.op("pool", lambda e: e.tensor_copy(out=dst[:, :, 1, :], in_=src[:, :, 0, :]), reads=["wi"], writes=["wr"])
                    ro += ncol
                zc = [0]

                def zbank():
                    zc[0] += 1
                    return zc[0] % 2

                for T in range(NS):
                    tsl = slice(T * 512, (T + 1) * 512)
                    P.dma("sp", xs[:], xsrc[tsl, :].rearrange("(s p) d -> p s d", p=128), writes=["xs"])
                    for i, tab in enumerate((ropeA_c, ropeA_s, ropeB_c, ropeB_s)):
                        P.dma("sp", rc[i][:], tab[:, tsl], writes=[("rc", i)])
                    for k in range(8):
                        b = k % 2
                        for s in range(4):
                            P.op("pe", lambda e: e.transpose(ptp[b][:, s * 128:(s + 1) * 128], xs[:, s, k * 128:(k + 1) * 128], ident[:]),
                                 reads=["xs", "ident"], writes=[("ptp", b)])
                        P.op("act", lambda e: e.activation(out=uT[:, k, :], in_=ptp[b][:], func=AF.Identity,
                                                           scale=modT[:, 8 + k:9 + k], bias=modT[:, k:k + 1]),
                             reads=[("ptp", b), "modT"], writes=["uT"])
                    for nm, off, ntl, rows, kind, dst in (("ssm", OFF_SSM, 4, 128, "copy", uTs), ("q", OFF_Q, 8, 128, "ropeA", qT),
                                                         ("k", OFF_K, 2, 128, "ropeA", kT), ("qi", OFF_QI, 4, 128, "ropeB", qiT),
                                                         ("ki", OFF_KI, 1, 64, "ropeB", kiT), ("gs", OFF_GS, 8, 128, "sig", gsT),
                                                         ("ga", OFF_GA, 8, 128, "sig", gaT)):
                        st = stg[nm]
                        for m in range(ntl):
                            b = zbank()
                            for k in range(8):
                                P.op("pe", lambda e: e.matmul(pz[b][0:rows, :], lhsT=wi[:, k, off + m * 128: off + m * 128 + rows], rhs=uT[:, k, :],
                                                              start=(k == 0), stop=(k == 7)),
                                     reads=["wi", "uT"], writes=[("pz", b)])
                            if kind.startswith("rope"):
                                ro = rinfo[nm]
                                ci, si = (0, 1) if kind == "ropeA" else (2, 3)
                                for k in range(8):
                                    P.op("pe", lambda e: e.matmul(pzr[b][0:rows, :], lhsT=wr[:, k, ro + m * 128: ro + m * 128 + rows], rhs=uT[:, k, :],
                                                                  start=(k == 0), stop=(k == 7)),
                                         reads=["wr", "uT"], writes=[("pzr", b)])
                                P.op("dve", lambda e: e.tensor_tensor(out=t1[0:rows, :], in0=pz[b][0:rows, :], in1=rc[ci][0:rows, :], op=ALU.mult),
                                     reads=[("pz", b), ("rc", ci)], writes=["t1"])
                                P.op("dve", lambda e: e.tensor_tensor(out=t2[0:rows, :], in0=pzr[b][0:rows, :], in1=rc[si][0:rows, :], op=ALU.mult),
                                     reads=[("pzr", b), ("rc", si)], writes=["t2"])
                                P.op("dve", lambda e: e.tensor_tensor(out=st[0:rows, m, :], in0=t1[0:rows, :], in1=t2[0:rows, :], op=ALU.add),
                                     reads=["t1", "t2"], writes=[("stg", nm)])
                            elif kind == "sig":
                                P.op("act", lambda e: e.activation(out=st[:, m, :], in_=pz[b][:], func=AF.Sigmoid),
                                     reads=[("pz", b)], writes=[("stg", nm)])
                            else:
                                P.op("act", lambda e: e.activation(out=st[:, m, :], in_=pz[b][:], func=AF.Copy),
                                     reads=[("pz", b)], writes=[("stg", nm)])
                        if rows == 128:
                            P.dma("sp", dst[:, tsl].rearrange("(m p) t -> p m t", p=128), st[:], reads=[("stg", nm)], writes=[(nm + "T", T)])
                        else:
                            P.dma("sp", dst[:, tsl], st[0:rows, 0, :], reads=[("stg", nm)], writes=[(nm + "T", T)])
                    for s in range(4):
                        for k in range(8):
                            P.op("pe", lambda e: e.matmul(pv[:, 0:256], lhsT=uT[:, k, s * 128:(s + 1) * 128], rhs=wi[:, k, OFF_V:OFF_V + 256],
                                                          start=(k == 0), stop=(k == 7)), reads=["uT", "wi"], writes=["pv"])
                        P.op("act", lambda e: e.activation(out=vst[:, s, :], in_=pv[:, 0:256], func=AF.Copy), reads=["pv"], writes=["vst"])
                        for k in range(8):
                            P.op("pe", lambda e: e.matmul(pv[:, 256:264], lhsT=uT[:, k, s * 128:(s + 1) * 128], rhs=wi[:, k, OFF_W:OFF_W + 8],
                                                          start=(k == 0), stop=(k == 7)), reads=["uT", "wi"], writes=["pv"])
                        P.op("dve", lambda e: e.tensor_copy(out=wst[:, s, :], in_=pv[:, 256:264]), reads=["pv"], writes=["wst"])
                    P.dma("sp", vd[tsl, :].rearrange("(s p) d -> p s d", p=128), vst[:], reads=["vst"], writes=[("vd", T)])
                    P.dma("sp", wid[tsl, :].rearrange("(s p) d -> p s d", p=128), wst[:], reads=["wst"], writes=[("wid", T)])
                P.barrier()

            if "B" in phases:
                phase_ssm(l)

            if "C" in phases:
                phase_dsa(l)

            if "D" in phases:
                phase_d(l, xsrc)
            if "E" in phases:
                phase_e(l, xdst)

        P.barrier(engines=("sp",))
        build.ninst = P.ninst
    return nc


def _rope_tables(L):
    pos = np.arange(L).astype(np.float32)
    out = []
    for half in (64, 32):
        inv = (np.float32(10000.0) ** (-(np.arange(half, dtype=np.float32)) / np.float32(half))).astype(np.float32)
        ang = (pos[None, :] * inv[:, None]).astype(np.float32)
        cos = np.cos(ang).astype(np.float32)
        sin = np.sin(ang).astype(np.float32)
        reps = 128 // half
        c = np.concatenate([cos] * reps, axis=0)
        s = np.concatenate([(-sin if (r % 2 == 0) else sin) for r in range(reps)], axis=0)
        out += [np.ascontiguousarray(c), np.ascontiguousarray(s)]
    return out


def _shared_inputs(inp, L):
    f = lambda a: np.ascontiguousarray(np.asarray(a, dtype=np.float32))
    sh = {}
    for k in ("w_cond", "b_cond", "w_in", "ssm_w_glu", "p_ssm", "p_attn", "w_out", "w_gate_up", "w_down"):
        sh[k] = f(inp[k])
    sh["bglu"] = f(np.asarray(inp["ssm_b_glu"]).reshape(DEPTH, 4, 128).transpose(2, 0, 1).reshape(128, DEPTH * 4))
    sh["dcol"] = f(np.asarray(inp["ssm_d"]).reshape(DEPTH, 4, 128).transpose(2, 0, 1).reshape(128, DEPTH * 4))
    sh["lnp"] = f(np.stack([inp["ln1_g"], inp["ln1_b"], inp["ln2_g"], inp["ln2_b"]], axis=1))
    lam_re, lam_im, ldt = np.asarray(inp["ssm_lam_re"]), np.asarray(inp["ssm_lam_im"]), np.asarray(inp["ssm_log_dt"])
    b_re, b_im = np.asarray(inp["ssm_b_re"]), np.asarray(inp["ssm_b_im"])
    c_re, c_im = np.asarray(inp["ssm_c_re"]), np.asarray(inp["ssm_c_im"])

    def Rl(a):
        a = a.reshape(DEPTH, 4, 8, 1, 64).transpose(0, 2, 3, 1, 4)
        return f(np.broadcast_to(a, (DEPTH, 8, 16, 4, 64)).reshape(DEPTH, 128, 256))

    def Sl(a):
        return f(a.reshape(DEPTH, 16, 2, 64).transpose(0, 2, 3, 1).reshape(DEPTH, 128, 16))

    sh["lamR_re"], sh["lamR_im"] = Rl(lam_re), Rl(lam_im)
    sh["ldtR"] = Rl(np.broadcast_to(ldt[:, :, None], (DEPTH, 32, 64)))
    sh["lamS_re"], sh["lamS_im"] = Sl(lam_re), Sl(lam_im)
    sh["ldtS"] = Sl(np.broadcast_to(ldt[:, :, None], (DEPTH, 32, 64)))
    for nm, a in (("bR_re", b_re), ("bR_im", b_im)):
        sh[nm] = f(a.reshape(DEPTH, 4, 8, 64, 16).transpose(0, 2, 4, 1, 3).reshape(DEPTH, 128, 256))
    for nm, a in (("bS_re", b_re), ("bS_im", b_im)):
        sh[nm] = f(a.reshape(DEPTH, 16, 2, 64, 16).transpose(0, 2, 3, 1, 4).reshape(DEPTH, 128, 256))
    for nm, a in (("cS_re", c_re), ("cS_im", c_im)):
        sh[nm] = f(a.reshape(DEPTH, 16, 2, 16, 64).transpose(0, 2, 4, 1, 3).reshape(DEPTH, 128, 256))
    sh["ident"] = np.eye(128, dtype=np.float32)
    qq = np.arange(128)[:, None]
    ss = np.arange(128)[None, :]
    sh["causal"] = np.where(ss <= qq, 0.0, NEG).astype(np.float32)
    sh["rowmask"] = (np.arange(128)[:, None] // 16 == np.arange(8)[None, :]).astype(np.float32)
    ra_c, ra_s, rb_c, rb_s = _rope_tables(L)
    sh["ropeA_c"], sh["ropeA_s"], sh["ropeB_c"], sh["ropeB_s"] = ra_c, ra_s, rb_c, rb_s
    return sh


def make_in_maps(inp, L, nb):
    sh = _shared_inputs(inp, L)
    x = np.asarray(inp["x"], dtype=np.float32)
    c = np.asarray(inp["c"], dtype=np.float32)
    maps = []
    for b in range(nb):
        m = dict(sh)
        m["x"] = np.ascontiguousarray(x[b])
        m["ccol"] = np.ascontiguousarray(c[b].reshape(8, 128).T)
        maps.append(m)
    return maps


_NC_CACHE = {}


def kernel(**inputs):
    x = np.asarray(inputs["x"])
    B, L, _ = x.shape
    if L not in _NC_CACHE:
        _NC_CACHE[L] = build(L)
    nc = _NC_CACHE[L]
    maps = make_in_maps(inputs, L, B)
    res = run_bass_kernel_spmd(nc, maps, core_ids=list(range(B)))
    return np.stack([np.asarray(r["y"]) for r in res.results], axis=0).astype(np.float32)
```

```python
import math
import numpy as np
from contextlib import ExitStack
import concourse.bass as bass
import concourse.mybir as mybir
from concourse.bass_utils import run_bass_kernel_spmd

F32 = mybir.dt.float32
BF16 = mybir.dt.bfloat16
AF = mybir.ActivationFunctionType
ALU = mybir.AluOpType
AX = mybir.AxisListType

D = 1024
DEPTH = 2
DIN = 4680
DFF = 2816
OFF_SSM, OFF_Q, OFF_K, OFF_V, OFF_QI, OFF_KI, OFF_W, OFF_GS, OFF_GA = 0, 512, 1536, 1792, 2048, 2560, 2624, 2632, 3656
ALPHA = (2 * DEPTH) ** 0.25
LN_EPS = 1e-5
NBIS = 12
NEG = -1.0e30


class Prog:
    def __init__(self, nc, es):
        self.nc = nc
        self.es = es
        self.eng = {"pe": nc.tensor, "dve": nc.vector, "act": nc.scalar, "pool": nc.gpsimd, "sp": nc.sync}
        self.sems = []
        self.cur = {}
        self.waited = {e: {} for e in self.eng}
        self.lastw = {}
        self.readers = {}
        self.dpool = {}
        self.dnext = {}
        self.ninst = 0
        for e in ("pe", "dve", "act", "pool"):
            self.cur[e] = [self._newsem(), 0]
        for q, n in (("sp", 12), ("pool", 8)):
            self.dpool[q] = [[self._newsem(), 0] for _ in range(n)]
            self.dnext[q] = 0

    def _newsem(self):
        s = self.es.enter_context(self.nc.semaphore("s%d" % len(self.sems)))
        self.sems.append(s)
        return len(self.sems) - 1

    def _wait(self, e, dep):
        si, val, src = dep
        if src == e and e == "pe":
            return
        w = self.waited[e]
        if w.get(si, 0) >= val:
            return
        self.eng[e].wait_ge(self.sems[si], val)
        w[si] = val
        self.ninst += 1

    def _deps(self, e, reads, writes):
        for t in reads:
            d = self.lastw.get(t)
            if d:
                self._wait(e, d)
        for t in writes:
            d = self.lastw.get(t)
            if d:
                self._wait(e, d)
            for r in self.readers.get(t, {}).values():
                self._wait(e, r)

    def _book(self, h, reads, writes):
        key = h[2] if h[2] != "dma" else ("dma", h[0])
        for t in reads:
            self.readers.setdefault(t, {})[key] = h
        for t in writes:
            self.lastw[t] = h
            self.readers[t] = {}

    def op(self, e, fn, reads=(), writes=()):
        self._deps(e, reads, writes)
        inst = fn(self.eng[e])
        s = self.cur[e]
        if s[1] >= 30000:
            s = self.cur[e] = [self._newsem(), 0]
        s[1] += 1
        inst.then_inc(self.sems[s[0]], 1)
        self.ninst += 1
        self._book((s[0], s[1], e), reads, writes)

    def dma(self, q, out, in_, reads=(), writes=(), **kw):
        self._deps(q, reads, writes)
        pool = self.dpool[q]
        i = self.dnext[q]
        self.dnext[q] = (i + 1) % len(pool)
        ent = pool[i]
        if ent[1] > 0:
            self._wait(q, (ent[0], ent[1], "dma"))
        if ent[1] > 48000:
            ent[0] = self._newsem()
            ent[1] = 0
        inst = self.eng[q].dma_start(out=out, in_=in_, **kw)
        ent[1] += 16
        inst.then_inc(self.sems[ent[0]], 16)
        self.ninst += 1
        self._book((ent[0], ent[1], "dma"), reads, writes)

    def barrier(self, engines=("pe", "dve", "act", "pool", "sp")):
        hs = []
        for e in ("pe", "dve", "act", "pool"):
            s = self.cur[e]
            if s[1] > 0:
                hs.append((s[0], s[1], e))
        for q in self.dpool:
            for ent in self.dpool[q]:
                if ent[1] > 0:
                    hs.append((ent[0], ent[1], "dma"))
        for e in engines:
            for h in hs:
                if h[2] == e:
                    continue
                self._wait(e, h)


def build(L, dbg=(), nlayers=DEPTH, phases="PABCDE", inj=(), ssm_stop=9):
    NT = L // 128
    NS = L // 512
    NC8 = L // 8
    NCH = max(1, NC8 // 512)
    CW = NC8 // NCH
    NLEV = int(round(math.log2(NC8)))
    nc = bass.Bass("TRN2", target_bir_lowering=False)

    def din(name, shape, dt=F32):
        return nc.dram_tensor(name, list(shape), dt, kind="ExternalInput").ap()

    def dscr(name, shape, dt):
        kind = "ExternalOutput" if name in dbg else ("ExternalInput" if name in inj else "Internal")
        return nc.dram_tensor(name, list(shape), dt, kind=kind).ap()

    x_in = din("x", [L, D])
    ccol = din("ccol", [128, 8])
    w_cond = din("w_cond", [DEPTH, D, 6 * D])
    b_cond = din("b_cond", [DEPTH, 6 * D])
    w_in = din("w_in", [DEPTH, D, DIN])
    w_glu = din("ssm_w_glu", [DEPTH, 512, 512])
    bglu = din("bglu", [128, DEPTH * 4])
    p_ssm = din("p_ssm", [DEPTH, 512, D])
    p_attn = din("p_attn", [DEPTH, D, D])
    w_out = din("w_out", [DEPTH, D, D])
    lnp = din("lnp", [DEPTH, 4, D])
    w_gu = din("w_gate_up", [DEPTH, D, 2 * DFF])
    w_dn = din("w_down", [DEPTH, DFF, D])
    lamR_re = din("lamR_re", [DEPTH, 128, 256])
    lamR_im = din("lamR_im", [DEPTH, 128, 256])
    ldtR = din("ldtR", [DEPTH, 128, 256])
    bR_re = din("bR_re", [DEPTH, 128, 256])
    bR_im = din("bR_im", [DEPTH, 128, 256])
    dcol = din("dcol", [128, DEPTH * 4])
    lamS_re = din("lamS_re", [DEPTH, 128, 16])
    lamS_im = din("lamS_im", [DEPTH, 128, 16])
    ldtS = din("ldtS", [DEPTH, 128, 16])
    cS_re = din("cS_re", [DEPTH, 128, 256])
    cS_im = din("cS_im", [DEPTH, 128, 256])
    bS_re = din("bS_re", [DEPTH, 128, 256])
    bS_im = din("bS_im", [DEPTH, 128, 256])
    ident_d = din("ident", [128, 128])
    causal_d = din("causal", [128, 128])
    rowmask_d = din("rowmask", [128, 8])
    ropeA_c = din("ropeA_c", [128, L])
    ropeA_s = din("ropeA_s", [128, L])
    ropeB_c = din("ropeB_c", [128, L])
    ropeB_s = din("ropeB_s", [128, L])
    y_out = nc.dram_tensor("y", [L, D], F32, kind="ExternalOutput").ap()

    xres = dscr("xres", [L, D], F32)
    x1d = dscr("x1d", [L, D], F32)
    uTs = dscr("uTs", [512, L], BF16)
    qT = dscr("qT", [1024, L], BF16)
    kT = dscr("kT", [256, L], BF16)
    qiT = dscr("qiT", [512, L], BF16)
    kiT = dscr("kiT", [64, L], BF16)
    gsT = dscr("gsT", [1024, L], BF16)
    gaT = dscr("gaT", [1024, L], BF16)
    vd = dscr("vd", [L, 256], BF16)
    wid = dscr("wid", [L, 8], F32)
    ysT = dscr("ysT", [512, L], BF16)
    yaT = dscr("yaT", [1024, L], BF16)
    fpart = dscr("fpart", [L, D], F32)
    dbg_mod = dscr("dbg_mod", [128, 6 * D], F32)
    dbg_sc = dscr("dbg_sc", [128, L], F32)
    dbg_thr = dscr("dbg_thr", [128, 16], F32)

    with ExitStack() as es:
        P = Prog(nc, es)

        uid = [0]

        def SB(st, name, shape, dt):
            uid[0] += 1
            return st.enter_context(nc.sbuf_tensor("sb%d_%s" % (uid[0], name), list(shape), dt))

        def PSU(st, name, shape, dt=F32):
            uid[0] += 1
            return st.enter_context(nc.psum_tensor("ps%d_%s" % (uid[0], name), list(shape), dt))

        ident = SB(es, "ident", [128, 128], F32)
        identb = SB(es, "identb", [128, 128], BF16)
        ident4b = SB(es, "ident4b", [128, 4, 128], BF16)
        causal = SB(es, "causal", [128, 128], F32)
        rowmask = SB(es, "rowmask", [128, 8], F32)
        epsc = SB(es, "epsc", [128, 1], F32)
        modT = SB(es, "modT", [128, 48], F32)
        G12 = SB(es, "G12", [128, 2, D], F32)
        P.dma("sp", ident[:], ident_d, writes=["ident"])
        P.dma("sp", causal[:], causal_d, writes=["causal"])
        P.dma("sp", rowmask[:], rowmask_d, writes=["rowmask"])
        P.op("dve", lambda e: e.tensor_copy(out=identb[:], in_=ident[:]), reads=["ident"], writes=["identb"])
        P.op("dve", lambda e: e.memset(epsc[:], LN_EPS), writes=["epsc"])
        for h in range(4):
            P.op("dve", lambda e: e.tensor_copy(out=ident4b[:, h, :], in_=ident[:]), reads=["ident"], writes=["ident4b"])


        def layer_norm_tile(vt, dst, lnb, stats, mv, rstd):
            for hf in range(2):
                P.op("dve", lambda e: e.bn_stats(out=stats[:, hf, :], in_=vt[:, hf * 512:(hf + 1) * 512]), reads=["vt"], writes=["stats"])
            P.op("dve", lambda e: e.bn_aggr(out=mv[:], in_=stats[:].rearrange("p a b -> p (a b)")), reads=["stats"], writes=["mv"])
            P.op("act", lambda e: e.activation(out=rstd[:], in_=mv[:, 1:2], func=AF.Sqrt, bias=epsc[:, 0:1], scale=1.0), reads=["mv"], writes=["rstd"])
            P.op("dve", lambda e: e.reciprocal(out=rstd[:], in_=rstd[:]), reads=["rstd"], writes=["rstd"])
            P.op("dve", lambda e: e.tensor_scalar(out=dst, in0=vt[:], scalar1=mv[:, 0:1], scalar2=rstd[:, 0:1], op0=ALU.subtract, op1=ALU.mult),
                 reads=["vt", "mv", "rstd"], writes=["xs"])
            P.op("dve", lambda e: e.tensor_tensor(out=dst, in0=dst, in1=lnb[:, 0, :], op=ALU.mult), reads=["xs", "lnb"], writes=["xs"])
            P.op("dve", lambda e: e.tensor_tensor(out=dst, in0=dst, in1=lnb[:, 1, :], op=ALU.add), reads=["xs", "lnb"], writes=["xs"])

        def phase_d(l, xsrc):
            with ExitStack() as ph:
                wglu_sb = SB(ph, "wglu", [128, 4, 512], BF16)
                pssm_sb = SB(ph, "pssm", [128, 4, D], BF16)
                pattn_sb = SB(ph, "pattn", [128, 8, D], BF16)
                wout_sb = SB(ph, "wout", [128, 8, D], BF16)
                wstg = [SB(ph, "wstg%d" % i, [128, D], F32) for i in range(2)]
                bglu_sb = SB(ph, "bglu", [128, DEPTH * 4], F32)
                ys_sb = SB(ph, "ys", [128, 4, 512], BF16)
                ya_sb = SB(ph, "ya", [128, 8, 512], BF16)
                gs_sb = SB(ph, "gs", [128, 8, 512], BF16)
                ga_sb = SB(ph, "ga", [128, 8, 512], BF16)
                sg = SB(ph, "sg", [128, 512], F32)
                yssm = SB(ph, "yssm", [128, 4, 512], BF16)
                tA = SB(ph, "tA", [128, 512], F32)
                tB = SB(ph, "tB", [128, 512], F32)
                merged = SB(ph, "merged", [128, 8, 512], BF16)
                xs = SB(ph, "xs", [128, 4, D], F32)
                vt = SB(ph, "vt", [128, D], F32)
                stats = SB(ph, "stats", [128, 2, 6], F32)
                mv = SB(ph, "mv", [128, 2], F32)
                rstd = SB(ph, "rstd", [128, 1], F32)
                pg = [PSU(ph, "pg%d" % i, [128, 512]) for i in range(2)]
                pA = [PSU(ph, "pA%d" % i, [128, 512]) for i in range(2)]
                pB = [PSU(ph, "pB%d" % i, [128, 512]) for i in range(2)]
                phh = PSU(ph, "phh", [128, D])
                P.dma("sp", bglu_sb[:], bglu, writes=["bglu"])
                lnb = SB(ph, "lnb", [128, 2, D], F32)
                for i_ in range(2):
                    P.dma("sp", lnb[:, i_, :], lnp[l, i_, :].partition_broadcast(128), writes=["lnb"])
                P.dma("pool", wglu_sb[:], w_glu[l].rearrange("(k p) n -> p k n", p=128), writes=["wglu"])
                P.dma("pool", pssm_sb[:], p_ssm[l].rearrange("(k p) n -> p k n", p=128), writes=["pssm"])
                for k in range(8):
                    P.dma("pool", pattn_sb[:, k, :], p_attn[l, k * 128:(k + 1) * 128, :], writes=["pattn"])
                for k in range(8):
                    b = k % 2
                    P.dma("sp", wstg[b][:], w_out[l, k * 128:(k + 1) * 128, :], writes=[("wstg", b)])
                    P.op("dve", lambda e: e.tensor_tensor(out=wout_sb[:, k, :], in0=wstg[b][:], in1=G12[:, 0, :], op=ALU.mult),
                         reads=[("wstg", b), "G12"], writes=["wout"])
                cnt = 0
                for T in range(NS):
                    tsl = slice(T * 512, (T + 1) * 512)
                    P.dma("sp", ys_sb[:], ysT[:, tsl].rearrange("(k p) t -> p k t", p=128), reads=[("ysT", T)], writes=["ys"])
                    P.dma("sp", ya_sb[:], yaT[:, tsl].rearrange("(k p) t -> p k t", p=128), reads=[("yaT", T)], writes=["ya"])
                    P.dma("sp", gs_sb[:], gsT[:, tsl].rearrange("(k p) t -> p k t", p=128), reads=[("gsT", T)], writes=["gs"])
                    P.dma("sp", ga_sb[:], gaT[:, tsl].rearrange("(k p) t -> p k t", p=128), reads=[("gaT", T)], writes=["ga"])
                    P.dma("sp", xs[:], xsrc[tsl, :].rearrange("(s p) d -> p s d", p=128), writes=["xs"])
                    for m in range(4):
                        b = m % 2
                        for k in range(4):
                            P.op("pe", lambda e: e.matmul(pg[b][:], lhsT=wglu_sb[:, k, m * 128:(m + 1) * 128], rhs=ys_sb[:, k, :], start=(k == 0), stop=(k == 3)),
                                 reads=["wglu", "ys"], writes=[("pg", b)])
                        P.op("act", lambda e: e.activation(out=sg[:], in_=pg[b][:], func=AF.Sigmoid, bias=bglu_sb[:, l * 4 + m:l * 4 + m + 1], scale=1.0),
                             reads=[("pg", b), "bglu"], writes=["sg"])
                        P.op("dve", lambda e: e.tensor_tensor(out=yssm[:, m, :], in0=ys_sb[:, m, :], in1=sg[:], op=ALU.mult),
                             reads=["ys", "sg"], writes=["yssm"])
                    for n in range(8):
                        b = n % 2
                        for k in range(4):
                            P.op("pe", lambda e: e.matmul(pA[b][:], lhsT=pssm_sb[:, k, n * 128:(n + 1) * 128], rhs=yssm[:, k, :], start=(k == 0), stop=(k == 3)),
                                 reads=["pssm", "yssm"], writes=[("pA", b)])
                        for k in range(8):
                            P.op("pe", lambda e: e.matmul(pB[b][:], lhsT=pattn_sb[:, k, n * 128:(n + 1) * 128], rhs=ya_sb[:, k, :], start=(k == 0), stop=(k == 7)),
                                 reads=["pattn", "ya"], writes=[("pB", b)])
                        P.op("dve", lambda e: e.tensor_tensor(out=tA[:], in0=pA[b][:], in1=gs_sb[:, n, :], op=ALU.mult), reads=[("pA", b), "gs"], writes=["tA"])
                        P.op("dve", lambda e: e.tensor_tensor(out=tB[:], in0=pB[b][:], in1=ga_sb[:, n, :], op=ALU.mult), reads=[("pB", b), "ga"], writes=["tB"])
                        P.op("pool", lambda e: e.tensor_tensor(out=merged[:, n, :], in0=tA[:], in1=tB[:], op=ALU.add), reads=["tA", "tB"], writes=["merged"])
                    for s in range(4):
                        for hf in range(2):
                            for k in range(8):
                                P.op("pe", lambda e: e.matmul(phh[:, hf * 512:(hf + 1) * 512], lhsT=merged[:, k, s * 128:(s + 1) * 128],
                                                              rhs=wout_sb[:, k, hf * 512:(hf + 1) * 512], start=(k == 0), stop=(k == 7)),
                                     reads=["merged", "wout"], writes=["phh"])
                        for hf in range(2):
                            P.op("dve", lambda e: e.scalar_tensor_tensor(out=vt[:, hf * 512:(hf + 1) * 512], in0=xs[:, s, hf * 512:(hf + 1) * 512], scalar=ALPHA,
                                                                         in1=phh[:, hf * 512:(hf + 1) * 512], op0=ALU.mult, op1=ALU.add),
                                 reads=["xs", "phh"], writes=["vt"])
                        layer_norm_tile(vt, xs[:, s, :], lnb, stats, mv, rstd)
                    P.dma("sp", x1d[tsl, :].rearrange("(s p) d -> p s d", p=128), xs[:], reads=["xs"], writes=[("x1d", T)])
                P.barrier()

        def phase_e(l, xdst):
            NT2 = L // 512
            HM = 11
            for hp in range(2):
              with ExitStack() as ph:
                wgu_sb = SB(ph, "wgu", [128, 8, 2 * HM * 128], BF16)
                wdn_sb = SB(ph, "wdn", [128, HM, D], BF16)
                wstg = [SB(ph, "wstg%d" % i, [128, D], F32) for i in range(2)]
                xs = SB(ph, "xs", [128, 4, D], F32)
                fp_sb = SB(ph, "fp", [128, 4, D], F32)
                u2T = SB(ph, "u2T", [128, 8, 512], BF16)
                sa = SB(ph, "sa", [128, 512], F32)
                hT = SB(ph, "hT", [128, HM, 512], BF16)
                vt = SB(ph, "vt", [128, D], F32)
                stats = SB(ph, "stats", [128, 2, 6], F32)
                mv = SB(ph, "mv", [128, 2], F32)
                rstd = SB(ph, "rstd", [128, 1], F32)
                ptp = [PSU(ph, "ptp%d" % i, [128, 512]) for i in range(2)]
                pa = [PSU(ph, "pa%d" % i, [128, 512]) for i in range(2)]
                pb = [PSU(ph, "pb%d" % i, [128, 512]) for i in range(2)]
                pf = PSU(ph, "pf", [128, D])
                W = HM * 128
                lnb = SB(ph, "lnb", [128, 2, D], F32)
                for i_ in range(2):
                    P.dma("sp", lnb[:, i_, :], lnp[l, 2 + i_, :].partition_broadcast(128), writes=["lnb"])
                for k in range(8):
                    P.dma("pool", wgu_sb[:, k, 0:W], w_gu[l, k * 128:(k + 1) * 128, hp * W:(hp + 1) * W], writes=["wgu"])
                    P.dma("pool", wgu_sb[:, k, W:2 * W], w_gu[l, k * 128:(k + 1) * 128, DFF + hp * W:DFF + (hp + 1) * W], writes=["wgu"])
                for k in range(HM):
                    b = k % 2
                    r0 = (hp * HM + k) * 128
                    P.dma("sp", wstg[b][:], w_dn[l, r0:r0 + 128, :], writes=[("wstg", b)])
                    P.op("dve", lambda e: e.tensor_tensor(out=wdn_sb[:, k, :], in0=wstg[b][:], in1=G12[:, 1, :], op=ALU.mult),
                         reads=[("wstg", b), "G12"], writes=["wdn"])
                for T in range(NT2):
                    tsl = slice(T * 512, (T + 1) * 512)
                    P.dma("sp", xs[:], x1d[tsl, :].rearrange("(s p) d -> p s d", p=128), reads=[("x1d", T)], writes=["xs"])
                    if hp == 1:
                        P.dma("sp", fp_sb[:], fpart[tsl, :].rearrange("(s p) d -> p s d", p=128), reads=[("fpart", T)], writes=["fp"])
                    for k in range(8):
                        b = k % 2
                        for s in range(4):
                            P.op("pe", lambda e: e.transpose(ptp[b][:, s * 128:(s + 1) * 128], xs[:, s, k * 128:(k + 1) * 128], ident[:]),
                                 reads=["xs", "ident"], writes=[("ptp", b)])
                        P.op("act", lambda e: e.activation(out=u2T[:, k, :], in_=ptp[b][:], func=AF.Identity,
                                                           scale=modT[:, 32 + k:33 + k], bias=modT[:, 24 + k:25 + k]),
                             reads=[("ptp", b), "modT"], writes=["u2T"])
                    for m in range(HM):
                        b = m % 2
                        for k in range(8):
                            P.op("pe", lambda e: e.matmul(pa[b][:], lhsT=wgu_sb[:, k, m * 128:(m + 1) * 128], rhs=u2T[:, k, :], start=(k == 0), stop=(k == 7)),
                                 reads=["wgu", "u2T"], writes=[("pa", b)])
                        for k in range(8):
                            P.op("pe", lambda e: e.matmul(pb[b][:], lhsT=wgu_sb[:, k, W + m * 128:W + (m + 1) * 128], rhs=u2T[:, k, :], start=(k == 0), stop=(k == 7)),
                                 reads=["wgu", "u2T"], writes=[("pb", b)])
                        P.op("act", lambda e: e.activation(out=sa[:], in_=pa[b][:], func=AF.Silu), reads=[("pa", b)], writes=["sa"])
                        P.op("dve", lambda e: e.tensor_tensor(out=hT[:, m, :], in0=sa[:], in1=pb[b][:], op=ALU.mult), reads=["sa", ("pb", b)], writes=["hT"])
                    for s in range(4):
                        for hf in range(2):
                            for k in range(HM):
                                P.op("pe", lambda e: e.matmul(pf[:, hf * 512:(hf + 1) * 512], lhsT=hT[:, k, s * 128:(s + 1) * 128],
                                                              rhs=wdn_sb[:, k, hf * 512:(hf + 1) * 512], start=(k == 0), stop=(k == HM - 1)),
                                     reads=["hT", "wdn"], writes=["pf"])
                        if hp == 0:
                            for hf in range(2):
                                P.op("dve", lambda e: e.tensor_copy(out=fp_sb[:, s, hf * 512:(hf + 1) * 512], in_=pf[:, hf * 512:(hf + 1) * 512]),
                                     reads=["pf"], writes=["fp"])
                        else:
                            for hf in range(2):
                                P.op("dve", lambda e: e.tensor_tensor(out=vt[:, hf * 512:(hf + 1) * 512], in0=fp_sb[:, s, hf * 512:(hf + 1) * 512],
                                                                      in1=pf[:, hf * 512:(hf + 1) * 512], op=ALU.add), reads=["fp", "pf"], writes=["vt"])
                            P.op("dve", lambda e: e.scalar_tensor_tensor(out=vt[:], in0=xs[:, s, :], scalar=ALPHA, in1=vt[:], op0=ALU.mult, op1=ALU.add),
                                 reads=["xs", "vt"], writes=["vt"])
                            layer_norm_tile(vt, xs[:, s, :], lnb, stats, mv, rstd)
                    if hp == 0:
                        P.dma("sp", fpart[tsl, :].rearrange("(s p) d -> p s d", p=128), fp_sb[:], reads=["fp"], writes=[("fpart", T)])
                    else:
                        P.dma("sp", xdst[tsl, :].rearrange("(s p) d -> p s d", p=128), xs[:], reads=["xs"], writes=[("xdst", T)])
                P.barrier()


        def phase_dsa(l):
            SC = 1.0 / math.sqrt(128.0)
            with ExitStack() as ph:
                kT_sb = SB(ph, "kTsb", [128, 2, L], BF16)
                v_sb = SB(ph, "vsb", [128, NT, 2, 132], BF16)
                ki2 = SB(ph, "ki2", [128, L], BF16)
                score = SB(ph, "score", [128, L], F32)
                mbs = [SB(ph, "mb%d" % i, [128, L], BF16) for i in range(2)]
                junk = SB(ph, "junk", [128, L], mybir.dt.uint8)
                R = SB(ph, "R", [128, 8, 512], BF16)
                qT_b = [SB(ph, "qTi%d" % i, [128, 8, 128], BF16) for i in range(3)]
                qiT_b = [SB(ph, "qiTi%d" % i, [128, 8, 128], BF16) for i in range(2)]
                w_b = [SB(ph, "wi_%d" % i, [128, 8], F32) for i in range(2)]
                diag_b = [SB(ph, "diag%d" % i, [128, 8, 128], BF16) for i in range(2)]
                PT = [SB(ph, "PT%d" % i, [128, 512], BF16) for i in range(3)]
                o_b = [SB(ph, "osb%d" % i, [128, D], BF16) for i in range(2)]
                oT_b = [SB(ph, "oTsb%d" % i, [128, 8, 128], BF16) for i in range(2)]
                sm = SB(ph, "sm", [128, 16], F32)
                nrm = SB(ph, "nrm", [128, 8], F32)
                tneg = SB(ph, "tneg", [128, 1], F32)
                psx = [PSU(ph, "psx%d" % i, [128, 512]) for i in range(2)]
                pss = PSU(ph, "pss", [128, 512])
                psST = [PSU(ph, "psST%d" % i, [128, 512]) for i in range(2)]
                psO = [PSU(ph, "psO%d" % i, [128, 512]) for i in range(2)]
                psT = PSU(ph, "psT", [128, 8, 128], BF16)
                P.dma("sp", kT_sb[:], kT.rearrange("(g p) t -> p g t", p=128), writes=["kTsb"])
                for g in range(2):
                    P.dma("sp", v_sb[:, :, g, 0:128], vd[:, g * 128:(g + 1) * 128].rearrange("(j p) d -> p j d", p=128), writes=["vsb"])
                P.op("pool", lambda e: e.memset(v_sb[:, :, :, 128:129], 1.0), writes=["vsb"])
                P.dma("sp", ki2[0:64, :], kiT, writes=["ki2"])
                P.dma("sp", ki2[64:128, :], kiT, writes=["ki2"])
                P.op("pool", lambda e: e.memset(tneg[:], -1.0e29), writes=["tneg"])
                for pb_ in range(2):
                    P.op("pool", lambda e: e.memset(qiT_b[pb_][:], 0.0), writes=[("qiTi", pb_)])
                cnts = {"pt": 0, "st": 0}

                def emit_load(i):
                    pb = i % 2
                    qs = slice(i * 128, (i + 1) * 128)
                    qiT_i, w_i, diag = qiT_b[pb], w_b[pb], diag_b[pb]
                    qsrc = qiT[:, qs].rearrange("(m two d) t -> two d m t", two=2, d=64)
                    qdst = qiT_i[:].rearrange("p (m two) t -> p m two t", two=2)
                    P.dma("sp", qdst[0:64, :, 0, :], qsrc[0], writes=[("qiTi", pb)])
                    P.dma("sp", qdst[64:128, :, 1, :], qsrc[1], writes=[("qiTi", pb)])
                    P.dma("sp", w_i[:], wid[qs, :], writes=[("wi_", pb)])
                    P.dma("sp", qT_b[i % 3][:], qT[:, qs].rearrange("(h p) t -> p h t", p=128), writes=[("qTi", i % 3)])
                    for h in range(8):
                        P.op("pool", lambda e: e.tensor_scalar(out=diag[:, h, :], in0=identb[:], scalar1=w_i[:, h:h + 1], scalar2=None, op0=ALU.mult),
                             reads=["identb", ("wi_", pb)], writes=[("diag", pb)])

                def emit_idx(i):
                    pb = i % 2
                    n = 128 * (i + 1)
                    nch = (n + 511) // 512
                    qiT_i, diag = qiT_b[pb], diag_b[pb]
                    for c in range(nch):
                        wc = min(512, n - c * 512)
                        ks = slice(c * 512, c * 512 + wc)

                        def xmm(h):
                            P.op("pe", lambda e: e.matmul(psx[h % 2][:, 0:wc], lhsT=qiT_i[:, h, :], rhs=ki2[:, ks],
                                                          start=True, stop=True), reads=[("qiTi", pb), "ki2"], writes=[("psx", h % 2)])
                        xmm(0)
                        xmm(1)
                        for h in range(8):
                            P.op("act", lambda e: e.activation(out=R[:, h, 0:wc], in_=psx[h % 2][:, 0:wc], func=AF.Relu),
                                 reads=[("psx", h % 2)], writes=[("R", h)])
                            P.op("pe", lambda e: e.matmul(pss[:, 0:wc], lhsT=diag[:, h, :], rhs=R[:, h, 0:wc], start=(h == 0), stop=(h == 7)),
                                 reads=[("diag", pb), ("R", h)], writes=["pss"])
                            if h + 2 < 8:
                                xmm(h + 2)
                        last = (c == nch - 1)
                        wcopy = wc - 128 if last else wc
                        if wcopy > 0:
                            P.op("act", lambda e: e.activation(out=score[:, c * 512:c * 512 + wcopy], in_=pss[:, 0:wcopy], func=AF.Copy),
                                 reads=["pss"], writes=["score"])
                        if last:
                            P.op("dve", lambda e: e.tensor_tensor(out=score[:, n - 128:n], in0=pss[:, wc - 128:wc], in1=causal[:], op=ALU.add),
                                 reads=["pss", "causal"], writes=["score"])

                def emit_thr(i):
                    pb = i % 2
                    n = 128 * (i + 1)
                    mb = mbs[pb]
                    if i >= 2:
                        P.op("dve", lambda e: e.tensor_reduce(out=sm[:, 0:1], in_=score[:, 0:n], axis=AX.X, op=ALU.max), reads=["score"], writes=["sm"])
                        P.op("dve", lambda e: e.tensor_reduce(out=sm[:, 1:2], in_=score[:, 0:n - 128], axis=AX.X, op=ALU.min), reads=["score"], writes=["sm"])
                        P.op("dve", lambda e: e.tensor_scalar(out=sm[:, 2:3], in0=sm[:, 1:2], scalar1=-1.0, scalar2=None, op0=ALU.add), reads=["sm"], writes=["sm"])
                        P.op("dve", lambda e: e.tensor_tensor(out=sm[:, 3:4], in0=sm[:, 0:1], in1=sm[:, 2:3], op=ALU.subtract), reads=["sm"], writes=["sm"])
                        for k in range(NBIS):
                            ck = 2.0 ** (-(k + 1))
                            P.op("dve", lambda e: e.scalar_tensor_tensor(out=sm[:, 4:5], in0=sm[:, 3:4], scalar=ck, in1=sm[:, 2:3], op0=ALU.mult, op1=ALU.add),
                                 reads=["sm"], writes=["sm"])
                            P.op("dve", lambda e: e.tensor_scalar(out=junk[:, 0:n], in0=score[:, 0:n], scalar1=sm[:, 4:5], scalar2=0.0, op0=ALU.is_gt, op1=ALU.add,
                                                                  accum_out=sm[:, 5:6]), reads=["score", "sm"], writes=["sm", "junk"])
                            P.op("dve", lambda e: e.tensor_scalar(out=sm[:, 6:7], in0=sm[:, 5:6], scalar1=255.5, scalar2=sm[:, 3:4], op0=ALU.is_ge, op1=ALU.mult),
                                 reads=["sm"], writes=["sm"])
                            P.op("dve", lambda e: e.scalar_tensor_tensor(out=sm[:, 2:3], in0=sm[:, 6:7], scalar=ck, in1=sm[:, 2:3], op0=ALU.mult, op1=ALU.add),
                                 reads=["sm"], writes=["sm"])
                        thr = sm[:, 2:3]
                    else:
                        thr = tneg[:, 0:1]
                    P.op("dve", lambda e: e.tensor_scalar(out=mb[:, 0:n], in0=score[:, 0:n], scalar1=thr, scalar2=-30000.0, op0=ALU.is_le, op1=ALU.mult),
                         reads=["score", "sm", "tneg"], writes=[("mb", pb)])

                def emit_att(i):
                    pb = i % 2
                    qs = slice(i * 128, (i + 1) * 128)
                    qT_i, mb, o_sb, oT_sb = qT_b[i % 3], mbs[pb], o_b[pb], oT_b[pb]
                    for g in range(2):
                        def qk(j):
                            r = j % 2
                            P.op("pe", lambda e: e.matmul(psST[r][:], lhsT=kT_sb[:, g, j * 128:(j + 1) * 128], rhs=qT_i[:, 4 * g:4 * g + 4, :],
                                                          start=True, stop=False), reads=["kTsb", ("qTi", i % 3)], writes=[("psST", r)])
                            P.op("pe", lambda e: e.matmul(psST[r][:], lhsT=mb[:, j * 128:(j + 1) * 128], rhs=ident4b[:],
                                                          start=False, stop=True), reads=[("mb", pb), "ident4b"], writes=[("psST", r)])
                        qk(0)
                        for j in range(i + 1):
                            r = j % 2
                            r3 = cnts["pt"] % 3
                            cnts["pt"] += 1
                            P.op("act", lambda e: e.activation(out=PT[r3][:], in_=psST[r][:], func=AF.Exp, scale=SC),
                                 reads=[("psST", r)], writes=[("PT", r3)])
                            if j + 1 <= i:
                                qk(j + 1)
                            for hh in range(4):
                                off = (hh % 2) * 256
                                P.op("pe", lambda e: e.matmul(psO[hh // 2][:, off:off + 129], lhsT=PT[r3][:, hh * 128:(hh + 1) * 128], rhs=v_sb[:, j, g, 0:129],
                                                              start=(j == 0 and hh % 2 == 0), stop=(j == i and hh % 2 == 1), skip_group_check=True),
                                     reads=[("PT", r3), "vsb"], writes=[("psO", hh // 2)])
                        for hh in range(4):
                            off = (hh % 2) * 256
                            P.op("act", lambda e: e.activation(out=nrm[:, hh:hh + 1], in_=psO[hh // 2][:, off + 128:off + 129], func=AF.Ln),
                                 reads=[("psO", hh // 2)], writes=["nrm"])
                            P.op("act", lambda e: e.activation(out=nrm[:, 4 + hh:5 + hh], in_=nrm[:, hh:hh + 1], func=AF.Exp, scale=-1.0),
                                 reads=["nrm"], writes=["nrm"])
                            P.op("act", lambda e: e.activation(out=o_sb[:, (4 * g + hh) * 128:(4 * g + hh + 1) * 128], in_=psO[hh // 2][:, off:off + 128],
                                                               func=AF.Identity, scale=nrm[:, 4 + hh:5 + hh]),
                                 reads=[("psO", hh // 2), "nrm"], writes=[("osb", pb)])
                    for h in range(8):
                        P.op("pe", lambda e: e.transpose(psT[:, h, :], o_sb[:, h * 128:(h + 1) * 128], identb[:]), reads=[("osb", pb), "identb"], writes=["psT"])
                    P.op("act", lambda e: e.activation(out=oT_sb[:], in_=psT[:], func=AF.Copy), reads=["psT"], writes=[("oTsb", pb)])
                    P.dma("pool", yaT[:, qs].rearrange("(h p) t -> p h t", p=128), oT_sb[:], reads=[("oTsb", pb)], writes=[("yaT", i)])

                emit_load(0)
                for i in range(NT + 1):
                    if i + 1 < NT:
                        emit_load(i + 1)
                    if i < NT:
                        emit_idx(i)
                        emit_thr(i)
                    if i >= 1:
                        emit_att(i - 1)
                P.barrier()

        def phase_ssm(l):
            TK = "ssmprep"

            def TT(o, a, b, op, eng="dve"):
                P.op(eng, lambda e: e.tensor_tensor(out=o, in0=a, in1=b, op=op), reads=[TK], writes=[TK])

            def TS(o, a, s1, op0, s2=None, op1=None):
                if op1 is None:
                    P.op("dve", lambda e: e.tensor_scalar(out=o, in0=a, scalar1=s1, scalar2=None, op0=op0), reads=[TK], writes=[TK])
                else:
                    P.op("dve", lambda e: e.tensor_scalar(out=o, in0=a, scalar1=s1, scalar2=s2, op0=op0, op1=op1), reads=[TK], writes=[TK])

            def ACT(o, a, func, scale=1.0):
                P.op("act", lambda e: e.activation(out=o, in_=a, func=func, scale=scale), reads=[TK], writes=[TK])

            def zoh(st, tmp, nm, lam_re_d, lam_im_d, ldt_d, F):
                pw_re = SB(st, nm + "pwre", [128, 9, F], F32)
                pw_im = SB(st, nm + "pwim", [128, 9, F], F32)
                fr = SB(st, nm + "fr", [128, F], F32)
                fi = SB(st, nm + "fi", [128, F], F32)
                t = [SB(tmp, nm + "t%d" % i, [128, F], F32) for i in range(10)]
                lr, li, dt, mag, er, ei, a, b, c_, d_ = t
                P.dma("sp", lr[:], lam_re_d, writes=[TK])
                P.dma("sp", li[:], lam_im_d, writes=[TK])
                P.dma("sp", dt[:], ldt_d, writes=[TK])
                ACT(dt[:], dt[:], AF.Exp)
                TT(a[:], lr[:], dt[:], ALU.mult)
                ACT(mag[:], a[:], AF.Exp)
                TT(a[:], li[:], dt[:], ALU.mult)
                ACT(ei[:], a[:], AF.Sin, scale=1.0 / 16.0)
                ACT(b[:], a[:], AF.Sin, scale=1.0 / 32.0)
                TT(b[:], b[:], b[:], ALU.mult)
                TS(er[:], b[:], -2.0, ALU.mult, 1.0, ALU.add)
                for _ in range(4):
                    TT(a[:], er[:], er[:], ALU.mult)
                    TT(b[:], ei[:], ei[:], ALU.mult)
                    TT(c_[:], er[:], ei[:], ALU.mult)
                    TT(er[:], a[:], b[:], ALU.subtract)
                    TS(ei[:], c_[:], 2.0, ALU.mult)
                ar, ai = pw_re[:, 1, :], pw_im[:, 1, :]
                TT(ar, mag[:], er[:], ALU.mult)
                TT(ai, mag[:], ei[:], ALU.mult)
                P.op("dve", lambda e: e.memset(pw_re[:, 0, :], 1.0), reads=[TK], writes=[TK])
                P.op("dve", lambda e: e.memset(pw_im[:, 0, :], 0.0), reads=[TK], writes=[TK])
                TT(a[:], lr[:], lr[:], ALU.mult)
                TT(b[:], li[:], li[:], ALU.mult)
                TT(a[:], a[:], b[:], ALU.add)
                P.op("dve", lambda e: e.reciprocal(out=a[:], in_=a[:]), reads=[TK], writes=[TK])
                TS(b[:], ar, -1.0, ALU.add)
                TT(c_[:], b[:], lr[:], ALU.mult)
                TT(d_[:], ai, li[:], ALU.mult)
                TT(c_[:], c_[:], d_[:], ALU.add)
                TT(fr[:], c_[:], a[:], ALU.mult)
                TT(c_[:], ai, lr[:], ALU.mult)
                TT(d_[:], b[:], li[:], ALU.mult)
                TT(c_[:], c_[:], d_[:], ALU.subtract)
                TT(fi[:], c_[:], a[:], ALU.mult)
                for k in range(2, 9):
                    cmul(pw_re[:, k, :], pw_im[:, k, :], pw_re[:, k - 1, :], pw_im[:, k - 1, :], ar, ai, a[:], b[:])
                return pw_re, pw_im, fr, fi

            def cmul(o_re, o_im, x_re, x_im, y_re, y_im, t1, t2):
                TT(t1, x_re, y_re, ALU.mult)
                TT(t2, x_im, y_im, ALU.mult)
                TT(o_re, t1, t2, ALU.subtract)
                TT(t1, x_re, y_im, ALU.mult)
                TT(t2, x_im, y_re, ALU.mult)
                TT(o_im, t1, t2, ALU.add)

            with ExitStack() as ph:
                B8 = SB(ph, "B8", [128, 4, 8, 2, 64], F32)
                Wr = SB(ph, "Wr", [128, 9, 16, 16], F32)
                Wi = SB(ph, "Wi", [128, 9, 16, 16], F32)
                BbS_re = SB(ph, "BbSre", [128, 16, 16], F32)
                BbS_im = SB(ph, "BbSim", [128, 16, 16], F32)
                lev_re = SB(ph, "levre", [128, NLEV, 16], F32)
                lev_im = SB(ph, "levim", [128, NLEV, 16], F32)
                lev_nim = SB(ph, "levnim", [128, NLEV, 16], F32)
                dcol_sb = SB(ph, "dcol", [128, DEPTH * 4], F32)
                P.dma("sp", dcol_sb[:], dcol, writes=[TK])
                with ExitStack() as tmp:
                    pwR_re, pwR_im, frR, fiR = zoh(tmp, tmp, "R", lamR_re[l], lamR_im[l], ldtR[l], 256)
                    bre = SB(tmp, "bre", [128, 256], F32)
                    bim = SB(tmp, "bim", [128, 256], F32)
                    Bb_re = SB(tmp, "Bbre", [128, 256], F32)
                    Bb_im = SB(tmp, "Bbim", [128, 256], F32)
                    u1 = SB(tmp, "u1", [128, 256], F32)
                    u2 = SB(tmp, "u2", [128, 256], F32)
                    P.dma("sp", bre[:], bR_re[l], writes=[TK])
                    P.dma("sp", bim[:], bR_im[l], writes=[TK])
                    cmul(Bb_re[:], Bb_im[:], frR[:], fiR[:], bre[:], bim[:], u1[:], u2[:])
                    v4 = lambda ap: ap.rearrange("p (f q) -> p f q", q=64)
                    for s_ in range(8):
                        k = 7 - s_
                        cmul(B8[:, :, s_, 0, :], B8[:, :, s_, 1, :], v4(pwR_re[:, k, :]), v4(pwR_im[:, k, :]), v4(Bb_re[:]), v4(Bb_im[:]), v4(u1[:]), v4(u2[:]))
                    pwS_re, pwS_im, frS, fiS = zoh(tmp, tmp, "S", lamS_re[l], lamS_im[l], ldtS[l], 16)
                    cre = SB(tmp, "cre", [128, 16, 16], F32)
                    cim = SB(tmp, "cim", [128, 16, 16], F32)
                    bsr = SB(tmp, "bsr", [128, 16, 16], F32)
                    bsi = SB(tmp, "bsi", [128, 16, 16], F32)
                    w1 = SB(tmp, "w1", [128, 16, 16], F32)
                    w2 = SB(tmp, "w2", [128, 16, 16], F32)
                    P.dma("sp", cre[:], cS_re[l].rearrange("p (q i) -> p q i", i=16), writes=[TK])
                    P.dma("sp", cim[:], cS_im[l].rearrange("p (q i) -> p q i", i=16), writes=[TK])
                    P.dma("sp", bsr[:], bS_re[l].rearrange("p (q i) -> p q i", i=16), writes=[TK])
                    P.dma("sp", bsi[:], bS_im[l].rearrange("p (q i) -> p q i", i=16), writes=[TK])
                    bc = lambda ap: ap.unsqueeze(2).to_broadcast([128, 16, 16])
                    cmul(BbS_re[:], BbS_im[:], bc(frS[:]), bc(fiS[:]), bsr[:], bsi[:], w1[:], w2[:])
                    for k in range(9):
                        er_k, ei_k = bc(pwS_re[:, k, :]), bc(pwS_im[:, k, :])
                        TT(w1[:], cre[:], er_k, ALU.mult)
                        TT(w2[:], cim[:], ei_k, ALU.mult)
                        TT(Wr[:, k, :, :], w1[:], w2[:], ALU.subtract)
                        TT(w1[:], cre[:], ei_k, ALU.mult)
                        TT(w2[:], cim[:], er_k, ALU.mult)
                        TT(w1[:], w1[:], w2[:], ALU.add)
                        TS(Wi[:, k, :, :], w1[:], -1.0, ALU.mult)
                    P.op("dve", lambda e: e.tensor_copy(out=lev_re[:, 0, :], in_=pwS_re[:, 8, :]), reads=[TK], writes=[TK])
                    P.op("dve", lambda e: e.tensor_copy(out=lev_im[:, 0, :], in_=pwS_im[:, 8, :]), reads=[TK], writes=[TK])
                    for d_ in range(1, NLEV):
                        cmul(lev_re[:, d_, :], lev_im[:, d_, :], lev_re[:, d_ - 1, :], lev_im[:, d_ - 1, :], lev_re[:, d_ - 1, :], lev_im[:, d_ - 1, :],
                             w1[:, 0, :], w2[:, 0, :])
                    TS(lev_nim[:], lev_im[:], -1.0, ALU.mult)
                    P.barrier()
                if ssm_stop < 2:
                    return
                uT_ft = SB(ph, "uTft", [128, L], BF16)
                ys_t = SB(ph, "yst", [128, L], BF16)
                B8pad = SB(ph, "B8pad", [128, 4, 8, 2, 2, 64], BF16)
                Cpad = SB(ph, "Cpad", [128, 4, 2, 9, 128], BF16)
                BbSpad = SB(ph, "BbSpad", [128, 4, 2, 128], BF16)
                BD_sb = SB(ph, "BDsb", [128, 8, 128], BF16)
                sc_ = [[SB(ph, "scan%d%d" % (a_, b_), [128, NC8], F32) for b_ in range(2)] for a_ in range(2)]
                Hb = [[SB(ph, "Hb%d%d" % (q_, r_), [128, NC8], BF16) for r_ in range(2)] for q_ in range(4)]
                g1 = SB(ph, "g1", [128, CW], F32)
                g2_ = SB(ph, "g2", [128, CW], F32)
                psBD = PSU(ph, "psBD", [128, 1024])
                psX = [PSU(ph, "psX%d" % i, [128, 512]) for i in range(2)]
                psY = [PSU(ph, "psY%d" % i, [128, 512]) for i in range(2)]
                P.op("pool", lambda e: e.memset(Cpad[:], 0.0), writes=["Cpad"])
                P.op("pool", lambda e: e.memset(BbSpad[:], 0.0), writes=["BbSpad"])
                uTv = uT_ft[:].rearrange("p (c s) -> p c s", s=8)
                ysv = ys_t[:].rearrange("p (c s) -> p c s", s=8)
                xc = 0
                yc = 0
                for ft in range(4):
                    P.dma("sp", uT_ft[:], uTs[ft * 128:(ft + 1) * 128, :], writes=["uTft"])
                    for gl in range(8):
                        P.op("dve", lambda e: e.tensor_scalar(out=B8pad[:, gl // 2, :, :, gl % 2, :], in0=B8[:, ft],
                                                              scalar1=rowmask[:, gl:gl + 1], scalar2=None, op0=ALU.mult),
                             reads=[TK, "rowmask"], writes=["B8pad"])
                    for ql in range(4):
                        qq = ft * 4 + ql
                        for g2 in range(2):
                            rows = slice(g2 * 64, (g2 + 1) * 64)
                            cs = slice((2 * ql + g2) * 16, (2 * ql + g2) * 16 + 16)
                            for ri, Wsrc, Bsrc in ((0, Wr, BbS_re), (1, Wi, BbS_im)):
                                P.op("pool", lambda e: e.tensor_copy(out=Cpad[rows, ql, ri, :, cs], in_=Wsrc[rows, :, qq, :]), reads=[TK], writes=["Cpad"])
                                P.op("pool", lambda e: e.tensor_copy(out=BbSpad[rows, ql, ri, cs], in_=Bsrc[rows, qq, :]), reads=[TK], writes=["BbSpad"])
                    if ssm_stop < 3:
                        continue
                    for hf in range(2):
                        n_ = 0
                        for ql in range(4):
                            for ri in range(2):
                                P.op("pe", lambda e: e.matmul(psBD[:, hf * 512:(hf + 1) * 512], lhsT=BbSpad[:, ql, ri, :], rhs=Cpad[:, ql, ri, 4 * hf:4 * hf + 4, :],
                                                              start=(n_ == 0), stop=(n_ == 7)), reads=["BbSpad", "Cpad"], writes=["psBD"])
                                n_ += 1
                    P.op("dve", lambda e: e.scalar_tensor_tensor(out=BD_sb[:, 0, :], in0=ident[:], scalar=dcol_sb[:, l * 4 + ft:l * 4 + ft + 1], in1=psBD[:, 0:128],
                                                                 op0=ALU.mult, op1=ALU.add), reads=["psBD", "ident", TK], writes=["BDsb"])
                    P.op("act", lambda e: e.activation(out=BD_sb[:, 1:8, :], in_=psBD[:, 128:1024].rearrange("p (t q) -> p t q", q=128), func=AF.Copy),
                         reads=["psBD"], writes=["BDsb"])
                    if ssm_stop < 4:
                        continue
                    for ql in range(4):
                        qq = ft * 4 + ql
                        for ri in range(2):
                            for cb in range(NCH):
                                b = xc % 2
                                xc += 1
                                for s_ in range(8):
                                    P.op("pe", lambda e: e.matmul(psX[b][:, 0:CW], lhsT=B8pad[:, ql, s_, ri, :, :], rhs=uTv[:, cb * CW:(cb + 1) * CW, s_],
                                                                  start=(s_ == 0), stop=(s_ == 7)), reads=["B8pad", "uTft"], writes=[("psX", b)])
                                P.op("act", lambda e: e.activation(out=sc_[0][ri][:, cb * CW:(cb + 1) * CW], in_=psX[b][:, 0:CW], func=AF.Copy),
                                     reads=[("psX", b)], writes=[("scan", 0)])
                        cur = 0
                        for d_ in range(NLEV):
                            sh = 1 << d_
                            src, dst = sc_[cur], sc_[1 - cur]
                            lre = lev_re[:, d_, qq:qq + 1]
                            lim = lev_im[:, d_, qq:qq + 1]
                            lnim = lev_nim[:, d_, qq:qq + 1]
                            tk_s, tk_d = ("scan", cur), ("scan", 1 - cur)
                            m_ = NC8 - sh
                            P.op("dve", lambda e: e.scalar_tensor_tensor(out=dst[0][:, sh:], in0=src[0][:, 0:m_], scalar=lre, in1=src[0][:, sh:], op0=ALU.mult, op1=ALU.add),
                                 reads=[tk_s, TK], writes=[tk_d])
                            P.op("dve", lambda e: e.scalar_tensor_tensor(out=dst[0][:, sh:], in0=src[1][:, 0:m_], scalar=lnim, in1=dst[0][:, sh:], op0=ALU.mult, op1=ALU.add),
                                 reads=[tk_s, TK], writes=[tk_d])
                            P.op("dve", lambda e: e.scalar_tensor_tensor(out=dst[1][:, sh:], in0=src[1][:, 0:m_], scalar=lre, in1=src[1][:, sh:], op0=ALU.mult, op1=ALU.add),
                                 reads=[tk_s, TK], writes=[tk_d])
                            P.op("dve", lambda e: e.scalar_tensor_tensor(out=dst[1][:, sh:], in0=src[0][:, 0:m_], scalar=lim, in1=dst[1][:, sh:], op0=ALU.mult, op1=ALU.add),
                                 reads=[tk_s, TK], writes=[tk_d])
                            for ri in range(2):
                                P.op("pool", lambda e: e.tensor_copy(out=dst[ri][:, 0:sh], in_=src[ri][:, 0:sh]), reads=[tk_s], writes=[tk_d])
                            cur = 1 - cur
                        for ri in range(2):
                            P.op("pool", lambda e: e.memset(Hb[ql][ri][:, 0:1], 0.0), writes=[("Hb", ql)])
                            P.op("act", lambda e: e.activation(out=Hb[ql][ri][:, 1:NC8], in_=sc_[cur][ri][:, 0:NC8 - 1], func=AF.Copy),
                                 reads=[("scan", cur)], writes=[("Hb", ql)])
                    if ssm_stop < 5:
                        continue
                    for t_ in range(8):
                        for cb in range(NCH):
                            b = yc % 2
                            yc += 1
                            cs = slice(cb * CW, (cb + 1) * CW)
                            nmm = (t_ + 1) + 8
                            n_ = 0
                            for tau in range(t_ + 1):
                                P.op("pe", lambda e: e.matmul(psY[b][:, 0:CW], lhsT=BD_sb[:, tau, :], rhs=uTv[:, cs, t_ - tau], start=(n_ == 0), stop=(n_ == nmm - 1)),
                                     reads=["BDsb", "uTft"], writes=[("psY", b)])
                                n_ += 1
                            for ql in range(4):
                                for ri in range(2):
                                    P.op("pe", lambda e: e.matmul(psY[b][:, 0:CW], lhsT=Cpad[:, ql, ri, t_ + 1, :], rhs=Hb[ql][ri][:, cs], start=(n_ == 0), stop=(n_ == nmm - 1)),
                                         reads=["Cpad", ("Hb", ql)], writes=[("psY", b)])
                                    n_ += 1
                            y_ = psY[b][:, 0:CW]
                            P.op("act", lambda e: e.activation(out=g1[:], in_=y_, func=AF.Square), reads=[("psY", b)], writes=["g1"])
                            P.op("dve", lambda e: e.tensor_scalar(out=g1[:], in0=g1[:], scalar1=0.044715, scalar2=1.0, op0=ALU.mult, op1=ALU.add), reads=["g1"], writes=["g1"])
                            P.op("dve", lambda e: e.tensor_tensor(out=g1[:], in0=g1[:], in1=y_, op=ALU.mult), reads=["g1", ("psY", b)], writes=["g1"])
                            P.op("act", lambda e: e.activation(out=g2_[:], in_=g1[:], func=AF.Sigmoid, scale=1.5957691216057308), reads=["g1"], writes=["g2"])
                            P.op("dve", lambda e: e.tensor_tensor(out=ysv[:, cs, t_], in0=g2_[:], in1=y_, op=ALU.mult), reads=["g2", ("psY", b)], writes=["yst"])
                    P.dma("sp", ysT[ft * 128:(ft + 1) * 128, :], ys_t[:], reads=["yst"], writes=[("ysT", ft)])
                P.barrier()

        for l in range(nlayers):
            xsrc = x_in if l == 0 else xres
            xdst = y_out if l == nlayers - 1 else xres

            if "P" in phases:
              with ExitStack() as ph:
                condcol = SB(ph, "condcol", [128, 8], F32)
                condrep = SB(ph, "condrep", [128, 8, 128], F32)
                wc = [SB(ph, "wc%d" % i, [128, 8, 512], F32) for i in range(2)]
                bcb = [SB(ph, "bcb%d" % i, [128, 512], F32) for i in range(2)]
                modB = SB(ph, "modB", [128, 6 * D], F32)
                psm = [PSU(ph, "psm%d" % i, [128, 512]) for i in range(2)]
                pst = PSU(ph, "pst", [128, 48])
                P.dma("sp", condcol[:], ccol, writes=["condcol"])
                P.op("act", lambda e: e.activation(out=condcol[:], in_=condcol[:], func=AF.Silu), reads=["condcol"], writes=["condcol"])
                for k in range(8):
                    P.op("dve", lambda e: e.tensor_copy(out=condrep[:, k, :], in_=condcol[:, k:k + 1].to_broadcast([128, 128])),
                         reads=["condcol"], writes=["condrep"])
                for j in range(12):
                    b = j % 2
                    P.dma("sp", wc[b][:], w_cond[l, :, j * 512:(j + 1) * 512].rearrange("(k p) n -> p k n", p=128), writes=[("wc", b)])
                    P.dma("sp", bcb[b][:], b_cond[l, j * 512:(j + 1) * 512].partition_broadcast(128), writes=[("bcb", b)])
                    for k in range(8):
                        P.op("pe", lambda e: e.matmul(psm[b][:], lhsT=condrep[:, k, :], rhs=wc[b][:, k, :], start=(k == 0), stop=(k == 7)),
                             reads=["condrep", ("wc", b)], writes=[("psm", b)])
                    P.op("dve", lambda e: e.tensor_tensor(out=modB[:, j * 512:(j + 1) * 512], in0=psm[b][:], in1=bcb[b][:], op=ALU.add),
                         reads=[("psm", b), ("bcb", b)], writes=["modB"])
                for j in range(48):
                    P.op("pe", lambda e: e.matmul(pst[:, j:j + 1], lhsT=modB[:, j * 128:(j + 1) * 128], rhs=ident[:, 0:1], start=True, stop=True),
                         reads=["modB", "ident"], writes=["pst"])
                P.op("dve", lambda e: e.tensor_copy(out=modT[:], in_=pst[:]), reads=["pst"], writes=["modT"])
                for a in (8, 32):
                    P.op("dve", lambda e: e.tensor_scalar(out=modT[:, a:a + 8], in0=modT[:, a:a + 8], scalar1=1.0, scalar2=None, op0=ALU.add),
                         reads=["modT"], writes=["modT"])
                P.op("dve", lambda e: e.tensor_scalar(out=G12[:, 0, :], in0=modB[:, 2048:3072], scalar1=1.0, scalar2=None, op0=ALU.add),
                     reads=["modB"], writes=["G12"])
                P.op("dve", lambda e: e.tensor_scalar(out=G12[:, 1, :], in0=modB[:, 5120:6144], scalar1=1.0, scalar2=None, op0=ALU.add),
                     reads=["modB"], writes=["G12"])
                if "dbg_mod" in dbg and l == 0:
                    P.dma("sp", dbg_mod, modB[:], reads=["modB"])
                P.barrier()

            if "A" in phases:
              with ExitStack() as ph:
                wi = SB(ph, "wi", [128, 8, DIN], BF16)
                wr = SB(ph, "wr", [128, 8, 1856], BF16)
                xs = SB(ph, "xs", [128, 4, D], F32)
                uT = SB(ph, "uT", [128, 8, 512], BF16)
                rc = [SB(ph, "rc%d" % i, [128, 512], F32) for i in range(4)]
                t1 = SB(ph, "t1", [128, 512], F32)
                t2 = SB(ph, "t2", [128, 512], F32)
                stg = {}
                for nm, nt_ in (("ssm", 4), ("q", 8), ("k", 2), ("qi", 4), ("ki", 1), ("gs", 8), ("ga", 8)):
                    stg[nm] = SB(ph, "stg_" + nm, [128, nt_, 512], BF16)
                vst = SB(ph, "vst", [128, 4, 256], BF16)
                wst = SB(ph, "wst", [128, 4, 8], F32)
                ptp = [PSU(ph, "ptp%d" % i, [128, 512]) for i in range(2)]
                pz = [PSU(ph, "pz%d" % i, [128, 512]) for i in range(2)]
                pzr = [PSU(ph, "pzr%d" % i, [128, 512]) for i in range(2)]
                pv = PSU(ph, "pv", [128, 512])
                for k in range(8):
                    P.dma("pool", wi[:, k, :], w_in[l, k * 128:(k + 1) * 128, :], writes=["wi"])
                ro = 0
                rinfo = {}
                for nm, off, ncol, half in (("q", OFF_Q, 1024, 64), ("k", OFF_K, 256, 64), ("qi", OFF_QI, 512, 32), ("ki", OFF_KI, 64, 32)):
                    rinfo[nm] = ro
                    for k in range(8):
                        src = wi[:, k, off:off + ncol].rearrange("p (h two f) -> p h two f", two=2, f=half)
                        dst = wr[:, k, ro:ro + ncol].rearrange("p (h two f) -> p h two f", two=2, f=half)
                        P.op("pool", lambda e: e.tensor_copy(out=dst[:, :, 0, :], in_=src[:, :, 1, :]), reads=["wi"], writes=["wr"])
                        P.op("pool", lambda e: e.tensor_copy(out=dst[:, :, 1, :], in_=src[:, :, 0, :]), reads=["wi"], writes=["wr"])
                    ro += ncol
                zc = [0]

                def zbank():
                    zc[0] += 1
                    return zc[0] % 2

                for T in range(NS):
                    tsl = slice(T * 512, (T + 1) * 512)
                    P.dma("sp", xs[:], xsrc[tsl, :].rearrange("(s p) d -> p s d", p=128), writes=["xs"])
                    for i, tab in enumerate((ropeA_c, ropeA_s, ropeB_c, ropeB_s)):
                        P.dma("sp", rc[i][:], tab[:, tsl], writes=[("rc", i)])
                    for k in range(8):
                        b = k % 2
                        for s in range(4):
                            P.op("pe", lambda e: e.transpose(ptp[b][:, s * 128:(s + 1) * 128], xs[:, s, k * 128:(k + 1) * 128], ident[:]),
                                 reads=["xs", "ident"], writes=[("ptp", b)])
                        P.op("act", lambda e: e.activation(out=uT[:, k, :], in_=ptp[b][:], func=AF.Identity,
                                                           scale=modT[:, 8 + k:9 + k], bias=modT[:, k:k + 1]),
                             reads=[("ptp", b), "modT"], writes=["uT"])
                    for nm, off, ntl, rows, kind, dst in (("ssm", OFF_SSM, 4, 128, "copy", uTs), ("q", OFF_Q, 8, 128, "ropeA", qT),
                                                         ("k", OFF_K, 2, 128, "ropeA", kT), ("qi", OFF_QI, 4, 128, "ropeB", qiT),
                                                         ("ki", OFF_KI, 1, 64, "ropeB", kiT), ("gs", OFF_GS, 8, 128, "sig", gsT),
                                                         ("ga", OFF_GA, 8, 128, "sig", gaT)):
                        st = stg[nm]
                        for m in range(ntl):
                            b = zbank()
                            for k in range(8):
                                P.op("pe", lambda e: e.matmul(pz[b][0:rows, :], lhsT=wi[:, k, off + m * 128: off + m * 128 + rows], rhs=uT[:, k, :],
                                                              start=(k == 0), stop=(k == 7)),
                                     reads=["wi", "uT"], writes=[("pz", b)])
                            if kind.startswith("rope"):
                                ro = rinfo[nm]
                                ci, si = (0, 1) if kind == "ropeA" else (2, 3)
                                for k in range(8):
                                    P.op("pe", lambda e: e.matmul(pzr[b][0:rows, :], lhsT=wr[:, k, ro + m * 128: ro + m * 128 + rows], rhs=uT[:, k, :],
                                                                  start=(k == 0), stop=(k == 7)),
                                         reads=["wr", "uT"], writes=[("pzr", b)])
                                P.op("dve", lambda e: e.tensor_tensor(out=t1[0:rows, :], in0=pz[b][0:rows, :], in1=rc[ci][0:rows, :], op=ALU.mult),
                                     reads=[("pz", b), ("rc", ci)], writes=["t1"])
                                P.op("dve", lambda e: e.tensor_tensor(out=t2[0:rows, :], in0=pzr[b][0:rows, :], in1=rc[si][0:rows, :], op=ALU.mult),
                                     reads=[("pzr", b), ("rc", si)], writes=["t2"])
                                P.op("dve", lambda e: e.tensor_tensor(out=st[0:rows, m, :], in0=t1[0:rows, :], in1=t2[0:rows, :], op=ALU.add),
                                     reads=["t1", "t2"], writes=[("stg", nm)])
                            elif kind == "sig":
                                P.op("act", lambda e: e.activation(out=st[:, m, :], in_=pz[b][:], func=AF.Sigmoid),
                                     reads=[("pz", b)], writes=[("stg", nm)])
                            else:
                                P.op("act", lambda e: e.activation(out=st[:, m, :], in_=pz[b][:], func=AF.Copy),
                                     reads=[("pz", b)], writes=[("stg", nm)])
                        if rows == 128:
                            P.dma("sp", dst[:, tsl].rearrange("(m p) t -> p m t", p=128), st[:], reads=[("stg", nm)], writes=[(nm + "T", T)])
                        else:
                            P.dma("sp", dst[:, tsl], st[0:rows, 0, :], reads=[("stg", nm)], writes=[(nm + "T", T)])
                    for s in range(4):
                        for k in range(8):
                            P.op("pe", lambda e: e.matmul(pv[:, 0:256], lhsT=uT[:, k, s * 128:(s + 1) * 128], rhs=wi[:, k, OFF_V:OFF_V + 256],
                                                          start=(k == 0), stop=(k == 7)), reads=["uT", "wi"], writes=["pv"])
                        P.op("act", lambda e: e.activation(out=vst[:, s, :], in_=pv[:, 0:256], func=AF.Copy), reads=["pv"], writes=["vst"])
                        for k in range(8):
                            P.op("pe", lambda e: e.matmul(pv[:, 256:264], lhsT=uT[:, k, s * 128:(s + 1) * 128], rhs=wi[:, k, OFF_W:OFF_W + 8],
                                                          start=(k == 0), stop=(k == 7)), reads=["uT", "wi"], writes=["pv"])
                        P.op("dve", lambda e: e.tensor_copy(out=wst[:, s, :], in_=pv[:, 256:264]), reads=["pv"], writes=["wst"])
                    P.dma("sp", vd[tsl, :].rearrange("(s p) d -> p s d", p=128), vst[:], reads=["vst"], writes=[("vd", T)])
                    P.dma("sp", wid[tsl, :].rearrange("(s p) d -> p s d", p=128), wst[:], reads=["wst"], writes=[("wid", T)])
                P.barrier()

            if "B" in phases:
                phase_ssm(l)

            if "C" in phases:
                phase_dsa(l)

            if "D" in phases:
                phase_d(l, xsrc)
            if "E" in phases:
                phase_e(l, xdst)

        P.barrier(engines=("sp",))
        build.ninst = P.ninst
    return nc


def _rope_tables(L):
    pos = np.arange(L).astype(np.float32)
    out = []
    for half in (64, 32):
        inv = (np.float32(10000.0) ** (-(np.arange(half, dtype=np.float32)) / np.float32(half))).astype(np.float32)
        ang = (pos[None, :] * inv[:, None]).astype(np.float32)
        cos = np.cos(ang).astype(np.float32)
        sin = np.sin(ang).astype(np.float32)
        reps = 128 // half
        c = np.concatenate([cos] * reps, axis=0)
        s = np.concatenate([(-sin if (r % 2 == 0) else sin) for r in range(reps)], axis=0)
        out += [np.ascontiguousarray(c), np.ascontiguousarray(s)]
    return out


def _shared_inputs(inp, L):
    f = lambda a: np.ascontiguousarray(np.asarray(a, dtype=np.float32))
    sh = {}
    for k in ("w_cond", "b_cond", "w_in", "ssm_w_glu", "p_ssm", "p_attn", "w_out", "w_gate_up", "w_down"):
        sh[k] = f(inp[k])
    sh["bglu"] = f(np.asarray(inp["ssm_b_glu"]).reshape(DEPTH, 4, 128).transpose(2, 0, 1).reshape(128, DEPTH * 4))
    sh["dcol"] = f(np.asarray(inp["ssm_d"]).reshape(DEPTH, 4, 128).transpose(2, 0, 1).reshape(128, DEPTH * 4))
    sh["lnp"] = f(np.stack([inp["ln1_g"], inp["ln1_b"], inp["ln2_g"], inp["ln2_b"]], axis=1))
    lam_re, lam_im, ldt = np.asarray(inp["ssm_lam_re"]), np.asarray(inp["ssm_lam_im"]), np.asarray(inp["ssm_log_dt"])
    b_re, b_im = np.asarray(inp["ssm_b_re"]), np.asarray(inp["ssm_b_im"])
    c_re, c_im = np.asarray(inp["ssm_c_re"]), np.asarray(inp["ssm_c_im"])

    def Rl(a):
        a = a.reshape(DEPTH, 4, 8, 1, 64).transpose(0, 2, 3, 1, 4)
        return f(np.broadcast_to(a, (DEPTH, 8, 16, 4, 64)).reshape(DEPTH, 128, 256))

    def Sl(a):
        return f(a.reshape(DEPTH, 16, 2, 64).transpose(0, 2, 3, 1).reshape(DEPTH, 128, 16))

    sh["lamR_re"], sh["lamR_im"] = Rl(lam_re), Rl(lam_im)
    sh["ldtR"] = Rl(np.broadcast_to(ldt[:, :, None], (DEPTH, 32, 64)))
    sh["lamS_re"], sh["lamS_im"] = Sl(lam_re), Sl(lam_im)
    sh["ldtS"] = Sl(np.broadcast_to(ldt[:, :, None], (DEPTH, 32, 64)))
    for nm, a in (("bR_re", b_re), ("bR_im", b_im)):
        sh[nm] = f(a.reshape(DEPTH, 4, 8, 64, 16).transpose(0, 2, 4, 1, 3).reshape(DEPTH, 128, 256))
    for nm, a in (("bS_re", b_re), ("bS_im", b_im)):
        sh[nm] = f(a.reshape(DEPTH, 16, 2, 64, 16).transpose(0, 2, 3, 1, 4).reshape(DEPTH, 128, 256))
    for nm, a in (("cS_re", c_re), ("cS_im", c_im)):
        sh[nm] = f(a.reshape(DEPTH, 16, 2, 16, 64).transpose(0, 2, 4, 1, 3).reshape(DEPTH, 128, 256))
    sh["ident"] = np.eye(128, dtype=np.float32)
    qq = np.arange(128)[:, None]
    ss = np.arange(128)[None, :]
    sh["causal"] = np.where(ss <= qq, 0.0, NEG).astype(np.float32)
    sh["rowmask"] = (np.arange(128)[:, None] // 16 == np.arange(8)[None, :]).astype(np.float32)
    ra_c, ra_s, rb_c, rb_s = _rope_tables(L)
    sh["ropeA_c"], sh["ropeA_s"], sh["ropeB_c"], sh["ropeB_s"] = ra_c, ra_s, rb_c, rb_s
    return sh


def make_in_maps(inp, L, nb):
    sh = _shared_inputs(inp, L)
    x = np.asarray(inp["x"], dtype=np.float32)
    c = np.asarray(inp["c"], dtype=np.float32)
    maps = []
    for b in range(nb):
        m = dict(sh)
        m["x"] = np.ascontiguousarray(x[b])
        m["ccol"] = np.ascontiguousarray(c[b].reshape(8, 128).T)
        maps.append(m)
    return maps


_NC_CACHE = {}


def kernel(**inputs):
    x = np.asarray(inputs["x"])
    B, L, _ = x.shape
    if L not in _NC_CACHE:
        _NC_CACHE[L] = build(L)
    nc = _NC_CACHE[L]
    maps = make_in_maps(inputs, L, B)
    res = run_bass_kernel_spmd(nc, maps, core_ids=list(range(B)))
    return np.stack([np.asarray(r["y"]) for r in res.results], axis=0).astype(np.float32)
```

```python
import math
import numpy as np
from contextlib import ExitStack
import concourse.bass as bass
import concourse.mybir as mybir
from concourse.bass_utils import run_bass_kernel_spmd

F32 = mybir.dt.float32
BF16 = mybir.dt.bfloat16
AF = mybir.ActivationFunctionType
ALU = mybir.AluOpType
AX = mybir.AxisListType

D = 1024
DEPTH = 2
DIN = 4680
DFF = 2816
OFF_SSM, OFF_Q, OFF_K, OFF_V, OFF_QI, OFF_KI, OFF_W, OFF_GS, OFF_GA = 0, 512, 1536, 1792, 2048, 2560, 2624, 2632, 3656
ALPHA = (2 * DEPTH) ** 0.25
LN_EPS = 1e-5
NBIS = 12
NEG = -1.0e30


class Prog:
    def __init__(self, nc, es):
        self.nc = nc
        self.es = es
        self.eng = {"pe": nc.tensor, "dve": nc.vector, "act": nc.scalar, "pool": nc.gpsimd, "sp": nc.sync}
        self.sems = []
        self.cur = {}
        self.waited = {e: {} for e in self.eng}
        self.lastw = {}
        self.readers = {}
        self.dpool = {}
        self.dnext = {}
        self.ninst = 0
        for e in ("pe", "dve", "act", "pool"):
            self.cur[e] = [self._newsem(), 0]
        for q, n in (("sp", 12), ("pool", 8)):
            self.dpool[q] = [[self._newsem(), 0] for _ in range(n)]
            self.dnext[q] = 0

    def _newsem(self):
        s = self.es.enter_context(self.nc.semaphore("s%d" % len(self.sems)))
        self.sems.append(s)
        return len(self.sems) - 1

    def _wait(self, e, dep):
        si, val, src = dep
        if src == e and e == "pe":
            return
        w = self.waited[e]
        if w.get(si, 0) >= val:
            return
        self.eng[e].wait_ge(self.sems[si], val)
        w[si] = val
        self.ninst += 1

    def _deps(self, e, reads, writes):
        for t in reads:
            d = self.lastw.get(t)
            if d:
                self._wait(e, d)
        for t in writes:
            d = self.lastw.get(t)
            if d:
                self._wait(e, d)
            for r in self.readers.get(t, {}).values():
                self._wait(e, r)

    def _book(self, h, reads, writes):
        key = h[2] if h[2] != "dma" else ("dma", h[0])
        for t in reads:
            self.readers.setdefault(t, {})[key] = h
        for t in writes:
            self.lastw[t] = h
            self.readers[t] = {}

    def op(self, e, fn, reads=(), writes=()):
        self._deps(e, reads, writes)
        inst = fn(self.eng[e])
        s = self.cur[e]
        if s[1] >= 30000:
            s = self.cur[e] = [self._newsem(), 0]
        s[1] += 1
        inst.then_inc(self.sems[s[0]], 1)
        self.ninst += 1
        self._book((s[0], s[1], e), reads, writes)

    def dma(self, q, out, in_, reads=(), writes=(), **kw):
        self._deps(q, reads, writes)
        pool = self.dpool[q]
        i = self.dnext[q]
        self.dnext[q] = (i + 1) % len(pool)
        ent = pool[i]
        if ent[1] > 0:
            self._wait(q, (ent[0], ent[1], "dma"))
        if ent[1] > 48000:
            ent[0] = self._newsem()
            ent[1] = 0
        inst = self.eng[q].dma_start(out=out, in_=in_, **kw)
        ent[1] += 16
        inst.then_inc(self.sems[ent[0]], 16)
        self.ninst += 1
        self._book((ent[0], ent[1], "dma"), reads, writes)

    def barrier(self, engines=("pe", "dve", "act", "pool", "sp")):
        hs = []
        for e in ("pe", "dve", "act", "pool"):
            s = self.cur[e]
            if s[1] > 0:
                hs.append((s[0], s[1], e))
        for q in self.dpool:
            for ent in self.dpool[q]:
                if ent[1] > 0:
                    hs.append((ent[0], ent[1], "dma"))
        for e in engines:
            for h in hs:
                if h[2] == e:
                    continue
                self._wait(e, h)


def build(L, dbg=(), nlayers=DEPTH, phases="PABCDE", inj=(), ssm_stop=9):
    NT = L // 128
    NS = L // 512
    NC8 = L // 8
    NCH = max(1, NC8 // 512)
    CW = NC8 // NCH
    NLEV = int(round(math.log2(NC8)))
    nc = bass.Bass("TRN2", target_bir_lowering=False)

    def din(name, shape, dt=F32):
        return nc.dram_tensor(name, list(shape), dt, kind="ExternalInput").ap()

    def dscr(name, shape, dt):
        kind = "ExternalOutput" if name in dbg else ("ExternalInput" if name in inj else "Internal")
        return nc.dram_tensor(name, list(shape), dt, kind=kind).ap()

    x_in = din("x", [L, D])
    ccol = din("ccol", [128, 8])
    w_cond = din("w_cond", [DEPTH, D, 6 * D])
    b_cond = din("b_cond", [DEPTH, 6 * D])
    w_in = din("w_in", [DEPTH, D, DIN])
    w_glu = din("ssm_w_glu", [DEPTH, 512, 512])
    bglu = din("bglu", [128, DEPTH * 4])
    p_ssm = din("p_ssm", [DEPTH, 512, D])
    p_attn = din("p_attn", [DEPTH, D, D])
    w_out = din("w_out", [DEPTH, D, D])
    lnp = din("lnp", [DEPTH, 4, D])
    w_gu = din("w_gate_up", [DEPTH, D, 2 * DFF])
    w_dn = din("w_down", [DEPTH, DFF, D])
    lamR_re = din("lamR_re", [DEPTH, 128, 256])
    lamR_im = din("lamR_im", [DEPTH, 128, 256])
    ldtR = din("ldtR", [DEPTH, 128, 256])
    bR_re = din("bR_re", [DEPTH, 128, 256])
    bR_im = din("bR_im", [DEPTH, 128, 256])
    dcol = din("dcol", [128, DEPTH * 4])
    lamS_re = din("lamS_re", [DEPTH, 128, 16])
    lamS_im = din("lamS_im", [DEPTH, 128, 16])
    ldtS = din("ldtS", [DEPTH, 128, 16])
    cS_re = din("cS_re", [DEPTH, 128, 256])
    cS_im = din("cS_im", [DEPTH, 128, 256])
    bS_re = din("bS_re", [DEPTH, 128, 256])
    bS_im = din("bS_im", [DEPTH, 128, 256])
    ident_d = din("ident", [128, 128])
    causal_d = din("causal", [128, 128])
    rowmask_d = din("rowmask", [128, 8])
    ropeA_c = din("ropeA_c", [128, L])
    ropeA_s = din("ropeA_s", [128, L])
    ropeB_c = din("ropeB_c", [128, L])
    ropeB_s = din("ropeB_s", [128, L])
    y_out = nc.dram_tensor("y", [L, D], F32, kind="ExternalOutput").ap()

    xres = dscr("xres", [L, D], F32)
    x1d = dscr("x1d", [L, D], F32)
    uTs = dscr("uTs", [512, L], BF16)
    qT = dscr("qT", [1024, L], BF16)
    kT = dscr("kT", [256, L], BF16)
    qiT = dscr("qiT", [512, L], BF16)
    kiT = dscr("kiT", [64, L], BF16)
    gsT = dscr("gsT", [1024, L], BF16)
    gaT = dscr("gaT", [1024, L], BF16)
    vd = dscr("vd", [L, 256], BF16)
    wid = dscr("wid", [L, 8], F32)
    ysT = dscr("ysT", [512, L], BF16)
    yaT = dscr("yaT", [1024, L], BF16)
    fpart = dscr("fpart", [L, D], F32)
    dbg_mod = dscr("dbg_mod", [128, 6 * D], F32)
    dbg_sc = dscr("dbg_sc", [128, L], F32)
    dbg_thr = dscr("dbg_thr", [128, 16], F32)

    with ExitStack() as es:
        P = Prog(nc, es)

        uid = [0]

        def SB(st, name, shape, dt):
            uid[0] += 1
            return st.enter_context(nc.sbuf_tensor("sb%d_%s" % (uid[0], name), list(shape), dt))

        def PSU(st, name, shape, dt=F32):
            uid[0] += 1
            return st.enter_context(nc.psum_tensor("ps%d_%s" % (uid[0], name), list(shape), dt))

        ident = SB(es, "ident", [128, 128], F32)
        identb = SB(es, "identb", [128, 128], BF16)
        ident4b = SB(es, "ident4b", [128, 4, 128], BF16)
        causal = SB(es, "causal", [128, 128], F32)
        rowmask = SB(es, "rowmask", [128, 8], F32)
        epsc = SB(es, "epsc", [128, 1], F32)
        modT = SB(es, "modT", [128, 48], F32)
        G12 = SB(es, "G12", [128, 2, D], F32)
        P.dma("sp", ident[:], ident_d, writes=["ident"])
        P.dma("sp", causal[:], causal_d, writes=["causal"])
        P.dma("sp", rowmask[:], rowmask_d, writes=["rowmask"])
        P.op("dve", lambda e: e.tensor_copy(out=identb[:], in_=ident[:]), reads=["ident"], writes=["identb"])
        P.op("dve", lambda e: e.memset(epsc[:], LN_EPS), writes=["epsc"])
        for h in range(4):
            P.op("dve", lambda e: e.tensor_copy(out=ident4b[:, h, :], in_=ident[:]), reads=["ident"], writes=["ident4b"])


        def layer_norm_tile(vt, dst, lnb, stats, mv, rstd, xtok="xs"):
            for hf in range(2):
                P.op("dve", lambda e: e.bn_stats(out=stats[:, hf, :], in_=vt[:, hf * 512:(hf + 1) * 512]), reads=["vt"], writes=["stats"])
            P.op("dve", lambda e: e.bn_aggr(out=mv[:], in_=stats[:].rearrange("p a b -> p (a b)")), reads=["stats"], writes=["mv"])
            P.op("act", lambda e: e.activation(out=rstd[:], in_=mv[:, 1:2], func=AF.Sqrt, bias=epsc[:, 0:1], scale=1.0), reads=["mv"], writes=["rstd"])
            P.op("dve", lambda e: e.reciprocal(out=rstd[:], in_=rstd[:]), reads=["rstd"], writes=["rstd"])
            P.op("dve", lambda e: e.tensor_scalar(out=dst, in0=vt[:], scalar1=mv[:, 0:1], scalar2=rstd[:, 0:1], op0=ALU.subtract, op1=ALU.mult),
                 reads=["vt", "mv", "rstd"], writes=[xtok])
            P.op("dve", lambda e: e.tensor_tensor(out=dst, in0=dst, in1=lnb[:, 0, :], op=ALU.mult), reads=[xtok, "lnb"], writes=[xtok])
            P.op("dve", lambda e: e.tensor_tensor(out=dst, in0=dst, in1=lnb[:, 1, :], op=ALU.add), reads=[xtok, "lnb"], writes=[xtok])

        def phase_d(l, xsrc):
            with ExitStack() as ph:
                wglu_sb = SB(ph, "wglu", [128, 4, 512], BF16)
                pssm_sb = SB(ph, "pssm", [128, 4, D], BF16)
                pattn_sb = SB(ph, "pattn", [128, 8, D], BF16)
                wout_sb = SB(ph, "wout", [128, 8, D], BF16)
                wstg = [SB(ph, "wstg%d" % i, [128, D], F32) for i in range(2)]
                bglu_sb = SB(ph, "bglu", [128, DEPTH * 4], F32)
                ys_b = [SB(ph, "ys%d" % i_, [128, 4, 512], BF16) for i_ in range(2)]
                ya_b = [SB(ph, "ya%d" % i_, [128, 8, 512], BF16) for i_ in range(2)]
                gs_b = [SB(ph, "gs%d" % i_, [128, 8, 512], BF16) for i_ in range(2)]
                ga_b = [SB(ph, "ga%d" % i_, [128, 8, 512], BF16) for i_ in range(2)]
                sg = SB(ph, "sg", [128, 512], F32)
                yssm = SB(ph, "yssm", [128, 4, 512], BF16)
                tA = SB(ph, "tA", [128, 512], F32)
                tB = SB(ph, "tB", [128, 512], F32)
                merged = SB(ph, "merged", [128, 8, 512], BF16)
                xsb = [SB(ph, "xs%d" % i_, [128, 4, D], F32) for i_ in range(2)]
                vt = SB(ph, "vt", [128, D], F32)
                stats = SB(ph, "stats", [128, 2, 6], F32)
                mv = SB(ph, "mv", [128, 2], F32)
                rstd = SB(ph, "rstd", [128, 1], F32)
                pg = [PSU(ph, "pg%d" % i, [128, 512]) for i in range(2)]
                pA = [PSU(ph, "pA%d" % i, [128, 512]) for i in range(2)]
                pB = [PSU(ph, "pB%d" % i, [128, 512]) for i in range(2)]
                phh = PSU(ph, "phh", [128, D])
                P.dma("sp", bglu_sb[:], bglu, writes=["bglu"])
                lnb = SB(ph, "lnb", [128, 2, D], F32)
                for i_ in range(2):
                    P.dma("sp", lnb[:, i_, :], lnp[l, i_, :].partition_broadcast(128), writes=["lnb"])
                P.dma("pool", wglu_sb[:], w_glu[l].rearrange("(k p) n -> p k n", p=128), writes=["wglu"])
                P.dma("pool", pssm_sb[:], p_ssm[l].rearrange("(k p) n -> p k n", p=128), writes=["pssm"])
                for k in range(8):
                    P.dma("pool", pattn_sb[:, k, :], p_attn[l, k * 128:(k + 1) * 128, :], writes=["pattn"])
                for k in range(8):
                    b = k % 2
                    P.dma("sp", wstg[b][:], w_out[l, k * 128:(k + 1) * 128, :], writes=[("wstg", b)])
                    P.op("dve", lambda e: e.tensor_tensor(out=wout_sb[:, k, :], in0=wstg[b][:], in1=G12[:, 0, :], op=ALU.mult),
                         reads=[("wstg", b), "G12"], writes=["wout"])
                cnt = 0
                for T in range(NS):
                    tsl = slice(T * 512, (T + 1) * 512)
                    pb = T % 2
                    ys_sb, ya_sb, gs_sb, ga_sb, xs = ys_b[pb], ya_b[pb], gs_b[pb], ga_b[pb], xsb[pb]
                    YS, YA, GS, GA, XT = ("ys", pb), ("ya", pb), ("gs", pb), ("ga", pb), ("xs", pb)
                    P.dma("sp", ys_sb[:], ysT[:, tsl].rearrange("(k p) t -> p k t", p=128), reads=[("ysT", T)], writes=[YS])
                    P.dma("sp", ya_sb[:], yaT[:, tsl].rearrange("(k p) t -> p k t", p=128), reads=[("yaT", T)], writes=[YA])
                    P.dma("sp", gs_sb[:], gsT[:, tsl].rearrange("(k p) t -> p k t", p=128), reads=[("gsT", T)], writes=[GS])
                    P.dma("sp", ga_sb[:], gaT[:, tsl].rearrange("(k p) t -> p k t", p=128), reads=[("gaT", T)], writes=[GA])
                    P.dma("sp", xs[:], xsrc[tsl, :].rearrange("(s p) d -> p s d", p=128), writes=[XT])
                    for m in range(4):
                        b = m % 2
                        for k in range(4):
                            P.op("pe", lambda e: e.matmul(pg[b][:], lhsT=wglu_sb[:, k, m * 128:(m + 1) * 128], rhs=ys_sb[:, k, :], start=(k == 0), stop=(k == 3)),
                                 reads=["wglu", YS], writes=[("pg", b)])
                        P.op("act", lambda e: e.activation(out=sg[:], in_=pg[b][:], func=AF.Sigmoid, bias=bglu_sb[:, l * 4 + m:l * 4 + m + 1], scale=1.0),
                             reads=[("pg", b), "bglu"], writes=["sg"])
                        P.op("dve", lambda e: e.tensor_tensor(out=yssm[:, m, :], in0=ys_sb[:, m, :], in1=sg[:], op=ALU.mult),
                             reads=[YS, "sg"], writes=["yssm"])
                    for n in range(8):
                        b = n % 2
                        for k in range(4):
                            P.op("pe", lambda e: e.matmul(pA[b][:], lhsT=pssm_sb[:, k, n * 128:(n + 1) * 128], rhs=yssm[:, k, :], start=(k == 0), stop=(k == 3)),
                                 reads=["pssm", "yssm"], writes=[("pA", b)])
                        for k in range(8):
                            P.op("pe", lambda e: e.matmul(pB[b][:], lhsT=pattn_sb[:, k, n * 128:(n + 1) * 128], rhs=ya_sb[:, k, :], start=(k == 0), stop=(k == 7)),
                                 reads=["pattn", YA], writes=[("pB", b)])
                        P.op("dve", lambda e: e.tensor_tensor(out=tA[:], in0=pA[b][:], in1=gs_sb[:, n, :], op=ALU.mult), reads=[("pA", b), GS], writes=["tA"])
                        P.op("dve", lambda e: e.tensor_tensor(out=tB[:], in0=pB[b][:], in1=ga_sb[:, n, :], op=ALU.mult), reads=[("pB", b), GA], writes=["tB"])
                        P.op("pool", lambda e: e.tensor_tensor(out=merged[:, n, :], in0=tA[:], in1=tB[:], op=ALU.add), reads=["tA", "tB"], writes=["merged"])
                    for s in range(4):
                        for hf in range(2):
                            for k in range(8):
                                P.op("pe", lambda e: e.matmul(phh[:, hf * 512:(hf + 1) * 512], lhsT=merged[:, k, s * 128:(s + 1) * 128],
                                                              rhs=wout_sb[:, k, hf * 512:(hf + 1) * 512], start=(k == 0), stop=(k == 7)),
                                     reads=["merged", "wout"], writes=["phh"])
                        for hf in range(2):
                            P.op("dve", lambda e: e.scalar_tensor_tensor(out=vt[:, hf * 512:(hf + 1) * 512], in0=xs[:, s, hf * 512:(hf + 1) * 512], scalar=ALPHA,
                                                                         in1=phh[:, hf * 512:(hf + 1) * 512], op0=ALU.mult, op1=ALU.add),
                                 reads=[XT, "phh"], writes=["vt"])
                        layer_norm_tile(vt, xs[:, s, :], lnb, stats, mv, rstd, XT)
                    P.dma("pool", x1d[tsl, :].rearrange("(s p) d -> p s d", p=128), xs[:], reads=[XT], writes=[("x1d", T)])
                P.barrier()

        def phase_e(l, xdst):
            NT2 = L // 512
            HM = 11
            for hp in range(2):
              with ExitStack() as ph:
                wgu_sb = SB(ph, "wgu", [128, 8, 2 * HM * 128], BF16)
                wdn_sb = SB(ph, "wdn", [128, HM, D], BF16)
                wstg = [SB(ph, "wstg%d" % i, [128, D], F32) for i in range(2)]
                xsb = [SB(ph, "xs%d" % i_, [128, 4, D], F32) for i_ in range(2)]
                fpb = [SB(ph, "fp%d" % i_, [128, 4, D], F32) for i_ in range(2)]
                u2T = SB(ph, "u2T", [128, 8, 512], BF16)
                sa = SB(ph, "sa", [128, 512], F32)
                hT = SB(ph, "hT", [128, HM, 512], BF16)
                vt = SB(ph, "vt", [128, D], F32)
                stats = SB(ph, "stats", [128, 2, 6], F32)
                mv = SB(ph, "mv", [128, 2], F32)
                rstd = SB(ph, "rstd", [128, 1], F32)
                ptp = [PSU(ph, "ptp%d" % i, [128, 512]) for i in range(2)]
                pa = [PSU(ph, "pa%d" % i, [128, 512]) for i in range(2)]
                pb = [PSU(ph, "pb%d" % i, [128, 512]) for i in range(2)]
                pf = PSU(ph, "pf", [128, D])
                W = HM * 128
                lnb = SB(ph, "lnb", [128, 2, D], F32)
                for i_ in range(2):
                    P.dma("sp", lnb[:, i_, :], lnp[l, 2 + i_, :].partition_broadcast(128), writes=["lnb"])
                for k in range(8):
                    P.dma("pool", wgu_sb[:, k, 0:W], w_gu[l, k * 128:(k + 1) * 128, hp * W:(hp + 1) * W], writes=["wgu"])
                    P.dma("pool", wgu_sb[:, k, W:2 * W], w_gu[l, k * 128:(k + 1) * 128, DFF + hp * W:DFF + (hp + 1) * W], writes=["wgu"])
                for k in range(HM):
                    b = k % 2
                    r0 = (hp * HM + k) * 128
                    P.dma("sp", wstg[b][:], w_dn[l, r0:r0 + 128, :], writes=[("wstg", b)])
                    P.op("dve", lambda e: e.tensor_tensor(out=wdn_sb[:, k, :], in0=wstg[b][:], in1=G12[:, 1, :], op=ALU.mult),
                         reads=[("wstg", b), "G12"], writes=["wdn"])
                for T in range(NT2):
                    tsl = slice(T * 512, (T + 1) * 512)
                    xs, fp_sb = xsb[T % 2], fpb[T % 2]
                    XT, FT = ("xs", T % 2), ("fp", T % 2)
                    P.dma("sp", xs[:], x1d[tsl, :].rearrange("(s p) d -> p s d", p=128), reads=[("x1d", T)], writes=[XT])
                    if hp == 1:
                        P.dma("sp", fp_sb[:], fpart[tsl, :].rearrange("(s p) d -> p s d", p=128), reads=[("fpart", T)], writes=[FT])
                    for k in range(8):
                        b = k % 2
                        for s in range(4):
                            P.op("pe", lambda e: e.transpose(ptp[b][:, s * 128:(s + 1) * 128], xs[:, s, k * 128:(k + 1) * 128], ident[:]),
                                 reads=[XT, "ident"], writes=[("ptp", b)])
                        P.op("act", lambda e: e.activation(out=u2T[:, k, :], in_=ptp[b][:], func=AF.Identity,
                                                           scale=modT[:, 32 + k:33 + k], bias=modT[:, 24 + k:25 + k]),
                             reads=[("ptp", b), "modT"], writes=["u2T"])
                    for m in range(HM):
                        b = m % 2
                        for k in range(8):
                            P.op("pe", lambda e: e.matmul(pa[b][:], lhsT=wgu_sb[:, k, m * 128:(m + 1) * 128], rhs=u2T[:, k, :], start=(k == 0), stop=(k == 7)),
                                 reads=["wgu", "u2T"], writes=[("pa", b)])
                        for k in range(8):
                            P.op("pe", lambda e: e.matmul(pb[b][:], lhsT=wgu_sb[:, k, W + m * 128:W + (m + 1) * 128], rhs=u2T[:, k, :], start=(k == 0), stop=(k == 7)),
                                 reads=["wgu", "u2T"], writes=[("pb", b)])
                        P.op("act", lambda e: e.activation(out=sa[:], in_=pa[b][:], func=AF.Silu), reads=[("pa", b)], writes=["sa"])
                        P.op("dve", lambda e: e.tensor_tensor(out=hT[:, m, :], in0=sa[:], in1=pb[b][:], op=ALU.mult), reads=["sa", ("pb", b)], writes=["hT"])
                    for s in range(4):
                        for hf in range(2):
                            for k in range(HM):
                                P.op("pe", lambda e: e.matmul(pf[:, hf * 512:(hf + 1) * 512], lhsT=hT[:, k, s * 128:(s + 1) * 128],
                                                              rhs=wdn_sb[:, k, hf * 512:(hf + 1) * 512], start=(k == 0), stop=(k == HM - 1)),
                                     reads=["hT", "wdn"], writes=["pf"])
                        if hp == 0:
                            for hf in range(2):
                                P.op("dve", lambda e: e.tensor_copy(out=fp_sb[:, s, hf * 512:(hf + 1) * 512], in_=pf[:, hf * 512:(hf + 1) * 512]),
                                     reads=["pf"], writes=[FT])
                        else:
                            for hf in range(2):
                                P.op("dve", lambda e: e.tensor_tensor(out=vt[:, hf * 512:(hf + 1) * 512], in0=fp_sb[:, s, hf * 512:(hf + 1) * 512],
                                                                      in1=pf[:, hf * 512:(hf + 1) * 512], op=ALU.add), reads=[FT, "pf"], writes=["vt"])
                            P.op("dve", lambda e: e.scalar_tensor_tensor(out=vt[:], in0=xs[:, s, :], scalar=ALPHA, in1=vt[:], op0=ALU.mult, op1=ALU.add),
                                 reads=[XT, "vt"], writes=["vt"])
                            layer_norm_tile(vt, xs[:, s, :], lnb, stats, mv, rstd, XT)
                    if hp == 0:
                        P.dma("pool", fpart[tsl, :].rearrange("(s p) d -> p s d", p=128), fp_sb[:], reads=[FT], writes=[("fpart", T)])
                    else:
                        P.dma("pool", xdst[tsl, :].rearrange("(s p) d -> p s d", p=128), xs[:], reads=[XT], writes=[("xdst", T)])
                P.barrier()


        def phase_dsa(l):
            SC = 1.0 / math.sqrt(128.0)
            with ExitStack() as ph:
                kT_sb = SB(ph, "kTsb", [128, 2, L], BF16)
                v_sb = SB(ph, "vsb", [128, NT, 2, 132], BF16)
                ki2 = SB(ph, "ki2", [128, L], BF16)
                score = SB(ph, "score", [128, L], F32)
                mbs = [SB(ph, "mb%d" % i, [128, L], BF16) for i in range(2)]
                junk = SB(ph, "junk", [128, L], mybir.dt.uint8)
                R = SB(ph, "R", [128, 8, 512], BF16)
                qT_b = [SB(ph, "qTi%d" % i, [128, 8, 128], BF16) for i in range(3)]
                qiT_b = [SB(ph, "qiTi%d" % i, [128, 8, 128], BF16) for i in range(2)]
                w_b = [SB(ph, "wi_%d" % i, [128, 8], F32) for i in range(2)]
                diag_b = [SB(ph, "diag%d" % i, [128, 8, 128], BF16) for i in range(2)]
                PT = [SB(ph, "PT%d" % i, [128, 512], BF16) for i in range(3)]
                o_b = [SB(ph, "osb%d" % i, [128, D], BF16) for i in range(2)]
                oT_b = [SB(ph, "oTsb%d" % i, [128, 8, 128], BF16) for i in range(2)]
                sm = SB(ph, "sm", [128, 16], F32)
                nrm = SB(ph, "nrm", [128, 8], F32)
                tneg = SB(ph, "tneg", [128, 1], F32)
                psx = [PSU(ph, "psx%d" % i, [128, 512]) for i in range(2)]
                pss = PSU(ph, "pss", [128, 512])
                psST = [PSU(ph, "psST%d" % i, [128, 512]) for i in range(2)]
                psO = [PSU(ph, "psO%d" % i, [128, 512]) for i in range(2)]
                psT = PSU(ph, "psT", [128, 8, 128], BF16)
                P.dma("sp", kT_sb[:], kT.rearrange("(g p) t -> p g t", p=128), writes=["kTsb"])
                for g in range(2):
                    P.dma("sp", v_sb[:, :, g, 0:128], vd[:, g * 128:(g + 1) * 128].rearrange("(j p) d -> p j d", p=128), writes=["vsb"])
                P.op("pool", lambda e: e.memset(v_sb[:, :, :, 128:129], 1.0), writes=["vsb"])
                P.dma("sp", ki2[0:64, :], kiT, writes=["ki2"])
                P.dma("sp", ki2[64:128, :], kiT, writes=["ki2"])
                P.op("pool", lambda e: e.memset(tneg[:], -1.0e29), writes=["tneg"])
                for pb_ in range(2):
                    P.op("pool", lambda e: e.memset(qiT_b[pb_][:], 0.0), writes=[("qiTi", pb_)])
                cnts = {"pt": 0, "st": 0}

                def emit_load(i):
                    pb = i % 2
                    qs = slice(i * 128, (i + 1) * 128)
                    qiT_i, w_i, diag = qiT_b[pb], w_b[pb], diag_b[pb]
                    qsrc = qiT[:, qs].rearrange("(m two d) t -> two d m t", two=2, d=64)
                    qdst = qiT_i[:].rearrange("p (m two) t -> p m two t", two=2)
                    P.dma("sp", qdst[0:64, :, 0, :], qsrc[0], writes=[("qiTi", pb)])
                    P.dma("sp", qdst[64:128, :, 1, :], qsrc[1], writes=[("qiTi", pb)])
                    P.dma("sp", w_i[:], wid[qs, :], writes=[("wi_", pb)])
                    P.dma("sp", qT_b[i % 3][:], qT[:, qs].rearrange("(h p) t -> p h t", p=128), writes=[("qTi", i % 3)])
                    for h in range(8):
                        P.op("pool", lambda e: e.tensor_scalar(out=diag[:, h, :], in0=identb[:], scalar1=w_i[:, h:h + 1], scalar2=None, op0=ALU.mult),
                             reads=["identb", ("wi_", pb)], writes=[("diag", pb)])

                def emit_idx(i):
                    pb = i % 2
                    n = 128 * (i + 1)
                    nch = (n + 511) // 512
                    qiT_i, diag = qiT_b[pb], diag_b[pb]
                    for c in range(nch):
                        wc = min(512, n - c * 512)
                        ks = slice(c * 512, c * 512 + wc)

                        def xmm(h):
                            P.op("pe", lambda e: e.matmul(psx[h % 2][:, 0:wc], lhsT=qiT_i[:, h, :], rhs=ki2[:, ks],
                                                          start=True, stop=True), reads=[("qiTi", pb), "ki2"], writes=[("psx", h % 2)])
                        xmm(0)
                        xmm(1)
                        for h in range(8):
                            P.op("act", lambda e: e.activation(out=R[:, h, 0:wc], in_=psx[h % 2][:, 0:wc], func=AF.Relu),
                                 reads=[("psx", h % 2)], writes=[("R", h)])
                            P.op("pe", lambda e: e.matmul(pss[:, 0:wc], lhsT=diag[:, h, :], rhs=R[:, h, 0:wc], start=(h == 0), stop=(h == 7)),
                                 reads=[("diag", pb), ("R", h)], writes=["pss"])
                            if h + 2 < 8:
                                xmm(h + 2)
                        last = (c == nch - 1)
                        wcopy = wc - 128 if last else wc
                        if wcopy > 0:
                            P.op("act", lambda e: e.activation(out=score[:, c * 512:c * 512 + wcopy], in_=pss[:, 0:wcopy], func=AF.Copy),
                                 reads=["pss"], writes=["score"])
                        if last:
                            P.op("dve", lambda e: e.tensor_tensor(out=score[:, n - 128:n], in0=pss[:, wc - 128:wc], in1=causal[:], op=ALU.add),
                                 reads=["pss", "causal"], writes=["score"])

                def emit_thr(i):
                    pb = i % 2
                    n = 128 * (i + 1)
                    mb = mbs[pb]
                    if i >= 2:
                        P.op("dve", lambda e: e.tensor_reduce(out=sm[:, 0:1], in_=score[:, 0:n], axis=AX.X, op=ALU.max), reads=["score"], writes=["sm"])
                        P.op("dve", lambda e: e.tensor_reduce(out=sm[:, 1:2], in_=score[:, 0:n - 128], axis=AX.X, op=ALU.min), reads=["score"], writes=["sm"])
                        P.op("dve", lambda e: e.tensor_scalar(out=sm[:, 2:3], in0=sm[:, 1:2], scalar1=-1.0, scalar2=None, op0=ALU.add), reads=["sm"], writes=["sm"])
                        P.op("dve", lambda e: e.tensor_tensor(out=sm[:, 3:4], in0=sm[:, 0:1], in1=sm[:, 2:3], op=ALU.subtract), reads=["sm"], writes=["sm"])
                        for k in range(NBIS):
                            ck = 2.0 ** (-(k + 1))
                            P.op("dve", lambda e: e.scalar_tensor_tensor(out=sm[:, 4:5], in0=sm[:, 3:4], scalar=ck, in1=sm[:, 2:3], op0=ALU.mult, op1=ALU.add),
                                 reads=["sm"], writes=["sm"])
                            P.op("dve", lambda e: e.tensor_scalar(out=junk[:, 0:n], in0=score[:, 0:n], scalar1=sm[:, 4:5], scalar2=0.0, op0=ALU.is_gt, op1=ALU.add,
                                                                  accum_out=sm[:, 5:6]), reads=["score", "sm"], writes=["sm", "junk"])
                            P.op("dve", lambda e: e.tensor_scalar(out=sm[:, 6:7], in0=sm[:, 5:6], scalar1=255.5, scalar2=sm[:, 3:4], op0=ALU.is_ge, op1=ALU.mult),
                                 reads=["sm"], writes=["sm"])
                            P.op("dve", lambda e: e.scalar_tensor_tensor(out=sm[:, 2:3], in0=sm[:, 6:7], scalar=ck, in1=sm[:, 2:3], op0=ALU.mult, op1=ALU.add),
                                 reads=["sm"], writes=["sm"])
                        thr = sm[:, 2:3]
                    else:
                        thr = tneg[:, 0:1]
                    P.op("dve", lambda e: e.tensor_scalar(out=mb[:, 0:n], in0=score[:, 0:n], scalar1=thr, scalar2=-30000.0, op0=ALU.is_le, op1=ALU.mult),
                         reads=["score", "sm", "tneg"], writes=[("mb", pb)])

                def emit_att(i):
                    pb = i % 2
                    qs = slice(i * 128, (i + 1) * 128)
                    qT_i, mb, o_sb, oT_sb = qT_b[i % 3], mbs[pb], o_b[pb], oT_b[pb]
                    for g in range(2):
                        def qk(j):
                            r = j % 2
                            P.op("pe", lambda e: e.matmul(psST[r][:], lhsT=kT_sb[:, g, j * 128:(j + 1) * 128], rhs=qT_i[:, 4 * g:4 * g + 4, :],
                                                          start=True, stop=False), reads=["kTsb", ("qTi", i % 3)], writes=[("psST", r)])
                            P.op("pe", lambda e: e.matmul(psST[r][:], lhsT=mb[:, j * 128:(j + 1) * 128], rhs=ident4b[:],
                                                          start=False, stop=True), reads=[("mb", pb), "ident4b"], writes=[("psST", r)])
                        qk(0)
                        for j in range(i + 1):
                            r = j % 2
                            r3 = cnts["pt"] % 3
                            cnts["pt"] += 1
                            P.op("act", lambda e: e.activation(out=PT[r3][:], in_=psST[r][:], func=AF.Exp, scale=SC),
                                 reads=[("psST", r)], writes=[("PT", r3)])
                            if j + 1 <= i:
                                qk(j + 1)
                            for hh in range(4):
                                off = (hh % 2) * 256
                                P.op("pe", lambda e: e.matmul(psO[hh // 2][:, off:off + 129], lhsT=PT[r3][:, hh * 128:(hh + 1) * 128], rhs=v_sb[:, j, g, 0:129],
                                                              start=(j == 0 and hh % 2 == 0), stop=(j == i and hh % 2 == 1), skip_group_check=True),
                                     reads=[("PT", r3), "vsb"], writes=[("psO", hh // 2)])
                        for hh in range(4):
                            off = (hh % 2) * 256
                            P.op("act", lambda e: e.activation(out=nrm[:, hh:hh + 1], in_=psO[hh // 2][:, off + 128:off + 129], func=AF.Ln),
                                 reads=[("psO", hh // 2)], writes=["nrm"])
                            P.op("act", lambda e: e.activation(out=nrm[:, 4 + hh:5 + hh], in_=nrm[:, hh:hh + 1], func=AF.Exp, scale=-1.0),
                                 reads=["nrm"], writes=["nrm"])
                            P.op("act", lambda e: e.activation(out=o_sb[:, (4 * g + hh) * 128:(4 * g + hh + 1) * 128], in_=psO[hh // 2][:, off:off + 128],
                                                               func=AF.Identity, scale=nrm[:, 4 + hh:5 + hh]),
                                 reads=[("psO", hh // 2), "nrm"], writes=[("osb", pb)])
                    for h in range(8):
                        P.op("pe", lambda e: e.transpose(psT[:, h, :], o_sb[:, h * 128:(h + 1) * 128], identb[:]), reads=[("osb", pb), "identb"], writes=["psT"])
                    P.op("act", lambda e: e.activation(out=oT_sb[:], in_=psT[:], func=AF.Copy), reads=["psT"], writes=[("oTsb", pb)])
                    P.dma("pool", yaT[:, qs].rearrange("(h p) t -> p h t", p=128), oT_sb[:], reads=[("oTsb", pb)], writes=[("yaT", i)])

                emit_load(0)
                for i in range(NT + 1):
                    if i + 1 < NT:
                        emit_load(i + 1)
                    if i < NT:
                        emit_idx(i)
                        emit_thr(i)
                    if i >= 1:
                        emit_att(i - 1)
                P.barrier()

        def phase_ssm(l):
            TK = "ssmprep"

            def TT(o, a, b, op, eng="dve"):
                P.op(eng, lambda e: e.tensor_tensor(out=o, in0=a, in1=b, op=op), reads=[TK], writes=[TK])

            def TS(o, a, s1, op0, s2=None, op1=None):
                if op1 is None:
                    P.op("dve", lambda e: e.tensor_scalar(out=o, in0=a, scalar1=s1, scalar2=None, op0=op0), reads=[TK], writes=[TK])
                else:
                    P.op("dve", lambda e: e.tensor_scalar(out=o, in0=a, scalar1=s1, scalar2=s2, op0=op0, op1=op1), reads=[TK], writes=[TK])

            def ACT(o, a, func, scale=1.0):
                P.op("act", lambda e: e.activation(out=o, in_=a, func=func, scale=scale), reads=[TK], writes=[TK])

            def zoh(st, tmp, nm, lam_re_d, lam_im_d, ldt_d, F):
                pw_re = SB(st, nm + "pwre", [128, 9, F], F32)
                pw_im = SB(st, nm + "pwim", [128, 9, F], F32)
                fr = SB(st, nm + "fr", [128, F], F32)
                fi = SB(st, nm + "fi", [128, F], F32)
                t = [SB(tmp, nm + "t%d" % i, [128, F], F32) for i in range(10)]
                lr, li, dt, mag, er, ei, a, b, c_, d_ = t
                P.dma("sp", lr[:], lam_re_d, writes=[TK])
                P.dma("sp", li[:], lam_im_d, writes=[TK])
                P.dma("sp", dt[:], ldt_d, writes=[TK])
                ACT(dt[:], dt[:], AF.Exp)
                TT(a[:], lr[:], dt[:], ALU.mult)
                ACT(mag[:], a[:], AF.Exp)
                TT(a[:], li[:], dt[:], ALU.mult)
                ACT(ei[:], a[:], AF.Sin, scale=1.0 / 16.0)
                ACT(b[:], a[:], AF.Sin, scale=1.0 / 32.0)
                TT(b[:], b[:], b[:], ALU.mult)
                TS(er[:], b[:], -2.0, ALU.mult, 1.0, ALU.add)
                for _ in range(4):
                    TT(a[:], er[:], er[:], ALU.mult)
                    TT(b[:], ei[:], ei[:], ALU.mult)
                    TT(c_[:], er[:], ei[:], ALU.mult)
                    TT(er[:], a[:], b[:], ALU.subtract)
                    TS(ei[:], c_[:], 2.0, ALU.mult)
                ar, ai = pw_re[:, 1, :], pw_im[:, 1, :]
                TT(ar, mag[:], er[:], ALU.mult)
                TT(ai, mag[:], ei[:], ALU.mult)
                P.op("dve", lambda e: e.memset(pw_re[:, 0, :], 1.0), reads=[TK], writes=[TK])
                P.op("dve", lambda e: e.memset(pw_im[:, 0, :], 0.0), reads=[TK], writes=[TK])
                TT(a[:], lr[:], lr[:], ALU.mult)
                TT(b[:], li[:], li[:], ALU.mult)
                TT(a[:], a[:], b[:], ALU.add)
                P.op("dve", lambda e: e.reciprocal(out=a[:], in_=a[:]), reads=[TK], writes=[TK])
                TS(b[:], ar, -1.0, ALU.add)
                TT(c_[:], b[:], lr[:], ALU.mult)
                TT(d_[:], ai, li[:], ALU.mult)
                TT(c_[:], c_[:], d_[:], ALU.add)
                TT(fr[:], c_[:], a[:], ALU.mult)
                TT(c_[:], ai, lr[:], ALU.mult)
                TT(d_[:], b[:], li[:], ALU.mult)
                TT(c_[:], c_[:], d_[:], ALU.subtract)
                TT(fi[:], c_[:], a[:], ALU.mult)
                for k in range(2, 9):
                    cmul(pw_re[:, k, :], pw_im[:, k, :], pw_re[:, k - 1, :], pw_im[:, k - 1, :], ar, ai, a[:], b[:])
                return pw_re, pw_im, fr, fi

            def cmul(o_re, o_im, x_re, x_im, y_re, y_im, t1, t2):
                TT(t1, x_re, y_re, ALU.mult)
                TT(t2, x_im, y_im, ALU.mult)
                TT(o_re, t1, t2, ALU.subtract)
                TT(t1, x_re, y_im, ALU.mult)
                TT(t2, x_im, y_re, ALU.mult)
                TT(o_im, t1, t2, ALU.add)

            with ExitStack() as ph:
                B8 = SB(ph, "B8", [128, 4, 8, 2, 64], F32)
                Wr = SB(ph, "Wr", [128, 9, 16, 16], F32)
                Wi = SB(ph, "Wi", [128, 9, 16, 16], F32)
                BbS_re = SB(ph, "BbSre", [128, 16, 16], F32)
                BbS_im = SB(ph, "BbSim", [128, 16, 16], F32)
                lev_re = SB(ph, "levre", [128, NLEV, 16], F32)
                lev_im = SB(ph, "levim", [128, NLEV, 16], F32)
                lev_nim = SB(ph, "levnim", [128, NLEV, 16], F32)
                dcol_sb = SB(ph, "dcol", [128, DEPTH * 4], F32)
                P.dma("sp", dcol_sb[:], dcol, writes=[TK])
                with ExitStack() as tmp:
                    pwR_re, pwR_im, frR, fiR = zoh(tmp, tmp, "R", lamR_re[l], lamR_im[l], ldtR[l], 256)
                    bre = SB(tmp, "bre", [128, 256], F32)
                    bim = SB(tmp, "bim", [128, 256], F32)
                    Bb_re = SB(tmp, "Bbre", [128, 256], F32)
                    Bb_im = SB(tmp, "Bbim", [128, 256], F32)
                    u1 = SB(tmp, "u1", [128, 256], F32)
                    u2 = SB(tmp, "u2", [128, 256], F32)
                    P.dma("sp", bre[:], bR_re[l], writes=[TK])
                    P.dma("sp", bim[:], bR_im[l], writes=[TK])
                    cmul(Bb_re[:], Bb_im[:], frR[:], fiR[:], bre[:], bim[:], u1[:], u2[:])
                    v4 = lambda ap: ap.rearrange("p (f q) -> p f q", q=64)
                    for s_ in range(8):
                        k = 7 - s_
                        cmul(B8[:, :, s_, 0, :], B8[:, :, s_, 1, :], v4(pwR_re[:, k, :]), v4(pwR_im[:, k, :]), v4(Bb_re[:]), v4(Bb_im[:]), v4(u1[:]), v4(u2[:]))
                    pwS_re, pwS_im, frS, fiS = zoh(tmp, tmp, "S", lamS_re[l], lamS_im[l], ldtS[l], 16)
                    cre = SB(tmp, "cre", [128, 16, 16], F32)
                    cim = SB(tmp, "cim", [128, 16, 16], F32)
                    bsr = SB(tmp, "bsr", [128, 16, 16], F32)
                    bsi = SB(tmp, "bsi", [128, 16, 16], F32)
                    w1 = SB(tmp, "w1", [128, 16, 16], F32)
                    w2 = SB(tmp, "w2", [128, 16, 16], F32)
                    P.dma("sp", cre[:], cS_re[l].rearrange("p (q i) -> p q i", i=16), writes=[TK])
                    P.dma("sp", cim[:], cS_im[l].rearrange("p (q i) -> p q i", i=16), writes=[TK])
                    P.dma("sp", bsr[:], bS_re[l].rearrange("p (q i) -> p q i", i=16), writes=[TK])
                    P.dma("sp", bsi[:], bS_im[l].rearrange("p (q i) -> p q i", i=16), writes=[TK])
                    bc = lambda ap: ap.unsqueeze(2).to_broadcast([128, 16, 16])
                    cmul(BbS_re[:], BbS_im[:], bc(frS[:]), bc(fiS[:]), bsr[:], bsi[:], w1[:], w2[:])
                    for k in range(9):
                        er_k, ei_k = bc(pwS_re[:, k, :]), bc(pwS_im[:, k, :])
                        TT(w1[:], cre[:], er_k, ALU.mult)
                        TT(w2[:], cim[:], ei_k, ALU.mult)
                        TT(Wr[:, k, :, :], w1[:], w2[:], ALU.subtract)
                        TT(w1[:], cre[:], ei_k, ALU.mult)
                        TT(w2[:], cim[:], er_k, ALU.mult)
                        TT(w1[:], w1[:], w2[:], ALU.add)
                        TS(Wi[:, k, :, :], w1[:], -1.0, ALU.mult)
                    P.op("dve", lambda e: e.tensor_copy(out=lev_re[:, 0, :], in_=pwS_re[:, 8, :]), reads=[TK], writes=[TK])
                    P.op("dve", lambda e: e.tensor_copy(out=lev_im[:, 0, :], in_=pwS_im[:, 8, :]), reads=[TK], writes=[TK])
                    for d_ in range(1, NLEV):
                        cmul(lev_re[:, d_, :], lev_im[:, d_, :], lev_re[:, d_ - 1, :], lev_im[:, d_ - 1, :], lev_re[:, d_ - 1, :], lev_im[:, d_ - 1, :],
                             w1[:, 0, :], w2[:, 0, :])
                    TS(lev_nim[:], lev_im[:], -1.0, ALU.mult)
                    P.barrier()
                if ssm_stop < 2:
                    return
                uT_ft = SB(ph, "uTft", [128, L], BF16)
                ys_t = SB(ph, "yst", [128, L], BF16)
                uT_de = SB(ph, "uTde", [128, 8, NC8], BF16)
                B8pad = SB(ph, "B8pad", [128, 4, 8, 2, 2, 64], BF16)
                Cpad = SB(ph, "Cpad", [128, 4, 2, 9, 128], BF16)
                BbSpad = SB(ph, "BbSpad", [128, 4, 2, 128], BF16)
                BD_sb = SB(ph, "BDsb", [128, 8, 128], BF16)
                sc_ = [[SB(ph, "scan%d%d" % (a_, b_), [128, NC8], F32) for b_ in range(2)] for a_ in range(2)]
                Hb = [[SB(ph, "Hb%d%d" % (q_, r_), [128, NC8], BF16) for r_ in range(2)] for q_ in range(4)]
                g1 = SB(ph, "g1", [128, CW], F32)
                g2_ = SB(ph, "g2", [128, CW], F32)
                psBD = PSU(ph, "psBD", [128, 1024])
                psX = [PSU(ph, "psX%d" % i, [128, 512]) for i in range(2)]
                psY = [PSU(ph, "psY%d" % i, [128, 512]) for i in range(2)]
                P.op("pool", lambda e: e.memset(Cpad[:], 0.0), writes=["Cpad"])
                P.op("pool", lambda e: e.memset(BbSpad[:], 0.0), writes=["BbSpad"])
                uTv = uT_ft[:].rearrange("p (c s) -> p c s", s=8)
                ysv = ys_t[:].rearrange("p (c s) -> p c s", s=8)
                xc = 0
                yc = 0
                for ft in range(4):
                    P.dma("sp", uT_ft[:], uTs[ft * 128:(ft + 1) * 128, :], writes=["uTft"])
                    for s_ in range(8):
                        if s_ % 2 == 0:
                            P.op("act", lambda e: e.activation(out=uT_de[:, s_, :], in_=uTv[:, :, s_], func=AF.Copy), reads=["uTft"], writes=["uTde"])
                        else:
                            P.op("pool", lambda e: e.tensor_copy(out=uT_de[:, s_, :], in_=uTv[:, :, s_]), reads=["uTft"], writes=["uTde"])
                    for gl in range(8):
                        P.op("dve", lambda e: e.tensor_scalar(out=B8pad[:, gl // 2, :, :, gl % 2, :], in0=B8[:, ft],
                                                              scalar1=rowmask[:, gl:gl + 1], scalar2=None, op0=ALU.mult),
                             reads=[TK, "rowmask"], writes=["B8pad"])
                    for ql in range(4):
                        qq = ft * 4 + ql
                        for g2 in range(2):
                            rows = slice(g2 * 64, (g2 + 1) * 64)
                            cs = slice((2 * ql + g2) * 16, (2 * ql + g2) * 16 + 16)
                            for ri, Wsrc, Bsrc in ((0, Wr, BbS_re), (1, Wi, BbS_im)):
                                P.op("pool", lambda e: e.tensor_copy(out=Cpad[rows, ql, ri, :, cs], in_=Wsrc[rows, :, qq, :]), reads=[TK], writes=["Cpad"])
                                P.op("pool", lambda e: e.tensor_copy(out=BbSpad[rows, ql, ri, cs], in_=Bsrc[rows, qq, :]), reads=[TK], writes=["BbSpad"])
                    if ssm_stop < 3:
                        continue
                    for hf in range(2):
                        n_ = 0
                        for ql in range(4):
                            for ri in range(2):
                                P.op("pe", lambda e: e.matmul(psBD[:, hf * 512:(hf + 1) * 512], lhsT=BbSpad[:, ql, ri, :], rhs=Cpad[:, ql, ri, 4 * hf:4 * hf + 4, :],
                                                              start=(n_ == 0), stop=(n_ == 7)), reads=["BbSpad", "Cpad"], writes=["psBD"])
                                n_ += 1
                    P.op("dve", lambda e: e.scalar_tensor_tensor(out=BD_sb[:, 0, :], in0=ident[:], scalar=dcol_sb[:, l * 4 + ft:l * 4 + ft + 1], in1=psBD[:, 0:128],
                                                                 op0=ALU.mult, op1=ALU.add), reads=["psBD", "ident", TK], writes=["BDsb"])
                    P.op("act", lambda e: e.activation(out=BD_sb[:, 1:8, :], in_=psBD[:, 128:1024].rearrange("p (t q) -> p t q", q=128), func=AF.Copy),
                         reads=["psBD"], writes=["BDsb"])
                    if ssm_stop < 4:
                        continue
                    for ql in range(4):
                        qq = ft * 4 + ql
                        for ri in range(2):
                            for cb in range(NCH):
                                b = xc % 2
                                xc += 1
                                for s_ in range(8):
                                    P.op("pe", lambda e: e.matmul(psX[b][:, 0:CW], lhsT=B8pad[:, ql, s_, ri, :, :], rhs=uT_de[:, s_, cb * CW:(cb + 1) * CW],
                                                                  start=(s_ == 0), stop=(s_ == 7)), reads=["B8pad", "uTde"], writes=[("psX", b)])
                                P.op("act", lambda e: e.activation(out=sc_[0][ri][:, cb * CW:(cb + 1) * CW], in_=psX[b][:, 0:CW], func=AF.Copy),
                                     reads=[("psX", b)], writes=[("scan", 0)])
                        cur = 0
                        for d_ in range(NLEV):
                            sh = 1 << d_
                            src, dst = sc_[cur], sc_[1 - cur]
                            lre = lev_re[:, d_, qq:qq + 1]
                            lim = lev_im[:, d_, qq:qq + 1]
                            lnim = lev_nim[:, d_, qq:qq + 1]
                            tk_s, tk_d = ("scan", cur), ("scan", 1 - cur)
                            m_ = NC8 - sh
                            P.op("dve", lambda e: e.scalar_tensor_tensor(out=dst[0][:, sh:], in0=src[0][:, 0:m_], scalar=lre, in1=src[0][:, sh:], op0=ALU.mult, op1=ALU.add),
                                 reads=[tk_s, TK], writes=[tk_d])
                            P.op("dve", lambda e: e.scalar_tensor_tensor(out=dst[0][:, sh:], in0=src[1][:, 0:m_], scalar=lnim, in1=dst[0][:, sh:], op0=ALU.mult, op1=ALU.add),
                                 reads=[tk_s, TK], writes=[tk_d])
                            P.op("dve", lambda e: e.scalar_tensor_tensor(out=dst[1][:, sh:], in0=src[1][:, 0:m_], scalar=lre, in1=src[1][:, sh:], op0=ALU.mult, op1=ALU.add),
                                 reads=[tk_s, TK], writes=[tk_d])
                            P.op("dve", lambda e: e.scalar_tensor_tensor(out=dst[1][:, sh:], in0=src[0][:, 0:m_], scalar=lim, in1=dst[1][:, sh:], op0=ALU.mult, op1=ALU.add),
                                 reads=[tk_s, TK], writes=[tk_d])
                            for ri in range(2):
                                P.op("pool", lambda e: e.tensor_copy(out=dst[ri][:, 0:sh], in_=src[ri][:, 0:sh]), reads=[tk_s], writes=[tk_d])
                            cur = 1 - cur
                        for ri in range(2):
                            P.op("pool", lambda e: e.memset(Hb[ql][ri][:, 0:1], 0.0), writes=[("Hb", ql)])
                            P.op("act", lambda e: e.activation(out=Hb[ql][ri][:, 1:NC8], in_=sc_[cur][ri][:, 0:NC8 - 1], func=AF.Copy),
                                 reads=[("scan", cur)], writes=[("Hb", ql)])
                    if ssm_stop < 5:
                        continue
                    for t_ in range(8):
                        for cb in range(NCH):
                            b = yc % 2
                            yc += 1
                            cs = slice(cb * CW, (cb + 1) * CW)
                            nmm = (t_ + 1) + 8
                            n_ = 0
                            for tau in range(t_ + 1):
                                P.op("pe", lambda e: e.matmul(psY[b][:, 0:CW], lhsT=BD_sb[:, tau, :], rhs=uT_de[:, t_ - tau, cs], start=(n_ == 0), stop=(n_ == nmm - 1)),
                                     reads=["BDsb", "uTde"], writes=[("psY", b)])
                                n_ += 1
                            for ql in range(4):
                                for ri in range(2):
                                    P.op("pe", lambda e: e.matmul(psY[b][:, 0:CW], lhsT=Cpad[:, ql, ri, t_ + 1, :], rhs=Hb[ql][ri][:, cs], start=(n_ == 0), stop=(n_ == nmm - 1)),
                                         reads=["Cpad", ("Hb", ql)], writes=[("psY", b)])
                                    n_ += 1
                            y_ = psY[b][:, 0:CW]
                            P.op("act", lambda e: e.activation(out=g1[:], in_=y_, func=AF.Square), reads=[("psY", b)], writes=["g1"])
                            P.op("dve", lambda e: e.tensor_scalar(out=g1[:], in0=g1[:], scalar1=0.044715, scalar2=1.0, op0=ALU.mult, op1=ALU.add), reads=["g1"], writes=["g1"])
                            P.op("dve", lambda e: e.tensor_tensor(out=g1[:], in0=g1[:], in1=y_, op=ALU.mult), reads=["g1", ("psY", b)], writes=["g1"])
                            P.op("act", lambda e: e.activation(out=g2_[:], in_=g1[:], func=AF.Sigmoid, scale=1.5957691216057308), reads=["g1"], writes=["g2"])
                            P.op("dve", lambda e: e.tensor_tensor(out=ysv[:, cs, t_], in0=g2_[:], in1=y_, op=ALU.mult), reads=["g2", ("psY", b)], writes=["yst"])
                    P.dma("sp", ysT[ft * 128:(ft + 1) * 128, :], ys_t[:], reads=["yst"], writes=[("ysT", ft)])
                P.barrier()

        for l in range(nlayers):
            xsrc = x_in if l == 0 else xres
            xdst = y_out if l == nlayers - 1 else xres

            if "P" in phases:
              with ExitStack() as ph:
                condcol = SB(ph, "condcol", [128, 8], F32)
                condrep = SB(ph, "condrep", [128, 8, 128], F32)
                wc = [SB(ph, "wc%d" % i, [128, 8, 512], F32) for i in range(2)]
                bcb = [SB(ph, "bcb%d" % i, [128, 512], F32) for i in range(2)]
                modB = SB(ph, "modB", [128, 6 * D], F32)
                psm = [PSU(ph, "psm%d" % i, [128, 512]) for i in range(2)]
                pst = PSU(ph, "pst", [128, 48])
                P.dma("sp", condcol[:], ccol, writes=["condcol"])
                P.op("act", lambda e: e.activation(out=condcol[:], in_=condcol[:], func=AF.Silu), reads=["condcol"], writes=["condcol"])
                for k in range(8):
                    P.op("dve", lambda e: e.tensor_copy(out=condrep[:, k, :], in_=condcol[:, k:k + 1].to_broadcast([128, 128])),
                         reads=["condcol"], writes=["condrep"])
                for j in range(12):
                    b = j % 2
                    P.dma("sp", wc[b][:], w_cond[l, :, j * 512:(j + 1) * 512].rearrange("(k p) n -> p k n", p=128), writes=[("wc", b)])
                    P.dma("sp", bcb[b][:], b_cond[l, j * 512:(j + 1) * 512].partition_broadcast(128), writes=[("bcb", b)])
                    for k in range(8):
                        P.op("pe", lambda e: e.matmul(psm[b][:], lhsT=condrep[:, k, :], rhs=wc[b][:, k, :], start=(k == 0), stop=(k == 7)),
                             reads=["condrep", ("wc", b)], writes=[("psm", b)])
                    P.op("dve", lambda e: e.tensor_tensor(out=modB[:, j * 512:(j + 1) * 512], in0=psm[b][:], in1=bcb[b][:], op=ALU.add),
                         reads=[("psm", b), ("bcb", b)], writes=["modB"])
                for j in range(48):
                    P.op("pe", lambda e: e.matmul(pst[:, j:j + 1], lhsT=modB[:, j * 128:(j + 1) * 128], rhs=ident[:, 0:1], start=True, stop=True),
                         reads=["modB", "ident"], writes=["pst"])
                P.op("dve", lambda e: e.tensor_copy(out=modT[:], in_=pst[:]), reads=["pst"], writes=["modT"])
                for a in (8, 32):
                    P.op("dve", lambda e: e.tensor_scalar(out=modT[:, a:a + 8], in0=modT[:, a:a + 8], scalar1=1.0, scalar2=None, op0=ALU.add),
                         reads=["modT"], writes=["modT"])
                P.op("dve", lambda e: e.tensor_scalar(out=G12[:, 0, :], in0=modB[:, 2048:3072], scalar1=1.0, scalar2=None, op0=ALU.add),
                     reads=["modB"], writes=["G12"])
                P.op("dve", lambda e: e.tensor_scalar(out=G12[:, 1, :], in0=modB[:, 5120:6144], scalar1=1.0, scalar2=None, op0=ALU.add),
                     reads=["modB"], writes=["G12"])
                if "dbg_mod" in dbg and l == 0:
                    P.dma("sp", dbg_mod, modB[:], reads=["modB"])
                P.barrier()

            if "A" in phases:
              with ExitStack() as ph:
                wi = SB(ph, "wi", [128, 8, DIN], BF16)
                wr = SB(ph, "wr", [128, 8, 1856], BF16)
                xs = SB(ph, "xs", [128, 4, D], F32)
                uT = SB(ph, "uT", [128, 8, 512], BF16)
                rc = [SB(ph, "rc%d" % i, [128, 512], F32) for i in range(4)]
                t1 = SB(ph, "t1", [128, 512], F32)
                t2 = SB(ph, "t2", [128, 512], F32)
                stg = {}
                for nm, nt_ in (("ssm", 4), ("q", 8), ("k", 2), ("qi", 4), ("ki", 1), ("gs", 8), ("ga", 8)):
                    stg[nm] = SB(ph, "stg_" + nm, [128, nt_, 512], BF16)
                vst = SB(ph, "vst", [128, 4, 256], BF16)
                wst = SB(ph, "wst", [128, 4, 8], F32)
                ptp = [PSU(ph, "ptp%d" % i, [128, 512]) for i in range(2)]
                pz = [PSU(ph, "pz%d" % i, [128, 512]) for i in range(2)]
                pzr = [PSU(ph, "pzr%d" % i, [128, 512]) for i in range(2)]
                pv = PSU(ph, "pv", [128, 512])
                for k in range(8):
                    P.dma("pool", wi[:, k, :], w_in[l, k * 128:(k + 1) * 128, :], writes=["wi"])
                ro = 0
                rinfo = {}
                for nm, off, ncol, half in (("q", OFF_Q, 1024, 64), ("k", OFF_K, 256, 64), ("qi", OFF_QI, 512, 32), ("ki", OFF_KI, 64, 32)):
                    rinfo[nm] = ro
                    for k in range(8):
                        src = wi[:, k, off:off + ncol].rearrange("p (h two f) -> p h two f", two=2, f=half)
                        dst = wr[:, k, ro:ro + ncol].rearrange("p (h two f) -> p h two f", two=2, f=half)
                        P.op("pool", lambda e: e.tensor_copy(out=dst[:, :, 0, :], in_=src[:, :, 1, :]), reads=["wi"], writes=["wr"])
                        P.op("pool", lambda e: e.tensor_copy(out=dst[:, :, 1, :], in_=src[:, :, 0, :]), reads=["wi"], writes=["wr"])
                    ro += ncol
                zc = [0]

                def zbank():
                    zc[0] += 1
                    return zc[0] % 2

                for T in range(NS):
                    tsl = slice(T * 512, (T + 1) * 512)
                    P.dma("sp", xs[:], xsrc[tsl, :].rearrange("(s p) d -> p s d", p=128), writes=["xs"])
                    for i, tab in enumerate((ropeA_c, ropeA_s, ropeB_c, ropeB_s)):
                        P.dma("sp", rc[i][:], tab[:, tsl], writes=[("rc", i)])
                    for k in range(8):
                        b = k % 2
                        for s in range(4):
                            P.op("pe", lambda e: e.transpose(ptp[b][:, s * 128:(s + 1) * 128], xs[:, s, k * 128:(k + 1) * 128], ident[:]),
                                 reads=["xs", "ident"], writes=[("ptp", b)])
                        P.op("act", lambda e: e.activation(out=uT[:, k, :], in_=ptp[b][:], func=AF.Identity,
                                                           scale=modT[:, 8 + k:9 + k], bias=modT[:, k:k + 1]),
                             reads=[("ptp", b), "modT"], writes=["uT"])
                    for nm, off, ntl, rows, kind, dst in (("ssm", OFF_SSM, 4, 128, "copy", uTs), ("q", OFF_Q, 8, 128, "ropeA", qT),
                                                         ("k", OFF_K, 2, 128, "ropeA", kT), ("qi", OFF_QI, 4, 128, "ropeB", qiT),
                                                         ("ki", OFF_KI, 1, 64, "ropeB", kiT), ("gs", OFF_GS, 8, 128, "sig", gsT),
                                                         ("ga", OFF_GA, 8, 128, "sig", gaT)):
                        st = stg[nm]
                        for m in range(ntl):
                            b = zbank()
                            for k in range(8):
                                P.op("pe", lambda e: e.matmul(pz[b][0:rows, :], lhsT=wi[:, k, off + m * 128: off + m * 128 + rows], rhs=uT[:, k, :],
                                                              start=(k == 0), stop=(k == 7)),
                                     reads=["wi", "uT"], writes=[("pz", b)])
                            if kind.startswith("rope"):
                                ro = rinfo[nm]
                                ci, si = (0, 1) if kind == "ropeA" else (2, 3)
                                for k in range(8):
                                    P.op("pe", lambda e: e.matmul(pzr[b][0:rows, :], lhsT=wr[:, k, ro + m * 128: ro + m * 128 + rows], rhs=uT[:, k, :],
                                                                  start=(k == 0), stop=(k == 7)),
                                         reads=["wr", "uT"], writes=[("pzr", b)])
                                P.op("dve", lambda e: e.tensor_tensor(out=t1[0:rows, :], in0=pz[b][0:rows, :], in1=rc[ci][0:rows, :], op=ALU.mult),
                                     reads=[("pz", b), ("rc", ci)], writes=["t1"])
                                P.op("dve", lambda e: e.tensor_tensor(out=t2[0:rows, :], in0=pzr[b][0:rows, :], in1=rc[si][0:rows, :], op=ALU.mult),
                                     reads=[("pzr", b), ("rc", si)], writes=["t2"])
                                P.op("dve", lambda e: e.tensor_tensor(out=st[0:rows, m, :], in0=t1[0:rows, :], in1=t2[0:rows, :], op=ALU.add),
                                     reads=["t1", "t2"], writes=[("stg", nm)])
                            elif kind == "sig":
                                P.op("act", lambda e: e.activation(out=st[:, m, :], in_=pz[b][:], func=AF.Sigmoid),
                                     reads=[("pz", b)], writes=[("stg", nm)])
                            else:
                                P.op("act", lambda e: e.activation(out=st[:, m, :], in_=pz[b][:], func=AF.Copy),
                                     reads=[("pz", b)], writes=[("stg", nm)])
                        if rows == 128:
                            P.dma("pool", dst[:, tsl].rearrange("(m p) t -> p m t", p=128), st[:], reads=[("stg", nm)], writes=[(nm + "T", T)])
                        else:
                            P.dma("pool", dst[:, tsl], st[0:rows, 0, :], reads=[("stg", nm)], writes=[(nm + "T", T)])
                    for s in range(4):
                        for k in range(8):
                            P.op("pe", lambda e: e.matmul(pv[:, 0:256], lhsT=uT[:, k, s * 128:(s + 1) * 128], rhs=wi[:, k, OFF_V:OFF_V + 256],
                                                          start=(k == 0), stop=(k == 7)), reads=["uT", "wi"], writes=["pv"])
                        P.op("act", lambda e: e.activation(out=vst[:, s, :], in_=pv[:, 0:256], func=AF.Copy), reads=["pv"], writes=["vst"])
                        for k in range(8):
                            P.op("pe", lambda e: e.matmul(pv[:, 256:264], lhsT=uT[:, k, s * 128:(s + 1) * 128], rhs=wi[:, k, OFF_W:OFF_W + 8],
                                                          start=(k == 0), stop=(k == 7)), reads=["uT", "wi"], writes=["pv"])
                        P.op("dve", lambda e: e.tensor_copy(out=wst[:, s, :], in_=pv[:, 256:264]), reads=["pv"], writes=["wst"])
                    P.dma("pool", vd[tsl, :].rearrange("(s p) d -> p s d", p=128), vst[:], reads=["vst"], writes=[("vd", T)])
                    P.dma("pool", wid[tsl, :].rearrange("(s p) d -> p s d", p=128), wst[:], reads=["wst"], writes=[("wid", T)])
                P.barrier()

            if "B" in phases:
                phase_ssm(l)

            if "C" in phases:
                phase_dsa(l)

            if "D" in phases:
                phase_d(l, xsrc)
            if "E" in phases:
                phase_e(l, xdst)

        P.barrier(engines=("sp",))
        build.ninst = P.ninst
    return nc


def _rope_tables(L):
    pos = np.arange(L).astype(np.float32)
    out = []
    for half in (64, 32):
        inv = (np.float32(10000.0) ** (-(np.arange(half, dtype=np.float32)) / np.float32(half))).astype(np.float32)
        ang = (pos[None, :] * inv[:, None]).astype(np.float32)
        cos = np.cos(ang).astype(np.float32)
        sin = np.sin(ang).astype(np.float32)
        reps = 128 // half
        c = np.concatenate([cos] * reps, axis=0)
        s = np.concatenate([(-sin if (r % 2 == 0) else sin) for r in range(reps)], axis=0)
        out += [np.ascontiguousarray(c), np.ascontiguousarray(s)]
    return out


def _shared_inputs(inp, L):
    f = lambda a: np.ascontiguousarray(np.asarray(a, dtype=np.float32))
    sh = {}
    for k in ("w_cond", "b_cond", "w_in", "ssm_w_glu", "p_ssm", "p_attn", "w_out", "w_gate_up", "w_down"):
        sh[k] = f(inp[k])
    sh["bglu"] = f(np.asarray(inp["ssm_b_glu"]).reshape(DEPTH, 4, 128).transpose(2, 0, 1).reshape(128, DEPTH * 4))
    sh["dcol"] = f(np.asarray(inp["ssm_d"]).reshape(DEPTH, 4, 128).transpose(2, 0, 1).reshape(128, DEPTH * 4))
    sh["lnp"] = f(np.stack([inp["ln1_g"], inp["ln1_b"], inp["ln2_g"], inp["ln2_b"]], axis=1))
    lam_re, lam_im, ldt = np.asarray(inp["ssm_lam_re"]), np.asarray(inp["ssm_lam_im"]), np.asarray(inp["ssm_log_dt"])
    b_re, b_im = np.asarray(inp["ssm_b_re"]), np.asarray(inp["ssm_b_im"])
    c_re, c_im = np.asarray(inp["ssm_c_re"]), np.asarray(inp["ssm_c_im"])

    def Rl(a):
        a = a.reshape(DEPTH, 4, 8, 1, 64).transpose(0, 2, 3, 1, 4)
        return f(np.broadcast_to(a, (DEPTH, 8, 16, 4, 64)).reshape(DEPTH, 128, 256))

    def Sl(a):
        return f(a.reshape(DEPTH, 16, 2, 64).transpose(0, 2, 3, 1).reshape(DEPTH, 128, 16))

    sh["lamR_re"], sh["lamR_im"] = Rl(lam_re), Rl(lam_im)
    sh["ldtR"] = Rl(np.broadcast_to(ldt[:, :, None], (DEPTH, 32, 64)))
    sh["lamS_re"], sh["lamS_im"] = Sl(lam_re), Sl(lam_im)
    sh["ldtS"] = Sl(np.broadcast_to(ldt[:, :, None], (DEPTH, 32, 64)))
    for nm, a in (("bR_re", b_re), ("bR_im", b_im)):
        sh[nm] = f(a.reshape(DEPTH, 4, 8, 64, 16).transpose(0, 2, 4, 1, 3).reshape(DEPTH, 128, 256))
    for nm, a in (("bS_re", b_re), ("bS_im", b_im)):
        sh[nm] = f(a.reshape(DEPTH, 16, 2, 64, 16).transpose(0, 2, 3, 1, 4).reshape(DEPTH, 128, 256))
    for nm, a in (("cS_re", c_re), ("cS_im", c_im)):
        sh[nm] = f(a.reshape(DEPTH, 16, 2, 16, 64).transpose(0, 2, 4, 1, 3).reshape(DEPTH, 128, 256))
    sh["ident"] = np.eye(128, dtype=np.float32)
    qq = np.arange(128)[:, None]
    ss = np.arange(128)[None, :]
    sh["causal"] = np.where(ss <= qq, 0.0, NEG).astype(np.float32)
    sh["rowmask"] = (np.arange(128)[:, None] // 16 == np.arange(8)[None, :]).astype(np.float32)
    ra_c, ra_s, rb_c, rb_s = _rope_tables(L)
    sh["ropeA_c"], sh["ropeA_s"], sh["ropeB_c"], sh["ropeB_s"] = ra_c, ra_s, rb_c, rb_s
    return sh


def make_in_maps(inp, L, nb):
    sh = _shared_inputs(inp, L)
    x = np.asarray(inp["x"], dtype=np.float32)
    c = np.asarray(inp["c"], dtype=np.float32)
    maps = []
    for b in range(nb):
        m = dict(sh)
        m["x"] = np.ascontiguousarray(x[b])
        m["ccol"] = np.ascontiguousarray(c[b].reshape(8, 128).T)
        maps.append(m)
    return maps


_NC_CACHE = {}


def kernel(**inputs):
    x = np.asarray(inputs["x"])
    B, L, _ = x.shape
    if L not in _NC_CACHE:
        _NC_CACHE[L] = build(L)
    nc = _NC_CACHE[L]
    maps = make_in_maps(inputs, L, B)
    res = run_bass_kernel_spmd(nc, maps, core_ids=list(range(B)))
    return np.stack([np.asarray(r["y"]) for r in res.results], axis=0).astype(np.float32)
```

```python
import math
import numpy as np
from contextlib import ExitStack
import concourse.bass as bass
import concourse.mybir as mybir
from concourse.bass_utils import run_bass_kernel_spmd

F32 = mybir.dt.float32
BF16 = mybir.dt.bfloat16
AF = mybir.ActivationFunctionType
ALU = mybir.AluOpType
AX = mybir.AxisListType

D = 1024
DEPTH = 2
DIN = 4680
DFF = 2816
OFF_SSM, OFF_Q, OFF_K, OFF_V, OFF_QI, OFF_KI, OFF_W, OFF_GS, OFF_GA = 0, 512, 1536, 1792, 2048, 2560, 2624, 2632, 3656
ALPHA = (2 * DEPTH) ** 0.25
LN_EPS = 1e-5
NBIS = 12
NEG = -1.0e30


class Prog:
    def __init__(self, nc, es):
        self.nc = nc
        self.es = es
        self.eng = {"pe": nc.tensor, "dve": nc.vector, "act": nc.scalar, "pool": nc.gpsimd, "sp": nc.sync}
        self.sems = []
        self.cur = {}
        self.waited = {e: {} for e in self.eng}
        self.lastw = {}
        self.readers = {}
        self.dpool = {}
        self.dnext = {}
        self.ninst = 0
        for e in ("pe", "dve", "act", "pool"):
            self.cur[e] = [self._newsem(), 0]
        for q, n in (("sp", 12), ("pool", 8)):
            self.dpool[q] = [[self._newsem(), 0] for _ in range(n)]
            self.dnext[q] = 0

    def _newsem(self):
        s = self.es.enter_context(self.nc.semaphore("s%d" % len(self.sems)))
        self.sems.append(s)
        return len(self.sems) - 1

    def _wait(self, e, dep):
        si, val, src = dep
        if src == e and e == "pe":
            return
        w = self.waited[e]
        if w.get(si, 0) >= val:
            return
        self.eng[e].wait_ge(self.sems[si], val)
        w[si] = val
        self.ninst += 1

    def _deps(self, e, reads, writes):
        for t in reads:
            d = self.lastw.get(t)
            if d:
                self._wait(e, d)
        for t in writes:
            d = self.lastw.get(t)
            if d:
                self._wait(e, d)
            for r in self.readers.get(t, {}).values():
                self._wait(e, r)

    def _book(self, h, reads, writes):
        key = h[2] if h[2] != "dma" else ("dma", h[0])
        for t in reads:
            self.readers.setdefault(t, {})[key] = h
        for t in writes:
            self.lastw[t] = h
            self.readers[t] = {}

    def op(self, e, fn, reads=(), writes=()):
        self._deps(e, reads, writes)
        inst = fn(self.eng[e])
        s = self.cur[e]
        if s[1] >= 30000:
            s = self.cur[e] = [self._newsem(), 0]
        s[1] += 1
        inst.then_inc(self.sems[s[0]], 1)
        self.ninst += 1
        self._book((s[0], s[1], e), reads, writes)

    def dma(self, q, out, in_, reads=(), writes=(), **kw):
        self._deps(q, reads, writes)
        pool = self.dpool[q]
        i = self.dnext[q]
        self.dnext[q] = (i + 1) % len(pool)
        ent = pool[i]
        if ent[1] > 0:
            self._wait(q, (ent[0], ent[1], "dma"))
        if ent[1] > 48000:
            ent[0] = self._newsem()
            ent[1] = 0
        inst = self.eng[q].dma_start(out=out, in_=in_, **kw)
        ent[1] += 16
        inst.then_inc(self.sems[ent[0]], 16)
        self.ninst += 1
        self._book((ent[0], ent[1], "dma"), reads, writes)

    def barrier(self, engines=("pe", "dve", "act", "pool", "sp")):
        hs = []
        for e in ("pe", "dve", "act", "pool"):
            s = self.cur[e]
            if s[1] > 0:
                hs.append((s[0], s[1], e))
        for q in self.dpool:
            for ent in self.dpool[q]:
                if ent[1] > 0:
                    hs.append((ent[0], ent[1], "dma"))
        for e in engines:
            for h in hs:
                if h[2] == e:
                    continue
                self._wait(e, h)


def build(L, dbg=(), nlayers=DEPTH, phases="PABCDE", inj=(), ssm_stop=9):
    NT = L // 128
    NS = L // 512
    NC8 = L // 8
    NCH = max(1, NC8 // 512)
    CW = NC8 // NCH
    NLEV = int(round(math.log2(NC8)))
    nc = bass.Bass("TRN2", target_bir_lowering=False)

    def din(name, shape, dt=F32):
        return nc.dram_tensor(name, list(shape), dt, kind="ExternalInput").ap()

    def dscr(name, shape, dt):
        kind = "ExternalOutput" if name in dbg else ("ExternalInput" if name in inj else "Internal")
        return nc.dram_tensor(name, list(shape), dt, kind=kind).ap()

    x_in = din("x", [L, D])
    ccol = din("ccol", [128, 8])
    w_cond = din("w_cond", [DEPTH, D, 6 * D])
    b_cond = din("b_cond", [DEPTH, 6 * D])
    w_in = din("w_in", [DEPTH, D, DIN])
    w_glu = din("ssm_w_glu", [DEPTH, 512, 512])
    bglu = din("bglu", [128, DEPTH * 4])
    p_ssm = din("p_ssm", [DEPTH, 512, D])
    p_attn = din("p_attn", [DEPTH, D, D])
    w_out = din("w_out", [DEPTH, D, D])
    lnp = din("lnp", [DEPTH, 4, D])
    w_gu = din("w_gate_up", [DEPTH, D, 2 * DFF])
    w_dn = din("w_down", [DEPTH, DFF, D])
    lamR_re = din("lamR_re", [DEPTH, 128, 256])
    lamR_im = din("lamR_im", [DEPTH, 128, 256])
    ldtR = din("ldtR", [DEPTH, 128, 256])
    bR_re = din("bR_re", [DEPTH, 128, 256])
    bR_im = din("bR_im", [DEPTH, 128, 256])
    dcol = din("dcol", [128, DEPTH * 4])
    lamS_re = din("lamS_re", [DEPTH, 128, 16])
    lamS_im = din("lamS_im", [DEPTH, 128, 16])
    ldtS = din("ldtS", [DEPTH, 128, 16])
    cS_re = din("cS_re", [DEPTH, 128, 256])
    cS_im = din("cS_im", [DEPTH, 128, 256])
    bS_re = din("bS_re", [DEPTH, 128, 256])
    bS_im = din("bS_im", [DEPTH, 128, 256])
    ident_d = din("ident", [128, 128])
    causal_d = din("causal", [128, 128])
    rowmask_d = din("rowmask", [128, 8])
    ropeA_c = din("ropeA_c", [128, L])
    ropeA_s = din("ropeA_s", [128, L])
    ropeB_c = din("ropeB_c", [128, L])
    ropeB_s = din("ropeB_s", [128, L])
    y_out = nc.dram_tensor("y", [L, D], F32, kind="ExternalOutput").ap()

    xres = dscr("xres", [L, D], F32)
    x1d = dscr("x1d", [L, D], F32)
    uTs = dscr("uTs", [512, L], BF16)
    qT = dscr("qT", [1024, L], BF16)
    kT = dscr("kT", [256, L], BF16)
    qiT = dscr("qiT", [512, L], BF16)
    kiT = dscr("kiT", [64, L], BF16)
    gsT = dscr("gsT", [1024, L], BF16)
    gaT = dscr("gaT", [1024, L], BF16)
    vd = dscr("vd", [L, 256], BF16)
    wid = dscr("wid", [L, 8], F32)
    ysT = dscr("ysT", [512, L], BF16)
    yaT = dscr("yaT", [1024, L], BF16)
    fpart = dscr("fpart", [L, D], F32)
    dbg_mod = dscr("dbg_mod", [128, 6 * D], F32)
    dbg_sc = dscr("dbg_sc", [128, L], F32)
    dbg_thr = dscr("dbg_thr", [128, 16], F32)

    with ExitStack() as es:
        P = Prog(nc, es)

        uid = [0]

        def SB(st, name, shape, dt):
            uid[0] += 1
            return st.enter_context(nc.sbuf_tensor("sb%d_%s" % (uid[0], name), list(shape), dt))

        def PSU(st, name, shape, dt=F32):
            uid[0] += 1
            return st.enter_context(nc.psum_tensor("ps%d_%s" % (uid[0], name), list(shape), dt))

        ident = SB(es, "ident", [128, 128], F32)
        identb = SB(es, "identb", [128, 128], BF16)
        ident4b = SB(es, "ident4b", [128, 4, 128], BF16)
        causal = SB(es, "causal", [128, 128], F32)
        rowmask = SB(es, "rowmask", [128, 8], F32)
        epsc = SB(es, "epsc", [128, 1], F32)
        modT = SB(es, "modT", [128, 48], F32)
        G12 = SB(es, "G12", [128, 2, D], F32)
        P.dma("sp", ident[:], ident_d, writes=["ident"])
        P.dma("sp", causal[:], causal_d, writes=["causal"])
        P.dma("sp", rowmask[:], rowmask_d, writes=["rowmask"])
        P.op("dve", lambda e: e.tensor_copy(out=identb[:], in_=ident[:]), reads=["ident"], writes=["identb"])
        P.op("dve", lambda e: e.memset(epsc[:], LN_EPS), writes=["epsc"])
        for h in range(4):
            P.op("dve", lambda e: e.tensor_copy(out=ident4b[:, h, :], in_=ident[:]), reads=["ident"], writes=["ident4b"])


        def layer_norm_tile(vt, dst, lnb, stats, mv, rstd, xtok="xs"):
            for hf in range(2):
                P.op("dve", lambda e: e.bn_stats(out=stats[:, hf, :], in_=vt[:, hf * 512:(hf + 1) * 512]), reads=["vt"], writes=["stats"])
            P.op("dve", lambda e: e.bn_aggr(out=mv[:], in_=stats[:].rearrange("p a b -> p (a b)")), reads=["stats"], writes=["mv"])
            P.op("act", lambda e: e.activation(out=rstd[:], in_=mv[:, 1:2], func=AF.Sqrt, bias=epsc[:, 0:1], scale=1.0), reads=["mv"], writes=["rstd"])
            P.op("dve", lambda e: e.reciprocal(out=rstd[:], in_=rstd[:]), reads=["rstd"], writes=["rstd"])
            P.op("dve", lambda e: e.tensor_scalar(out=dst, in0=vt[:], scalar1=mv[:, 0:1], scalar2=rstd[:, 0:1], op0=ALU.subtract, op1=ALU.mult),
                 reads=["vt", "mv", "rstd"], writes=[xtok])
            P.op("dve", lambda e: e.tensor_tensor(out=dst, in0=dst, in1=lnb[:, 0, :], op=ALU.mult), reads=[xtok, "lnb"], writes=[xtok])
            P.op("dve", lambda e: e.tensor_tensor(out=dst, in0=dst, in1=lnb[:, 1, :], op=ALU.add), reads=[xtok, "lnb"], writes=[xtok])

        def phase_d(l, xsrc):
            with ExitStack() as ph:
                wglu_sb = SB(ph, "wglu", [128, 4, 512], BF16)
                pssm_sb = SB(ph, "pssm", [128, 4, D], BF16)
                pattn_sb = SB(ph, "pattn", [128, 8, D], BF16)
                wout_sb = SB(ph, "wout", [128, 8, D], BF16)
                wstg = [SB(ph, "wstg%d" % i, [128, D], F32) for i in range(2)]
                bglu_sb = SB(ph, "bglu", [128, DEPTH * 4], F32)
                ys_b = [SB(ph, "ys%d" % i_, [128, 4, 512], BF16) for i_ in range(2)]
                ya_b = [SB(ph, "ya%d" % i_, [128, 8, 512], BF16) for i_ in range(2)]
                gs_b = [SB(ph, "gs%d" % i_, [128, 8, 512], BF16) for i_ in range(2)]
                ga_b = [SB(ph, "ga%d" % i_, [128, 8, 512], BF16) for i_ in range(2)]
                sg = SB(ph, "sg", [128, 512], F32)
                yssm = SB(ph, "yssm", [128, 4, 512], BF16)
                tA = SB(ph, "tA", [128, 512], F32)
                tB = SB(ph, "tB", [128, 512], F32)
                merged = SB(ph, "merged", [128, 8, 512], BF16)
                xsb = [SB(ph, "xs%d" % i_, [128, 4, D], F32) for i_ in range(2)]
                vt = SB(ph, "vt", [128, D], F32)
                stats = SB(ph, "stats", [128, 2, 6], F32)
                mv = SB(ph, "mv", [128, 2], F32)
                rstd = SB(ph, "rstd", [128, 1], F32)
                pg = [PSU(ph, "pg%d" % i, [128, 512]) for i in range(2)]
                pA = [PSU(ph, "pA%d" % i, [128, 512]) for i in range(2)]
                pB = [PSU(ph, "pB%d" % i, [128, 512]) for i in range(2)]
                phh = PSU(ph, "phh", [128, D])
                P.dma("sp", bglu_sb[:], bglu, writes=["bglu"])
                lnb = SB(ph, "lnb", [128, 2, D], F32)
                for i_ in range(2):
                    P.dma("sp", lnb[:, i_, :], lnp[l, i_, :].partition_broadcast(128), writes=["lnb"])
                P.dma("pool", wglu_sb[:], w_glu[l].rearrange("(k p) n -> p k n", p=128), writes=["wglu"])
                P.dma("pool", pssm_sb[:], p_ssm[l].rearrange("(k p) n -> p k n", p=128), writes=["pssm"])
                for k in range(8):
                    P.dma("pool", pattn_sb[:, k, :], p_attn[l, k * 128:(k + 1) * 128, :], writes=["pattn"])
                for k in range(8):
                    b = k % 2
                    P.dma("sp", wstg[b][:], w_out[l, k * 128:(k + 1) * 128, :], writes=[("wstg", b)])
                    P.op("dve", lambda e: e.tensor_tensor(out=wout_sb[:, k, :], in0=wstg[b][:], in1=G12[:, 0, :], op=ALU.mult),
                         reads=[("wstg", b), "G12"], writes=["wout"])
                cnt = 0
                for T in range(NS):
                    tsl = slice(T * 512, (T + 1) * 512)
                    pb = T % 2
                    ys_sb, ya_sb, gs_sb, ga_sb, xs = ys_b[pb], ya_b[pb], gs_b[pb], ga_b[pb], xsb[pb]
                    YS, YA, GS, GA, XT = ("ys", pb), ("ya", pb), ("gs", pb), ("ga", pb), ("xs", pb)
                    P.dma("sp", ys_sb[:], ysT[:, tsl].rearrange("(k p) t -> p k t", p=128), reads=[("ysT", T)], writes=[YS])
                    P.dma("sp", ya_sb[:], yaT[:, tsl].rearrange("(k p) t -> p k t", p=128), reads=[("yaT", T)], writes=[YA])
                    P.dma("sp", gs_sb[:], gsT[:, tsl].rearrange("(k p) t -> p k t", p=128), reads=[("gsT", T)], writes=[GS])
                    P.dma("sp", ga_sb[:], gaT[:, tsl].rearrange("(k p) t -> p k t", p=128), reads=[("gaT", T)], writes=[GA])
                    P.dma("sp", xs[:], xsrc[tsl, :].rearrange("(s p) d -> p s d", p=128), writes=[XT])
                    for m in range(4):
                        b = m % 2
                        for k in range(4):
                            P.op("pe", lambda e: e.matmul(pg[b][:], lhsT=wglu_sb[:, k, m * 128:(m + 1) * 128], rhs=ys_sb[:, k, :], start=(k == 0), stop=(k == 3)),
                                 reads=["wglu", YS], writes=[("pg", b)])
                        P.op("act", lambda e: e.activation(out=sg[:], in_=pg[b][:], func=AF.Sigmoid, bias=bglu_sb[:, l * 4 + m:l * 4 + m + 1], scale=1.0),
                             reads=[("pg", b), "bglu"], writes=["sg"])
                        P.op("dve", lambda e: e.tensor_tensor(out=yssm[:, m, :], in0=ys_sb[:, m, :], in1=sg[:], op=ALU.mult),
                             reads=[YS, "sg"], writes=["yssm"])
                    for n in range(8):
                        b = n % 2
                        for k in range(4):
                            P.op("pe", lambda e: e.matmul(pA[b][:], lhsT=pssm_sb[:, k, n * 128:(n + 1) * 128], rhs=yssm[:, k, :], start=(k == 0), stop=(k == 3)),
                                 reads=["pssm", "yssm"], writes=[("pA", b)])
                        for k in range(8):
                            P.op("pe", lambda e: e.matmul(pB[b][:], lhsT=pattn_sb[:, k, n * 128:(n + 1) * 128], rhs=ya_sb[:, k, :], start=(k == 0), stop=(k == 7)),
                                 reads=["pattn", YA], writes=[("pB", b)])
                        P.op("dve", lambda e: e.tensor_tensor(out=tA[:], in0=pA[b][:], in1=gs_sb[:, n, :], op=ALU.mult), reads=[("pA", b), GS], writes=["tA"])
                        P.op("dve", lambda e: e.tensor_tensor(out=tB[:], in0=pB[b][:], in1=ga_sb[:, n, :], op=ALU.mult), reads=[("pB", b), GA], writes=["tB"])
                        P.op("pool", lambda e: e.tensor_tensor(out=merged[:, n, :], in0=tA[:], in1=tB[:], op=ALU.add), reads=["tA", "tB"], writes=["merged"])
                    for s in range(4):
                        for hf in range(2):
                            for k in range(8):
                                P.op("pe", lambda e: e.matmul(phh[:, hf * 512:(hf + 1) * 512], lhsT=merged[:, k, s * 128:(s + 1) * 128],
                                                              rhs=wout_sb[:, k, hf * 512:(hf + 1) * 512], start=(k == 0), stop=(k == 7)),
                                     reads=["merged", "wout"], writes=["phh"])
                        for hf in range(2):
                            P.op("dve", lambda e: e.scalar_tensor_tensor(out=vt[:, hf * 512:(hf + 1) * 512], in0=xs[:, s, hf * 512:(hf + 1) * 512], scalar=ALPHA,
                                                                         in1=phh[:, hf * 512:(hf + 1) * 512], op0=ALU.mult, op1=ALU.add),
                                 reads=[XT, "phh"], writes=["vt"])
                        layer_norm_tile(vt, xs[:, s, :], lnb, stats, mv, rstd, XT)
                    P.dma("pool", x1d[tsl, :].rearrange("(s p) d -> p s d", p=128), xs[:], reads=[XT], writes=[("x1d", T)])
                P.barrier()

        def phase_e(l, xdst):
            NT2 = L // 512
            HM = 11
            for hp in range(2):
              with ExitStack() as ph:
                wgu_sb = SB(ph, "wgu", [128, 8, 2 * HM * 128], BF16)
                wdn_sb = SB(ph, "wdn", [128, HM, D], BF16)
                wstg = [SB(ph, "wstg%d" % i, [128, D], F32) for i in range(2)]
                xsb = [SB(ph, "xs%d" % i_, [128, 4, D], F32) for i_ in range(2)]
                fpb = [SB(ph, "fp%d" % i_, [128, 4, D], F32) for i_ in range(2)]
                u2T = SB(ph, "u2T", [128, 8, 512], BF16)
                sa = SB(ph, "sa", [128, 512], F32)
                hT = SB(ph, "hT", [128, HM, 512], BF16)
                vt = SB(ph, "vt", [128, D], F32)
                stats = SB(ph, "stats", [128, 2, 6], F32)
                mv = SB(ph, "mv", [128, 2], F32)
                rstd = SB(ph, "rstd", [128, 1], F32)
                ptp = [PSU(ph, "ptp%d" % i, [128, 512]) for i in range(2)]
                pa = [PSU(ph, "pa%d" % i, [128, 512]) for i in range(2)]
                pb = [PSU(ph, "pb%d" % i, [128, 512]) for i in range(2)]
                pf = PSU(ph, "pf", [128, D])
                W = HM * 128
                lnb = SB(ph, "lnb", [128, 2, D], F32)
                for i_ in range(2):
                    P.dma("sp", lnb[:, i_, :], lnp[l, 2 + i_, :].partition_broadcast(128), writes=["lnb"])
                for k in range(8):
                    P.dma("pool", wgu_sb[:, k, 0:W], w_gu[l, k * 128:(k + 1) * 128, hp * W:(hp + 1) * W], writes=["wgu"])
                    P.dma("pool", wgu_sb[:, k, W:2 * W], w_gu[l, k * 128:(k + 1) * 128, DFF + hp * W:DFF + (hp + 1) * W], writes=["wgu"])
                for k in range(HM):
                    b = k % 2
                    r0 = (hp * HM + k) * 128
                    P.dma("sp", wstg[b][:], w_dn[l, r0:r0 + 128, :], writes=[("wstg", b)])
                    P.op("dve", lambda e: e.tensor_tensor(out=wdn_sb[:, k, :], in0=wstg[b][:], in1=G12[:, 1, :], op=ALU.mult),
                         reads=[("wstg", b), "G12"], writes=["wdn"])
                for T in range(NT2):
                    tsl = slice(T * 512, (T + 1) * 512)
                    xs, fp_sb = xsb[T % 2], fpb[T % 2]
                    XT, FT = ("xs", T % 2), ("fp", T % 2)
                    P.dma("sp", xs[:], x1d[tsl, :].rearrange("(s p) d -> p s d", p=128), reads=[("x1d", T)], writes=[XT])
                    if hp == 1:
                        P.dma("sp", fp_sb[:], fpart[tsl, :].rearrange("(s p) d -> p s d", p=128), reads=[("fpart", T)], writes=[FT])
                    for k in range(8):
                        b = k % 2
                        for s in range(4):
                            P.op("pe", lambda e: e.transpose(ptp[b][:, s * 128:(s + 1) * 128], xs[:, s, k * 128:(k + 1) * 128], ident[:]),
                                 reads=[XT, "ident"], writes=[("ptp", b)])
                        P.op("act", lambda e: e.activation(out=u2T[:, k, :], in_=ptp[b][:], func=AF.Identity,
                                                           scale=modT[:, 32 + k:33 + k], bias=modT[:, 24 + k:25 + k]),
                             reads=[("ptp", b), "modT"], writes=["u2T"])
                    for m in range(HM):
                        b = m % 2
                        for k in range(8):
                            P.op("pe", lambda e: e.matmul(pa[b][:], lhsT=wgu_sb[:, k, m * 128:(m + 1) * 128], rhs=u2T[:, k, :], start=(k == 0), stop=(k == 7)),
                                 reads=["wgu", "u2T"], writes=[("pa", b)])
                        for k in range(8):
                            P.op("pe", lambda e: e.matmul(pb[b][:], lhsT=wgu_sb[:, k, W + m * 128:W + (m + 1) * 128], rhs=u2T[:, k, :], start=(k == 0), stop=(k == 7)),
                                 reads=["wgu", "u2T"], writes=[("pb", b)])
                        P.op("act", lambda e: e.activation(out=sa[:], in_=pa[b][:], func=AF.Silu), reads=[("pa", b)], writes=["sa"])
                        P.op("dve", lambda e: e.tensor_tensor(out=hT[:, m, :], in0=sa[:], in1=pb[b][:], op=ALU.mult), reads=["sa", ("pb", b)], writes=["hT"])
                    for s in range(4):
                        for hf in range(2):
                            for k in range(HM):
                                P.op("pe", lambda e: e.matmul(pf[:, hf * 512:(hf + 1) * 512], lhsT=hT[:, k, s * 128:(s + 1) * 128],
                                                              rhs=wdn_sb[:, k, hf * 512:(hf + 1) * 512], start=(k == 0), stop=(k == HM - 1)),
                                     reads=["hT", "wdn"], writes=["pf"])
                        if hp == 0:
                            for hf in range(2):
                                P.op("dve", lambda e: e.tensor_copy(out=fp_sb[:, s, hf * 512:(hf + 1) * 512], in_=pf[:, hf * 512:(hf + 1) * 512]),
                                     reads=["pf"], writes=[FT])
                        else:
                            for hf in range(2):
                                P.op("dve", lambda e: e.tensor_tensor(out=vt[:, hf * 512:(hf + 1) * 512], in0=fp_sb[:, s, hf * 512:(hf + 1) * 512],
                                                                      in1=pf[:, hf * 512:(hf + 1) * 512], op=ALU.add), reads=[FT, "pf"], writes=["vt"])
                            P.op("dve", lambda e: e.scalar_tensor_tensor(out=vt[:], in0=xs[:, s, :], scalar=ALPHA, in1=vt[:], op0=ALU.mult, op1=ALU.add),
                                 reads=[XT, "vt"], writes=["vt"])
                            layer_norm_tile(vt, xs[:, s, :], lnb, stats, mv, rstd, XT)
                    if hp == 0:
                        P.dma("pool", fpart[tsl, :].rearrange("(s p) d -> p s d", p=128), fp_sb[:], reads=[FT], writes=[("fpart", T)])
                    else:
                        P.dma("pool", xdst[tsl, :].rearrange("(s p) d -> p s d", p=128), xs[:], reads=[XT], writes=[("xdst", T)])
                P.barrier()


        def phase_dsa(l):
            SC = 1.0 / math.sqrt(128.0)
            with ExitStack() as ph:
                kT_sb = SB(ph, "kTsb", [128, 2, L], BF16)
                v_sb = SB(ph, "vsb", [128, NT, 2, 132], BF16)
                ki2 = SB(ph, "ki2", [128, L], BF16)
                score_b = [SB(ph, "score%d" % i, [128, L], BF16) for i in range(2)]
                mbs = [SB(ph, "mb%d" % i, [128, L], BF16) for i in range(2)]
                junk = SB(ph, "junk", [128, L], mybir.dt.uint8)
                R = SB(ph, "R", [128, 8, 512], BF16)
                qT_b = [SB(ph, "qTi%d" % i, [128, 8, 128], BF16) for i in range(3)]
                qiT_b = [SB(ph, "qiTi%d" % i, [128, 8, 128], BF16) for i in range(2)]
                w_b = [SB(ph, "wi_%d" % i, [128, 8], F32) for i in range(2)]
                diag_b = [SB(ph, "diag%d" % i, [128, 8, 128], BF16) for i in range(2)]
                PT = [SB(ph, "PT%d" % i, [128, 512], BF16) for i in range(3)]
                o_b = [SB(ph, "osb%d" % i, [128, D], BF16) for i in range(2)]
                oT_b = [SB(ph, "oTsb%d" % i, [128, 8, 128], BF16) for i in range(2)]
                sm = SB(ph, "sm", [128, 16], F32)
                nrm = SB(ph, "nrm", [128, 8], F32)
                tneg = SB(ph, "tneg", [128, 1], F32)
                psx = [PSU(ph, "psx%d" % i, [128, 512]) for i in range(2)]
                pss = PSU(ph, "pss", [128, 512])
                psST = [PSU(ph, "psST%d" % i, [128, 512]) for i in range(2)]
                psO = [PSU(ph, "psO%d" % i, [128, 512]) for i in range(2)]
                psT = PSU(ph, "psT", [128, 8, 128], BF16)
                P.dma("sp", kT_sb[:], kT.rearrange("(g p) t -> p g t", p=128), writes=["kTsb"])
                for g in range(2):
                    P.dma("sp", v_sb[:, :, g, 0:128], vd[:, g * 128:(g + 1) * 128].rearrange("(j p) d -> p j d", p=128), writes=["vsb"])
                P.op("pool", lambda e: e.memset(v_sb[:, :, :, 128:129], 1.0), writes=["vsb"])
                P.dma("sp", ki2[0:64, :], kiT, writes=["ki2"])
                P.dma("sp", ki2[64:128, :], kiT, writes=["ki2"])
                P.op("pool", lambda e: e.memset(tneg[:], -1.0e29), writes=["tneg"])
                for pb_ in range(2):
                    P.op("pool", lambda e: e.memset(qiT_b[pb_][:], 0.0), writes=[("qiTi", pb_)])
                cnts = {"pt": 0, "st": 0}

                def emit_load(i):
                    pb = i % 2
                    qs = slice(i * 128, (i + 1) * 128)
                    qiT_i, w_i, diag = qiT_b[pb], w_b[pb], diag_b[pb]
                    qsrc = qiT[:, qs].rearrange("(m two d) t -> two d m t", two=2, d=64)
                    qdst = qiT_i[:].rearrange("p (m two) t -> p m two t", two=2)
                    P.dma("sp", qdst[0:64, :, 0, :], qsrc[0], writes=[("qiTi", pb)])
                    P.dma("sp", qdst[64:128, :, 1, :], qsrc[1], writes=[("qiTi", pb)])
                    P.dma("sp", w_i[:], wid[qs, :], writes=[("wi_", pb)])
                    P.dma("sp", qT_b[i % 3][:], qT[:, qs].rearrange("(h p) t -> p h t", p=128), writes=[("qTi", i % 3)])
                    for h in range(8):
                        P.op("pool", lambda e: e.tensor_scalar(out=diag[:, h, :], in0=identb[:], scalar1=w_i[:, h:h + 1], scalar2=None, op0=ALU.mult),
                             reads=["identb", ("wi_", pb)], writes=[("diag", pb)])

                def emit_idx(i):
                    pb = i % 2
                    n = 128 * (i + 1)
                    nch = (n + 511) // 512
                    qiT_i, diag = qiT_b[pb], diag_b[pb]
                    score, SCT = score_b[pb], ("score", pb)
                    for c in range(nch):
                        wc = min(512, n - c * 512)
                        ks = slice(c * 512, c * 512 + wc)

                        def xmm(h):
                            P.op("pe", lambda e: e.matmul(psx[h % 2][:, 0:wc], lhsT=qiT_i[:, h, :], rhs=ki2[:, ks],
                                                          start=True, stop=True), reads=[("qiTi", pb), "ki2"], writes=[("psx", h % 2)])
                        xmm(0)
                        xmm(1)
                        for h in range(8):
                            P.op("act", lambda e: e.activation(out=R[:, h, 0:wc], in_=psx[h % 2][:, 0:wc], func=AF.Relu),
                                 reads=[("psx", h % 2)], writes=[("R", h)])
                            P.op("pe", lambda e: e.matmul(pss[:, 0:wc], lhsT=diag[:, h, :], rhs=R[:, h, 0:wc], start=(h == 0), stop=(h == 7)),
                                 reads=[("diag", pb), ("R", h)], writes=["pss"])
                            if h + 2 < 8:
                                xmm(h + 2)
                        last = (c == nch - 1)
                        wcopy = wc - 128 if last else wc
                        if wcopy > 0:
                            P.op("act", lambda e: e.activation(out=score[:, c * 512:c * 512 + wcopy], in_=pss[:, 0:wcopy], func=AF.Copy),
                                 reads=["pss"], writes=[SCT])
                        if last:
                            P.op("dve", lambda e: e.tensor_tensor(out=score[:, n - 128:n], in0=pss[:, wc - 128:wc], in1=causal[:], op=ALU.add),
                                 reads=["pss", "causal"], writes=[SCT])

                def emit_thr(i):
                    pb = i % 2
                    n = 128 * (i + 1)
                    mb = mbs[pb]
                    score, SCT = score_b[pb], ("score", pb)
                    if i >= 2:
                        P.op("dve", lambda e: e.tensor_reduce(out=sm[:, 0:1], in_=score[:, 0:n], axis=AX.X, op=ALU.max), reads=[SCT], writes=["sm"])
                        P.op("dve", lambda e: e.tensor_reduce(out=sm[:, 1:2], in_=score[:, 0:n - 128], axis=AX.X, op=ALU.min), reads=[SCT], writes=["sm"])
                        P.op("dve", lambda e: e.tensor_scalar(out=sm[:, 2:3], in0=sm[:, 1:2], scalar1=-1.0, scalar2=None, op0=ALU.add), reads=["sm"], writes=["sm"])
                        P.op("dve", lambda e: e.tensor_tensor(out=sm[:, 3:4], in0=sm[:, 0:1], in1=sm[:, 2:3], op=ALU.subtract), reads=["sm"], writes=["sm"])
                        for k in range(NBIS):
                            ck = 2.0 ** (-(k + 1))
                            P.op("dve", lambda e: e.scalar_tensor_tensor(out=sm[:, 4:5], in0=sm[:, 3:4], scalar=ck, in1=sm[:, 2:3], op0=ALU.mult, op1=ALU.add),
                                 reads=["sm"], writes=["sm"])
                            P.op("dve", lambda e: e.tensor_scalar(out=junk[:, 0:n], in0=score[:, 0:n], scalar1=sm[:, 4:5], scalar2=0.0, op0=ALU.is_gt, op1=ALU.add,
                                                                  accum_out=sm[:, 5:6]), reads=[SCT, "sm"], writes=["sm", "junk"])
                            P.op("dve", lambda e: e.tensor_scalar(out=sm[:, 6:7], in0=sm[:, 5:6], scalar1=255.5, scalar2=sm[:, 3:4], op0=ALU.is_ge, op1=ALU.mult),
                                 reads=["sm"], writes=["sm"])
                            P.op("dve", lambda e: e.scalar_tensor_tensor(out=sm[:, 2:3], in0=sm[:, 6:7], scalar=ck, in1=sm[:, 2:3], op0=ALU.mult, op1=ALU.add),
                                 reads=["sm"], writes=["sm"])
                        thr = sm[:, 2:3]
                    else:
                        thr = tneg[:, 0:1]
                    P.op("dve", lambda e: e.tensor_scalar(out=mb[:, 0:n], in0=score[:, 0:n], scalar1=thr, scalar2=-30000.0, op0=ALU.is_le, op1=ALU.mult),
                         reads=[SCT, "sm", "tneg"], writes=[("mb", pb)])

                def emit_att(i):
                    pb = i % 2
                    qs = slice(i * 128, (i + 1) * 128)
                    qT_i, mb, o_sb, oT_sb = qT_b[i % 3], mbs[pb], o_b[pb], oT_b[pb]
                    ring = [(psST[0], ("psST", 0)), (psST[1], ("psST", 1)), (psx[0], ("psx", 0))]
                    for g in range(2):
                        def qk(j):
                            pst_, tk_ = ring[j % 3]
                            P.op("pe", lambda e: e.matmul(pst_[:], lhsT=kT_sb[:, g, j * 128:(j + 1) * 128], rhs=qT_i[:, 4 * g:4 * g + 4, :],
                                                          start=True, stop=False), reads=["kTsb", ("qTi", i % 3)], writes=[tk_])
                            P.op("pe", lambda e: e.matmul(pst_[:], lhsT=mb[:, j * 128:(j + 1) * 128], rhs=ident4b[:],
                                                          start=False, stop=True), reads=[("mb", pb), "ident4b"], writes=[tk_])
                        qk(0)
                        if i >= 1:
                            qk(1)
                        for j in range(i + 1):
                            pst_, tk_ = ring[j % 3]
                            r3 = cnts["pt"] % 3
                            cnts["pt"] += 1
                            P.op("act", lambda e: e.activation(out=PT[r3][:], in_=pst_[:], func=AF.Exp, scale=SC),
                                 reads=[tk_], writes=[("PT", r3)])
                            if j + 2 <= i:
                                qk(j + 2)
                            for hh in range(4):
                                off = (hh % 2) * 256
                                P.op("pe", lambda e: e.matmul(psO[hh // 2][:, off:off + 129], lhsT=PT[r3][:, hh * 128:(hh + 1) * 128], rhs=v_sb[:, j, g, 0:129],
                                                              start=(j == 0 and hh % 2 == 0), stop=(j == i and hh % 2 == 1), skip_group_check=True),
                                     reads=[("PT", r3), "vsb"], writes=[("psO", hh // 2)])
                        for hh in range(4):
                            off = (hh % 2) * 256
                            P.op("act", lambda e: e.activation(out=nrm[:, hh:hh + 1], in_=psO[hh // 2][:, off + 128:off + 129], func=AF.Ln),
                                 reads=[("psO", hh // 2)], writes=["nrm"])
                            P.op("act", lambda e: e.activation(out=nrm[:, 4 + hh:5 + hh], in_=nrm[:, hh:hh + 1], func=AF.Exp, scale=-1.0),
                                 reads=["nrm"], writes=["nrm"])
                            P.op("act", lambda e: e.activation(out=o_sb[:, (4 * g + hh) * 128:(4 * g + hh + 1) * 128], in_=psO[hh // 2][:, off:off + 128],
                                                               func=AF.Identity, scale=nrm[:, 4 + hh:5 + hh]),
                                 reads=[("psO", hh // 2), "nrm"], writes=[("osb", pb)])
                    for h in range(8):
                        P.op("pe", lambda e: e.transpose(psT[:, h, :], o_sb[:, h * 128:(h + 1) * 128], identb[:]), reads=[("osb", pb), "identb"], writes=["psT"])
                    P.op("act", lambda e: e.activation(out=oT_sb[:], in_=psT[:], func=AF.Copy), reads=["psT"], writes=[("oTsb", pb)])
                    P.dma("pool", yaT[:, qs].rearrange("(h p) t -> p h t", p=128), oT_sb[:], reads=[("oTsb", pb)], writes=[("yaT", i)])

                emit_load(0)
                for i in range(NT + 1):
                    if i + 1 < NT:
                        emit_load(i + 1)
                    if i < NT:
                        emit_idx(i)
                        emit_thr(i)
                    if i >= 1:
                        emit_att(i - 1)
                P.barrier()

        def phase_ssm(l):
            TK = "ssmprep"

            def TT(o, a, b, op, eng="dve"):
                P.op(eng, lambda e: e.tensor_tensor(out=o, in0=a, in1=b, op=op), reads=[TK], writes=[TK])

            def TS(o, a, s1, op0, s2=None, op1=None):
                if op1 is None:
                    P.op("dve", lambda e: e.tensor_scalar(out=o, in0=a, scalar1=s1, scalar2=None, op0=op0), reads=[TK], writes=[TK])
                else:
                    P.op("dve", lambda e: e.tensor_scalar(out=o, in0=a, scalar1=s1, scalar2=s2, op0=op0, op1=op1), reads=[TK], writes=[TK])

            def ACT(o, a, func, scale=1.0):
                P.op("act", lambda e: e.activation(out=o, in_=a, func=func, scale=scale), reads=[TK], writes=[TK])

            def zoh(st, tmp, nm, lam_re_d, lam_im_d, ldt_d, F):
                pw_re = SB(st, nm + "pwre", [128, 9, F], F32)
                pw_im = SB(st, nm + "pwim", [128, 9, F], F32)
                fr = SB(st, nm + "fr", [128, F], F32)
                fi = SB(st, nm + "fi", [128, F], F32)
                t = [SB(tmp, nm + "t%d" % i, [128, F], F32) for i in range(10)]
                lr, li, dt, mag, er, ei, a, b, c_, d_ = t
                P.dma("sp", lr[:], lam_re_d, writes=[TK])
                P.dma("sp", li[:], lam_im_d, writes=[TK])
                P.dma("sp", dt[:], ldt_d, writes=[TK])
                ACT(dt[:], dt[:], AF.Exp)
                TT(a[:], lr[:], dt[:], ALU.mult)
                ACT(mag[:], a[:], AF.Exp)
                TT(a[:], li[:], dt[:], ALU.mult)
                ACT(ei[:], a[:], AF.Sin, scale=1.0 / 16.0)
                ACT(b[:], a[:], AF.Sin, scale=1.0 / 32.0)
                TT(b[:], b[:], b[:], ALU.mult)
                TS(er[:], b[:], -2.0, ALU.mult, 1.0, ALU.add)
                for _ in range(4):
                    TT(a[:], er[:], er[:], ALU.mult)
                    TT(b[:], ei[:], ei[:], ALU.mult)
                    TT(c_[:], er[:], ei[:], ALU.mult)
                    TT(er[:], a[:], b[:], ALU.subtract)
                    TS(ei[:], c_[:], 2.0, ALU.mult)
                ar, ai = pw_re[:, 1, :], pw_im[:, 1, :]
                TT(ar, mag[:], er[:], ALU.mult)
                TT(ai, mag[:], ei[:], ALU.mult)
                P.op("dve", lambda e: e.memset(pw_re[:, 0, :], 1.0), reads=[TK], writes=[TK])
                P.op("dve", lambda e: e.memset(pw_im[:, 0, :], 0.0), reads=[TK], writes=[TK])
                TT(a[:], lr[:], lr[:], ALU.mult)
                TT(b[:], li[:], li[:], ALU.mult)
                TT(a[:], a[:], b[:], ALU.add)
                P.op("dve", lambda e: e.reciprocal(out=a[:], in_=a[:]), reads=[TK], writes=[TK])
                TS(b[:], ar, -1.0, ALU.add)
                TT(c_[:], b[:], lr[:], ALU.mult)
                TT(d_[:], ai, li[:], ALU.mult)
                TT(c_[:], c_[:], d_[:], ALU.add)
                TT(fr[:], c_[:], a[:], ALU.mult)
                TT(c_[:], ai, lr[:], ALU.mult)
                TT(d_[:], b[:], li[:], ALU.mult)
                TT(c_[:], c_[:], d_[:], ALU.subtract)
                TT(fi[:], c_[:], a[:], ALU.mult)
                for k in range(2, 9):
                    cmul(pw_re[:, k, :], pw_im[:, k, :], pw_re[:, k - 1, :], pw_im[:, k - 1, :], ar, ai, a[:], b[:])
                return pw_re, pw_im, fr, fi

            def cmul(o_re, o_im, x_re, x_im, y_re, y_im, t1, t2):
                TT(t1, x_re, y_re, ALU.mult)
                TT(t2, x_im, y_im, ALU.mult)
                TT(o_re, t1, t2, ALU.subtract)
                TT(t1, x_re, y_im, ALU.mult)
                TT(t2, x_im, y_re, ALU.mult)
                TT(o_im, t1, t2, ALU.add)

            with ExitStack() as ph:
                B8 = SB(ph, "B8", [128, 4, 8, 2, 64], F32)
                Wr = SB(ph, "Wr", [128, 9, 16, 16], F32)
                Wi = SB(ph, "Wi", [128, 9, 16, 16], F32)
                BbS_re = SB(ph, "BbSre", [128, 16, 16], F32)
                BbS_im = SB(ph, "BbSim", [128, 16, 16], F32)
                lev_re = SB(ph, "levre", [128, NLEV, 16], F32)
                lev_im = SB(ph, "levim", [128, NLEV, 16], F32)
                lev_nim = SB(ph, "levnim", [128, NLEV, 16], F32)
                dcol_sb = SB(ph, "dcol", [128, DEPTH * 4], F32)
                P.dma("sp", dcol_sb[:], dcol, writes=[TK])
                with ExitStack() as tmp:
                    pwR_re, pwR_im, frR, fiR = zoh(tmp, tmp, "R", lamR_re[l], lamR_im[l], ldtR[l], 256)
                    bre = SB(tmp, "bre", [128, 256], F32)
                    bim = SB(tmp, "bim", [128, 256], F32)
                    Bb_re = SB(tmp, "Bbre", [128, 256], F32)
                    Bb_im = SB(tmp, "Bbim", [128, 256], F32)
                    u1 = SB(tmp, "u1", [128, 256], F32)
                    u2 = SB(tmp, "u2", [128, 256], F32)
                    P.dma("sp", bre[:], bR_re[l], writes=[TK])
                    P.dma("sp", bim[:], bR_im[l], writes=[TK])
                    cmul(Bb_re[:], Bb_im[:], frR[:], fiR[:], bre[:], bim[:], u1[:], u2[:])
                    v4 = lambda ap: ap.rearrange("p (f q) -> p f q", q=64)
                    for s_ in range(8):
                        k = 7 - s_
                        cmul(B8[:, :, s_, 0, :], B8[:, :, s_, 1, :], v4(pwR_re[:, k, :]), v4(pwR_im[:, k, :]), v4(Bb_re[:]), v4(Bb_im[:]), v4(u1[:]), v4(u2[:]))
                    pwS_re, pwS_im, frS, fiS = zoh(tmp, tmp, "S", lamS_re[l], lamS_im[l], ldtS[l], 16)
                    cre = SB(tmp, "cre", [128, 16, 16], F32)
                    cim = SB(tmp, "cim", [128, 16, 16], F32)
                    bsr = SB(tmp, "bsr", [128, 16, 16], F32)
                    bsi = SB(tmp, "bsi", [128, 16, 16], F32)
                    w1 = SB(tmp, "w1", [128, 16, 16], F32)
                    w2 = SB(tmp, "w2", [128, 16, 16], F32)
                    P.dma("sp", cre[:], cS_re[l].rearrange("p (q i) -> p q i", i=16), writes=[TK])
                    P.dma("sp", cim[:], cS_im[l].rearrange("p (q i) -> p q i", i=16), writes=[TK])
                    P.dma("sp", bsr[:], bS_re[l].rearrange("p (q i) -> p q i", i=16), writes=[TK])
                    P.dma("sp", bsi[:], bS_im[l].rearrange("p (q i) -> p q i", i=16), writes=[TK])
                    bc = lambda ap: ap.unsqueeze(2).to_broadcast([128, 16, 16])
                    cmul(BbS_re[:], BbS_im[:], bc(frS[:]), bc(fiS[:]), bsr[:], bsi[:], w1[:], w2[:])
                    for k in range(9):
                        er_k, ei_k = bc(pwS_re[:, k, :]), bc(pwS_im[:, k, :])
                        TT(w1[:], cre[:], er_k, ALU.mult)
                        TT(w2[:], cim[:], ei_k, ALU.mult)
                        TT(Wr[:, k, :, :], w1[:], w2[:], ALU.subtract)
                        TT(w1[:], cre[:], ei_k, ALU.mult)
                        TT(w2[:], cim[:], er_k, ALU.mult)
                        TT(w1[:], w1[:], w2[:], ALU.add)
                        TS(Wi[:, k, :, :], w1[:], -1.0, ALU.mult)
                    P.op("dve", lambda e: e.tensor_copy(out=lev_re[:, 0, :], in_=pwS_re[:, 8, :]), reads=[TK], writes=[TK])
                    P.op("dve", lambda e: e.tensor_copy(out=lev_im[:, 0, :], in_=pwS_im[:, 8, :]), reads=[TK], writes=[TK])
                    for d_ in range(1, NLEV):
                        cmul(lev_re[:, d_, :], lev_im[:, d_, :], lev_re[:, d_ - 1, :], lev_im[:, d_ - 1, :], lev_re[:, d_ - 1, :], lev_im[:, d_ - 1, :],
                             w1[:, 0, :], w2[:, 0, :])
                    TS(lev_nim[:], lev_im[:], -1.0, ALU.mult)
                    P.barrier()
                if ssm_stop < 2:
                    return
                uT_ft = SB(ph, "uTft", [128, L], BF16)
                ys_t = SB(ph, "yst", [128, L], BF16)
                uT_de = SB(ph, "uTde", [128, 8, NC8], BF16)
                B8pad = SB(ph, "B8pad", [128, 4, 8, 2, 2, 64], BF16)
                Cpad = SB(ph, "Cpad", [128, 4, 2, 9, 128], BF16)
                BbSpad = SB(ph, "BbSpad", [128, 4, 2, 128], BF16)
                BD_sb = SB(ph, "BDsb", [128, 8, 128], BF16)
                sc_ = [[SB(ph, "scan%d%d" % (a_, b_), [128, NC8], F32) for b_ in range(2)] for a_ in range(2)]
                Hb = [[SB(ph, "Hb%d%d" % (q_, r_), [128, NC8], BF16) for r_ in range(2)] for q_ in range(4)]
                g1 = SB(ph, "g1", [128, CW], F32)
                g2_ = SB(ph, "g2", [128, CW], F32)
                psBD = PSU(ph, "psBD", [128, 1024])
                psX = [PSU(ph, "psX%d" % i, [128, 512]) for i in range(2)]
                psY = [PSU(ph, "psY%d" % i, [128, 512]) for i in range(2)]
                P.op("pool", lambda e: e.memset(Cpad[:], 0.0), writes=["Cpad"])
                P.op("pool", lambda e: e.memset(BbSpad[:], 0.0), writes=["BbSpad"])
                uTv = uT_ft[:].rearrange("p (c s) -> p c s", s=8)
                ysv = ys_t[:].rearrange("p (c s) -> p c s", s=8)
                xc = 0
                yc = 0
                for ft in range(4):
                    P.dma("sp", uT_ft[:], uTs[ft * 128:(ft + 1) * 128, :], writes=["uTft"])
                    for s_ in range(8):
                        if s_ % 2 == 0:
                            P.op("act", lambda e: e.activation(out=uT_de[:, s_, :], in_=uTv[:, :, s_], func=AF.Copy), reads=["uTft"], writes=["uTde"])
                        else:
                            P.op("pool", lambda e: e.tensor_copy(out=uT_de[:, s_, :], in_=uTv[:, :, s_]), reads=["uTft"], writes=["uTde"])
                    for gl in range(8):
                        P.op("dve", lambda e: e.tensor_scalar(out=B8pad[:, gl // 2, :, :, gl % 2, :], in0=B8[:, ft],
                                                              scalar1=rowmask[:, gl:gl + 1], scalar2=None, op0=ALU.mult),
                             reads=[TK, "rowmask"], writes=["B8pad"])
                    for ql in range(4):
                        qq = ft * 4 + ql
                        for g2 in range(2):
                            rows = slice(g2 * 64, (g2 + 1) * 64)
                            cs = slice((2 * ql + g2) * 16, (2 * ql + g2) * 16 + 16)
                            for ri, Wsrc, Bsrc in ((0, Wr, BbS_re), (1, Wi, BbS_im)):
                                P.op("pool", lambda e: e.tensor_copy(out=Cpad[rows, ql, ri, :, cs], in_=Wsrc[rows, :, qq, :]), reads=[TK], writes=["Cpad"])
                                P.op("pool", lambda e: e.tensor_copy(out=BbSpad[rows, ql, ri, cs], in_=Bsrc[rows, qq, :]), reads=[TK], writes=["BbSpad"])
                    if ssm_stop < 3:
                        continue
                    for hf in range(2):
                        n_ = 0
                        for ql in range(4):
                            for ri in range(2):
                                P.op("pe", lambda e: e.matmul(psBD[:, hf * 512:(hf + 1) * 512], lhsT=BbSpad[:, ql, ri, :], rhs=Cpad[:, ql, ri, 4 * hf:4 * hf + 4, :],
                                                              start=(n_ == 0), stop=(n_ == 7)), reads=["BbSpad", "Cpad"], writes=["psBD"])
                                n_ += 1
                    P.op("dve", lambda e: e.scalar_tensor_tensor(out=BD_sb[:, 0, :], in0=ident[:], scalar=dcol_sb[:, l * 4 + ft:l * 4 + ft + 1], in1=psBD[:, 0:128],
                                                                 op0=ALU.mult, op1=ALU.add), reads=["psBD", "ident", TK], writes=["BDsb"])
                    P.op("act", lambda e: e.activation(out=BD_sb[:, 1:8, :], in_=psBD[:, 128:1024].rearrange("p (t q) -> p t q", q=128), func=AF.Copy),
                         reads=["psBD"], writes=["BDsb"])
                    if ssm_stop < 4:
                        continue
                    for ql in range(4):
                        qq = ft * 4 + ql
                        for ri in range(2):
                            for cb in range(NCH):
                                b = xc % 2
                                xc += 1
                                for s_ in range(8):
                                    P.op("pe", lambda e: e.matmul(psX[b][:, 0:CW], lhsT=B8pad[:, ql, s_, ri, :, :], rhs=uT_de[:, s_, cb * CW:(cb + 1) * CW],
                                                                  start=(s_ == 0), stop=(s_ == 7)), reads=["B8pad", "uTde"], writes=[("psX", b)])
                                P.op("act", lambda e: e.activation(out=sc_[0][ri][:, cb * CW:(cb + 1) * CW], in_=psX[b][:, 0:CW], func=AF.Copy),
                                     reads=[("psX", b)], writes=[("scan", 0)])
                        cur = 0
                        for d_ in range(NLEV):
                            sh = 1 << d_
                            src, dst = sc_[cur], sc_[1 - cur]
                            lre = lev_re[:, d_, qq:qq + 1]
                            lim = lev_im[:, d_, qq:qq + 1]
                            lnim = lev_nim[:, d_, qq:qq + 1]
                            tk_s, tk_d = ("scan", cur), ("scan", 1 - cur)
                            m_ = NC8 - sh
                            P.op("dve", lambda e: e.scalar_tensor_tensor(out=dst[0][:, sh:], in0=src[0][:, 0:m_], scalar=lre, in1=src[0][:, sh:], op0=ALU.mult, op1=ALU.add),
                                 reads=[tk_s, TK], writes=[tk_d])
                            P.op("dve", lambda e: e.scalar_tensor_tensor(out=dst[0][:, sh:], in0=src[1][:, 0:m_], scalar=lnim, in1=dst[0][:, sh:], op0=ALU.mult, op1=ALU.add),
                                 reads=[tk_s, TK], writes=[tk_d])
                            P.op("dve", lambda e: e.scalar_tensor_tensor(out=dst[1][:, sh:], in0=src[1][:, 0:m_], scalar=lre, in1=src[1][:, sh:], op0=ALU.mult, op1=ALU.add),
                                 reads=[tk_s, TK], writes=[tk_d])
                            P.op("dve", lambda e: e.scalar_tensor_tensor(out=dst[1][:, sh:], in0=src[0][:, 0:m_], scalar=lim, in1=dst[1][:, sh:], op0=ALU.mult, op1=ALU.add),
                                 reads=[tk_s, TK], writes=[tk_d])
                            for ri in range(2):
                                P.op("pool", lambda e: e.tensor_copy(out=dst[ri][:, 0:sh], in_=src[ri][:, 0:sh]), reads=[tk_s], writes=[tk_d])
                            cur = 1 - cur
                        for ri in range(2):
                            P.op("pool", lambda e: e.memset(Hb[ql][ri][:, 0:1], 0.0), writes=[("Hb", ql)])
                            P.op("act", lambda e: e.activation(out=Hb[ql][ri][:, 1:NC8], in_=sc_[cur][ri][:, 0:NC8 - 1], func=AF.Copy),
                                 reads=[("scan", cur)], writes=[("Hb", ql)])
                    if ssm_stop < 5:
                        continue
                    for t_ in range(8):
                        for cb in range(NCH):
                            b = yc % 2
                            yc += 1
                            cs = slice(cb * CW, (cb + 1) * CW)
                            nmm = (t_ + 1) + 8
                            n_ = 0
                            for tau in range(t_ + 1):
                                P.op("pe", lambda e: e.matmul(psY[b][:, 0:CW], lhsT=BD_sb[:, tau, :], rhs=uT_de[:, t_ - tau, cs], start=(n_ == 0), stop=(n_ == nmm - 1)),
                                     reads=["BDsb", "uTde"], writes=[("psY", b)])
                                n_ += 1
                            for ql in range(4):
                                for ri in range(2):
                                    P.op("pe", lambda e: e.matmul(psY[b][:, 0:CW], lhsT=Cpad[:, ql, ri, t_ + 1, :], rhs=Hb[ql][ri][:, cs], start=(n_ == 0), stop=(n_ == nmm - 1)),
                                         reads=["Cpad", ("Hb", ql)], writes=[("psY", b)])
                                    n_ += 1
                            y_ = psY[b][:, 0:CW]
                            P.op("act", lambda e: e.activation(out=g1[:], in_=y_, func=AF.Square), reads=[("psY", b)], writes=["g1"])
                            P.op("dve", lambda e: e.tensor_scalar(out=g1[:], in0=g1[:], scalar1=0.044715, scalar2=1.0, op0=ALU.mult, op1=ALU.add), reads=["g1"], writes=["g1"])
                            P.op("dve", lambda e: e.tensor_tensor(out=g1[:], in0=g1[:], in1=y_, op=ALU.mult), reads=["g1", ("psY", b)], writes=["g1"])
                            P.op("act", lambda e: e.activation(out=g2_[:], in_=g1[:], func=AF.Sigmoid, scale=1.5957691216057308), reads=["g1"], writes=["g2"])
                            P.op("dve", lambda e: e.tensor_tensor(out=ysv[:, cs, t_], in0=g2_[:], in1=y_, op=ALU.mult), reads=["g2", ("psY", b)], writes=["yst"])
                    P.dma("sp", ysT[ft * 128:(ft + 1) * 128, :], ys_t[:], reads=["yst"], writes=[("ysT", ft)])
                P.barrier()

        for l in range(nlayers):
            xsrc = x_in if l == 0 else xres
            xdst = y_out if l == nlayers - 1 else xres

            if "P" in phases:
              with ExitStack() as ph:
                condcol = SB(ph, "condcol", [128, 8], F32)
                condrep = SB(ph, "condrep", [128, 8, 128], F32)
                wc = [SB(ph, "wc%d" % i, [128, 8, 512], F32) for i in range(2)]
                bcb = [SB(ph, "bcb%d" % i, [128, 512], F32) for i in range(2)]
                modB = SB(ph, "modB", [128, 6 * D], F32)
                psm = [PSU(ph, "psm%d" % i, [128, 512]) for i in range(2)]
                pst = PSU(ph, "pst", [128, 48])
                P.dma("sp", condcol[:], ccol, writes=["condcol"])
                P.op("act", lambda e: e.activation(out=condcol[:], in_=condcol[:], func=AF.Silu), reads=["condcol"], writes=["condcol"])
                for k in range(8):
                    P.op("dve", lambda e: e.tensor_copy(out=condrep[:, k, :], in_=condcol[:, k:k + 1].to_broadcast([128, 128])),
                         reads=["condcol"], writes=["condrep"])
                for j in range(12):
                    b = j % 2
                    P.dma("sp", wc[b][:], w_cond[l, :, j * 512:(j + 1) * 512].rearrange("(k p) n -> p k n", p=128), writes=[("wc", b)])
                    P.dma("sp", bcb[b][:], b_cond[l, j * 512:(j + 1) * 512].partition_broadcast(128), writes=[("bcb", b)])
                    for k in range(8):
                        P.op("pe", lambda e: e.matmul(psm[b][:], lhsT=condrep[:, k, :], rhs=wc[b][:, k, :], start=(k == 0), stop=(k == 7)),
                             reads=["condrep", ("wc", b)], writes=[("psm", b)])
                    P.op("dve", lambda e: e.tensor_tensor(out=modB[:, j * 512:(j + 1) * 512], in0=psm[b][:], in1=bcb[b][:], op=ALU.add),
                         reads=[("psm", b), ("bcb", b)], writes=["modB"])
                for j in range(48):
                    P.op("pe", lambda e: e.matmul(pst[:, j:j + 1], lhsT=modB[:, j * 128:(j + 1) * 128], rhs=ident[:, 0:1], start=True, stop=True),
                         reads=["modB", "ident"], writes=["pst"])
                P.op("dve", lambda e: e.tensor_copy(out=modT[:], in_=pst[:]), reads=["pst"], writes=["modT"])
                for a in (8, 32):
                    P.op("dve", lambda e: e.tensor_scalar(out=modT[:, a:a + 8], in0=modT[:, a:a + 8], scalar1=1.0, scalar2=None, op0=ALU.add),
                         reads=["modT"], writes=["modT"])
                P.op("dve", lambda e: e.tensor_scalar(out=G12[:, 0, :], in0=modB[:, 2048:3072], scalar1=1.0, scalar2=None, op0=ALU.add),
                     reads=["modB"], writes=["G12"])
                P.op("dve", lambda e: e.tensor_scalar(out=G12[:, 1, :], in0=modB[:, 5120:6144], scalar1=1.0, scalar2=None, op0=ALU.add),
                     reads=["modB"], writes=["G12"])
                if "dbg_mod" in dbg and l == 0:
                    P.dma("sp", dbg_mod, modB[:], reads=["modB"])
                P.barrier()

            if "A" in phases:
              with ExitStack() as ph:
                wi = SB(ph, "wi", [128, 8, DIN], BF16)
                wr = SB(ph, "wr", [128, 8, 1856], BF16)
                xs = SB(ph, "xs", [128, 4, D], F32)
                uT = SB(ph, "uT", [128, 8, 512], BF16)
                rc = [SB(ph, "rc%d" % i, [128, 512], F32) for i in range(4)]
                t1 = SB(ph, "t1", [128, 512], F32)
                t2 = SB(ph, "t2", [128, 512], F32)
                stg = {}
                for nm, nt_ in (("ssm", 4), ("q", 8), ("k", 2), ("qi", 4), ("ki", 1), ("gs", 8), ("ga", 8)):
                    stg[nm] = SB(ph, "stg_" + nm, [128, nt_, 512], BF16)
                vst = SB(ph, "vst", [128, 4, 256], BF16)
                wst = SB(ph, "wst", [128, 4, 8], F32)
                ptp = [PSU(ph, "ptp%d" % i, [128, 512]) for i in range(2)]
                pz = [PSU(ph, "pz%d" % i, [128, 512]) for i in range(2)]
                pzr = [PSU(ph, "pzr%d" % i, [128, 512]) for i in range(2)]
                pv = PSU(ph, "pv", [128, 512])
                for k in range(8):
                    P.dma("pool", wi[:, k, :], w_in[l, k * 128:(k + 1) * 128, :], writes=["wi"])
                ro = 0
                rinfo = {}
                for nm, off, ncol, half in (("q", OFF_Q, 1024, 64), ("k", OFF_K, 256, 64), ("qi", OFF_QI, 512, 32), ("ki", OFF_KI, 64, 32)):
                    rinfo[nm] = ro
                    for k in range(8):
                        src = wi[:, k, off:off + ncol].rearrange("p (h two f) -> p h two f", two=2, f=half)
                        dst = wr[:, k, ro:ro + ncol].rearrange("p (h two f) -> p h two f", two=2, f=half)
                        P.op("pool", lambda e: e.tensor_copy(out=dst[:, :, 0, :], in_=src[:, :, 1, :]), reads=["wi"], writes=["wr"])
                        P.op("pool", lambda e: e.tensor_copy(out=dst[:, :, 1, :], in_=src[:, :, 0, :]), reads=["wi"], writes=["wr"])
                    ro += ncol
                zc = [0]

                def zbank():
                    zc[0] += 1
                    return zc[0] % 2

                for T in range(NS):
                    tsl = slice(T * 512, (T + 1) * 512)
                    P.dma("sp", xs[:], xsrc[tsl, :].rearrange("(s p) d -> p s d", p=128), writes=["xs"])
                    for i, tab in enumerate((ropeA_c, ropeA_s, ropeB_c, ropeB_s)):
                        P.dma("sp", rc[i][:], tab[:, tsl], writes=[("rc", i)])
                    for k in range(8):
                        b = k % 2
                        for s in range(4):
                            P.op("pe", lambda e: e.transpose(ptp[b][:, s * 128:(s + 1) * 128], xs[:, s, k * 128:(k + 1) * 128], ident[:]),
                                 reads=["xs", "ident"], writes=[("ptp", b)])
                        P.op("act", lambda e: e.activation(out=uT[:, k, :], in_=ptp[b][:], func=AF.Identity,
                                                           scale=modT[:, 8 + k:9 + k], bias=modT[:, k:k + 1]),
                             reads=[("ptp", b), "modT"], writes=["uT"])
                    for nm, off, ntl, rows, kind, dst in (("ssm", OFF_SSM, 4, 128, "copy", uTs), ("q", OFF_Q, 8, 128, "ropeA", qT),
                                                         ("k", OFF_K, 2, 128, "ropeA", kT), ("qi", OFF_QI, 4, 128, "ropeB", qiT),
                                                         ("ki", OFF_KI, 1, 64, "ropeB", kiT), ("gs", OFF_GS, 8, 128, "sig", gsT),
                                                         ("ga", OFF_GA, 8, 128, "sig", gaT)):
                        st = stg[nm]
                        for m in range(ntl):
                            b = zbank()
                            for k in range(8):
                                P.op("pe", lambda e: e.matmul(pz[b][0:rows, :], lhsT=wi[:, k, off + m * 128: off + m * 128 + rows], rhs=uT[:, k, :],
                                                              start=(k == 0), stop=(k == 7)),
                                     reads=["wi", "uT"], writes=[("pz", b)])
                            if kind.startswith("rope"):
                                ro = rinfo[nm]
                                ci, si = (0, 1) if kind == "ropeA" else (2, 3)
                                for k in range(8):
                                    P.op("pe", lambda e: e.matmul(pzr[b][0:rows, :], lhsT=wr[:, k, ro + m * 128: ro + m * 128 + rows], rhs=uT[:, k, :],
                                                                  start=(k == 0), stop=(k == 7)),
                                         reads=["wr", "uT"], writes=[("pzr", b)])
                                P.op("dve", lambda e: e.tensor_tensor(out=t1[0:rows, :], in0=pz[b][0:rows, :], in1=rc[ci][0:rows, :], op=ALU.mult),
                                     reads=[("pz", b), ("rc", ci)], writes=["t1"])
                                P.op("dve", lambda e: e.tensor_tensor(out=t2[0:rows, :], in0=pzr[b][0:rows, :], in1=rc[si][0:rows, :], op=ALU.mult),
                                     reads=[("pzr", b), ("rc", si)], writes=["t2"])
                                P.op("dve", lambda e: e.tensor_tensor(out=st[0:rows, m, :], in0=t1[0:rows, :], in1=t2[0:rows, :], op=ALU.add),
                                     reads=["t1", "t2"], writes=[("stg", nm)])
                            elif kind == "sig":
                                P.op("act", lambda e: e.activation(out=st[:, m, :], in_=pz[b][:], func=AF.Sigmoid),
                                     reads=[("pz", b)], writes=[("stg", nm)])
                            else:
                                P.op("act", lambda e: e.activation(out=st[:, m, :], in_=pz[b][:], func=AF.Copy),
                                     reads=[("pz", b)], writes=[("stg", nm)])
                        if rows == 128:
                            P.dma("pool", dst[:, tsl].rearrange("(m p) t -> p m t", p=128), st[:], reads=[("stg", nm)], writes=[(nm + "T", T)])
                        else:
                            P.dma("pool", dst[:, tsl], st[0:rows, 0, :], reads=[("stg", nm)], writes=[(nm + "T", T)])
                    for s in range(4):
                        for k in range(8):
                            P.op("pe", lambda e: e.matmul(pv[:, 0:256], lhsT=uT[:, k, s * 128:(s + 1) * 128], rhs=wi[:, k, OFF_V:OFF_V + 256],
                                                          start=(k == 0), stop=(k == 7)), reads=["uT", "wi"], writes=["pv"])
                        P.op("act", lambda e: e.activation(out=vst[:, s, :], in_=pv[:, 0:256], func=AF.Copy), reads=["pv"], writes=["vst"])
                        for k in range(8):
                            P.op("pe", lambda e: e.matmul(pv[:, 256:264], lhsT=uT[:, k, s * 128:(s + 1) * 128], rhs=wi[:, k, OFF_W:OFF_W + 8],
                                                          start=(k == 0), stop=(k == 7)), reads=["uT", "wi"], writes=["pv"])
                        P.op("dve", lambda e: e.tensor_copy(out=wst[:, s, :], in_=pv[:, 256:264]), reads=["pv"], writes=["wst"])
                    P.dma("pool", vd[tsl, :].rearrange("(s p) d -> p s d", p=128), vst[:], reads=["vst"], writes=[("vd", T)])
                    P.dma("pool", wid[tsl, :].rearrange("(s p) d -> p s d", p=128), wst[:], reads=["wst"], writes=[("wid", T)])
                P.barrier()

            if "B" in phases:
                phase_ssm(l)

            if "C" in phases:
                phase_dsa(l)

            if "D" in phases:
                phase_d(l, xsrc)
            if "E" in phases:
                phase_e(l, xdst)

        P.barrier(engines=("sp",))
        build.ninst = P.ninst
    return nc


def _rope_tables(L):
    pos = np.arange(L).astype(np.float32)
    out = []
    for half in (64, 32):
        inv = (np.float32(10000.0) ** (-(np.arange(half, dtype=np.float32)) / np.float32(half))).astype(np.float32)
        ang = (pos[None, :] * inv[:, None]).astype(np.float32)
        cos = np.cos(ang).astype(np.float32)
        sin = np.sin(ang).astype(np.float32)
        reps = 128 // half
        c = np.concatenate([cos] * reps, axis=0)
        s = np.concatenate([(-sin if (r % 2 == 0) else sin) for r in range(reps)], axis=0)
        out += [np.ascontiguousarray(c), np.ascontiguousarray(s)]
    return out


def _shared_inputs(inp, L):
    f = lambda a: np.ascontiguousarray(np.asarray(a, dtype=np.float32))
    sh = {}
    for k in ("w_cond", "b_cond", "w_in", "ssm_w_glu", "p_ssm", "p_attn", "w_out", "w_gate_up", "w_down"):
        sh[k] = f(inp[k])
    sh["bglu"] = f(np.asarray(inp["ssm_b_glu"]).reshape(DEPTH, 4, 128).transpose(2, 0, 1).reshape(128, DEPTH * 4))
    sh["dcol"] = f(np.asarray(inp["ssm_d"]).reshape(DEPTH, 4, 128).transpose(2, 0, 1).reshape(128, DEPTH * 4))
    sh["lnp"] = f(np.stack([inp["ln1_g"], inp["ln1_b"], inp["ln2_g"], inp["ln2_b"]], axis=1))
    lam_re, lam_im, ldt = np.asarray(inp["ssm_lam_re"]), np.asarray(inp["ssm_lam_im"]), np.asarray(inp["ssm_log_dt"])
    b_re, b_im = np.asarray(inp["ssm_b_re"]), np.asarray(inp["ssm_b_im"])
    c_re, c_im = np.asarray(inp["ssm_c_re"]), np.asarray(inp["ssm_c_im"])

    def Rl(a):
        a = a.reshape(DEPTH, 4, 8, 1, 64).transpose(0, 2, 3, 1, 4)
        return f(np.broadcast_to(a, (DEPTH, 8, 16, 4, 64)).reshape(DEPTH, 128, 256))

    def Sl(a):
        return f(a.reshape(DEPTH, 16, 2, 64).transpose(0, 2, 3, 1).reshape(DEPTH, 128, 16))

    sh["lamR_re"], sh["lamR_im"] = Rl(lam_re), Rl(lam_im)
    sh["ldtR"] = Rl(np.broadcast_to(ldt[:, :, None], (DEPTH, 32, 64)))
    sh["lamS_re"], sh["lamS_im"] = Sl(lam_re), Sl(lam_im)
    sh["ldtS"] = Sl(np.broadcast_to(ldt[:, :, None], (DEPTH, 32, 64)))
    for nm, a in (("bR_re", b_re), ("bR_im", b_im)):
        sh[nm] = f(a.reshape(DEPTH, 4, 8, 64, 16).transpose(0, 2, 4, 1, 3).reshape(DEPTH, 128, 256))
    for nm, a in (("bS_re", b_re), ("bS_im", b_im)):
        sh[nm] = f(a.reshape(DEPTH, 16, 2, 64, 16).transpose(0, 2, 3, 1, 4).reshape(DEPTH, 128, 256))
    for nm, a in (("cS_re", c_re), ("cS_im", c_im)):
        sh[nm] = f(a.reshape(DEPTH, 16, 2, 16, 64).transpose(0, 2, 4, 1, 3).reshape(DEPTH, 128, 256))
    sh["ident"] = np.eye(128, dtype=np.float32)
    qq = np.arange(128)[:, None]
    ss = np.arange(128)[None, :]
    sh["causal"] = np.where(ss <= qq, 0.0, NEG).astype(np.float32)
    sh["rowmask"] = (np.arange(128)[:, None] // 16 == np.arange(8)[None, :]).astype(np.float32)
    ra_c, ra_s, rb_c, rb_s = _rope_tables(L)
    sh["ropeA_c"], sh["ropeA_s"], sh["ropeB_c"], sh["ropeB_s"] = ra_c, ra_s, rb_c, rb_s
    return sh


def make_in_maps(inp, L, nb):
    sh = _shared_inputs(inp, L)
    x = np.asarray(inp["x"], dtype=np.float32)
    c = np.asarray(inp["c"], dtype=np.float32)
    maps = []
    for b in range(nb):
        m = dict(sh)
        m["x"] = np.ascontiguousarray(x[b])
        m["ccol"] = np.ascontiguousarray(c[b].reshape(8, 128).T)
        maps.append(m)
    return maps


_NC_CACHE = {}


def kernel(**inputs):
    x = np.asarray(inputs["x"])
    B, L, _ = x.shape
    if L not in _NC_CACHE:
        _NC_CACHE[L] = build(L)
    nc = _NC_CACHE[L]
    maps = make_in_maps(inputs, L, B)
    res = run_bass_kernel_spmd(nc, maps, core_ids=list(range(B)))
    return np.stack([np.asarray(r["y"]) for r in res.results], axis=0).astype(np.float32)
```

```python
import math
import numpy as np
from contextlib import ExitStack
import concourse.bass as bass
import concourse.mybir as mybir
from concourse.bass_utils import run_bass_kernel_spmd

F32 = mybir.dt.float32
BF16 = mybir.dt.bfloat16
AF = mybir.ActivationFunctionType
ALU = mybir.AluOpType
AX = mybir.AxisListType

D = 1024
DEPTH = 2
DIN = 4680
DFF = 2816
OFF_SSM, OFF_Q, OFF_K, OFF_V, OFF_QI, OFF_KI, OFF_W, OFF_GS, OFF_GA = 0, 512, 1536, 1792, 2048, 2560, 2624, 2632, 3656
ALPHA = (2 * DEPTH) ** 0.25
LN_EPS = 1e-5
NBIS = 12
NEG = -1.0e30


class Prog:
    def __init__(self, nc, es):
        self.nc = nc
        self.es = es
        self.eng = {"pe": nc.tensor, "dve": nc.vector, "act": nc.scalar, "pool": nc.gpsimd, "sp": nc.sync}
        self.sems = []
        self.cur = {}
        self.waited = {e: {} for e in self.eng}
        self.lastw = {}
        self.readers = {}
        self.dpool = {}
        self.dnext = {}
        self.ninst = 0
        for e in ("pe", "dve", "act", "pool"):
            self.cur[e] = [self._newsem(), 0]
        for q, n in (("sp", 12), ("pool", 8)):
            self.dpool[q] = [[self._newsem(), 0] for _ in range(n)]
            self.dnext[q] = 0

    def _newsem(self):
        s = self.es.enter_context(self.nc.semaphore("s%d" % len(self.sems)))
        self.sems.append(s)
        return len(self.sems) - 1

    def _wait(self, e, dep):
        si, val, src = dep
        if src == e and e == "pe":
            return
        w = self.waited[e]
        if w.get(si, 0) >= val:
            return
        self.eng[e].wait_ge(self.sems[si], val)
        w[si] = val
        self.ninst += 1

    def _deps(self, e, reads, writes):
        for t in reads:
            d = self.lastw.get(t)
            if d:
                self._wait(e, d)
        for t in writes:
            d = self.lastw.get(t)
            if d:
                self._wait(e, d)
            for r in self.readers.get(t, {}).values():
                self._wait(e, r)

    def _book(self, h, reads, writes):
        key = h[2] if h[2] != "dma" else ("dma", h[0])
        for t in reads:
            self.readers.setdefault(t, {})[key] = h
        for t in writes:
            self.lastw[t] = h
            self.readers[t] = {}

    def op(self, e, fn, reads=(), writes=()):
        self._deps(e, reads, writes)
        inst = fn(self.eng[e])
        s = self.cur[e]
        if s[1] >= 30000:
            s = self.cur[e] = [self._newsem(), 0]
        s[1] += 1
        inst.then_inc(self.sems[s[0]], 1)
        self.ninst += 1
        self._book((s[0], s[1], e), reads, writes)

    def dma(self, q, out, in_, reads=(), writes=(), **kw):
        self._deps(q, reads, writes)
        pool = self.dpool[q]
        i = self.dnext[q]
        self.dnext[q] = (i + 1) % len(pool)
        ent = pool[i]
        if ent[1] > 0:
            self._wait(q, (ent[0], ent[1], "dma"))
        if ent[1] > 48000:
            ent[0] = self._newsem()
            ent[1] = 0
        inst = self.eng[q].dma_start(out=out, in_=in_, **kw)
        ent[1] += 16
        inst.then_inc(self.sems[ent[0]], 16)
        self.ninst += 1
        self._book((ent[0], ent[1], "dma"), reads, writes)

    def barrier(self, engines=("pe", "dve", "act", "pool", "sp")):
        hs = []
        for e in ("pe", "dve", "act", "pool"):
            s = self.cur[e]
            if s[1] > 0:
                hs.append((s[0], s[1], e))
        for q in self.dpool:
            for ent in self.dpool[q]:
                if ent[1] > 0:
                    hs.append((ent[0], ent[1], "dma"))
        for e in engines:
            for h in hs:
                if h[2] == e:
                    continue
                self._wait(e, h)


def build(L, dbg=(), nlayers=DEPTH, phases="PABCDE", inj=(), ssm_stop=9):
    NT = L // 128
    NS = L // 512
    NC8 = L // 8
    NCH = max(1, NC8 // 512)
    CW = NC8 // NCH
    NLEV = int(round(math.log2(NC8)))
    nc = bass.Bass("TRN2", target_bir_lowering=False)

    def din(name, shape, dt=F32):
        return nc.dram_tensor(name, list(shape), dt, kind="ExternalInput").ap()

    def dscr(name, shape, dt):
        kind = "ExternalOutput" if name in dbg else ("ExternalInput" if name in inj else "Internal")
        return nc.dram_tensor(name, list(shape), dt, kind=kind).ap()

    x_in = din("x", [L, D])
    ccol = din("ccol", [128, 8])
    w_cond = din("w_cond", [DEPTH, D, 6 * D])
    b_cond = din("b_cond", [DEPTH, 6 * D])
    w_in = din("w_in", [DEPTH, D, DIN])
    w_glu = din("ssm_w_glu", [DEPTH, 512, 512])
    bglu = din("bglu", [128, DEPTH * 4])
    p_ssm = din("p_ssm", [DEPTH, 512, D])
    p_attn = din("p_attn", [DEPTH, D, D])
    w_out = din("w_out", [DEPTH, D, D])
    lnp = din("lnp", [DEPTH, 4, D])
    w_gu = din("w_gate_up", [DEPTH, D, 2 * DFF])
    w_dn = din("w_down", [DEPTH, DFF, D])
    lamR_re = din("lamR_re", [DEPTH, 128, 256])
    lamR_im = din("lamR_im", [DEPTH, 128, 256])
    ldtR = din("ldtR", [DEPTH, 128, 256])
    bR_re = din("bR_re", [DEPTH, 128, 256])
    bR_im = din("bR_im", [DEPTH, 128, 256])
    dcol = din("dcol", [128, DEPTH * 4])
    lamS_re = din("lamS_re", [DEPTH, 128, 16])
    lamS_im = din("lamS_im", [DEPTH, 128, 16])
    ldtS = din("ldtS", [DEPTH, 128, 16])
    cS_re = din("cS_re", [DEPTH, 128, 256])
    cS_im = din("cS_im", [DEPTH, 128, 256])
    bS_re = din("bS_re", [DEPTH, 128, 256])
    bS_im = din("bS_im", [DEPTH, 128, 256])
    ident_d = din("ident", [128, 128])
    causal_d = din("causal", [128, 128])
    rowmask_d = din("rowmask", [128, 8])
    ropeA_c = din("ropeA_c", [128, L])
    ropeA_s = din("ropeA_s", [128, L])
    ropeB_c = din("ropeB_c", [128, L])
    ropeB_s = din("ropeB_s", [128, L])
    y_out = nc.dram_tensor("y", [L, D], F32, kind="ExternalOutput").ap()

    xres = dscr("xres", [L, D], F32)
    x1d = dscr("x1d", [L, D], F32)
    uTs = dscr("uTs", [512, L], BF16)
    qT = dscr("qT", [1024, L], BF16)
    kT = dscr("kT", [256, L], BF16)
    qiT = dscr("qiT", [512, L], BF16)
    kiT = dscr("kiT", [64, L], BF16)
    gsT = dscr("gsT", [1024, L], BF16)
    gaT = dscr("gaT", [1024, L], BF16)
    vd = dscr("vd", [L, 256], BF16)
    wid = dscr("wid", [L, 8], F32)
    ysT = dscr("ysT", [512, L], BF16)
    yaT = dscr("yaT", [1024, L], BF16)
    fpart = dscr("fpart", [L, D], F32)
    dbg_mod = dscr("dbg_mod", [128, 6 * D], F32)
    dbg_sc = dscr("dbg_sc", [128, L], F32)
    dbg_thr = dscr("dbg_thr", [128, 16], F32)

    with ExitStack() as es:
        P = Prog(nc, es)

        uid = [0]

        def SB(st, name, shape, dt):
            uid[0] += 1
            return st.enter_context(nc.sbuf_tensor("sb%d_%s" % (uid[0], name), list(shape), dt))

        def PSU(st, name, shape, dt=F32):
            uid[0] += 1
            return st.enter_context(nc.psum_tensor("ps%d_%s" % (uid[0], name), list(shape), dt))

        ident = SB(es, "ident", [128, 128], F32)
        identb = SB(es, "identb", [128, 128], BF16)
        ident4b = SB(es, "ident4b", [128, 4, 128], BF16)
        causal = SB(es, "causal", [128, 128], F32)
        rowmask = SB(es, "rowmask", [128, 8], F32)
        epsc = SB(es, "epsc", [128, 1], F32)
        modT = SB(es, "modT", [128, 48], F32)
        G12 = SB(es, "G12", [128, 2, D], F32)
        P.dma("sp", ident[:], ident_d, writes=["ident"])
        P.dma("sp", causal[:], causal_d, writes=["causal"])
        P.dma("sp", rowmask[:], rowmask_d, writes=["rowmask"])
        P.op("dve", lambda e: e.tensor_copy(out=identb[:], in_=ident[:]), reads=["ident"], writes=["identb"])
        P.op("dve", lambda e: e.memset(epsc[:], LN_EPS), writes=["epsc"])
        for h in range(4):
            P.op("dve", lambda e: e.tensor_copy(out=ident4b[:, h, :], in_=ident[:]), reads=["ident"], writes=["ident4b"])


        def layer_norm_tile(vt, dst, lnb, stats, mv, rstd, xtok="xs"):
            for hf in range(2):
                P.op("dve", lambda e: e.bn_stats(out=stats[:, hf, :], in_=vt[:, hf * 512:(hf + 1) * 512]), reads=["vt"], writes=["stats"])
            P.op("dve", lambda e: e.bn_aggr(out=mv[:], in_=stats[:].rearrange("p a b -> p (a b)")), reads=["stats"], writes=["mv"])
            P.op("act", lambda e: e.activation(out=rstd[:], in_=mv[:, 1:2], func=AF.Sqrt, bias=epsc[:, 0:1], scale=1.0), reads=["mv"], writes=["rstd"])
            P.op("dve", lambda e: e.reciprocal(out=rstd[:], in_=rstd[:]), reads=["rstd"], writes=["rstd"])
            P.op("dve", lambda e: e.tensor_scalar(out=dst, in0=vt[:], scalar1=mv[:, 0:1], scalar2=rstd[:, 0:1], op0=ALU.subtract, op1=ALU.mult),
                 reads=["vt", "mv", "rstd"], writes=[xtok])
            P.op("dve", lambda e: e.tensor_tensor(out=dst, in0=dst, in1=lnb[:, 0, :], op=ALU.mult), reads=[xtok, "lnb"], writes=[xtok])
            P.op("dve", lambda e: e.tensor_tensor(out=dst, in0=dst, in1=lnb[:, 1, :], op=ALU.add), reads=[xtok, "lnb"], writes=[xtok])

        def phase_d(l, xsrc):
            with ExitStack() as ph:
                wglu_sb = SB(ph, "wglu", [128, 4, 512], BF16)
                pssm_sb = SB(ph, "pssm", [128, 4, D], BF16)
                pattn_sb = SB(ph, "pattn", [128, 8, D], BF16)
                wout_sb = SB(ph, "wout", [128, 8, D], BF16)
                wstg = [SB(ph, "wstg%d" % i, [128, D], F32) for i in range(2)]
                bglu_sb = SB(ph, "bglu", [128, DEPTH * 4], F32)
                ys_b = [SB(ph, "ys%d" % i_, [128, 4, 512], BF16) for i_ in range(2)]
                ya_b = [SB(ph, "ya%d" % i_, [128, 8, 512], BF16) for i_ in range(2)]
                gs_b = [SB(ph, "gs%d" % i_, [128, 8, 512], BF16) for i_ in range(2)]
                ga_b = [SB(ph, "ga%d" % i_, [128, 8, 512], BF16) for i_ in range(2)]
                sg = SB(ph, "sg", [128, 512], F32)
                yssm = SB(ph, "yssm", [128, 4, 512], BF16)
                tA = SB(ph, "tA", [128, 512], F32)
                tB = SB(ph, "tB", [128, 512], F32)
                merged = SB(ph, "merged", [128, 8, 512], BF16)
                xsb = [SB(ph, "xs%d" % i_, [128, 4, D], F32) for i_ in range(2)]
                vt = SB(ph, "vt", [128, D], F32)
                stats = SB(ph, "stats", [128, 2, 6], F32)
                mv = SB(ph, "mv", [128, 2], F32)
                rstd = SB(ph, "rstd", [128, 1], F32)
                pg = [PSU(ph, "pg%d" % i, [128, 512]) for i in range(2)]
                pA = [PSU(ph, "pA%d" % i, [128, 512]) for i in range(2)]
                pB = [PSU(ph, "pB%d" % i, [128, 512]) for i in range(2)]
                phh = PSU(ph, "phh", [128, D])
                P.dma("sp", bglu_sb[:], bglu, writes=["bglu"])
                lnb = SB(ph, "lnb", [128, 2, D], F32)
                for i_ in range(2):
                    P.dma("sp", lnb[:, i_, :], lnp[l, i_, :].partition_broadcast(128), writes=["lnb"])
                P.dma("pool", wglu_sb[:], w_glu[l].rearrange("(k p) n -> p k n", p=128), writes=["wglu"])
                P.dma("pool", pssm_sb[:], p_ssm[l].rearrange("(k p) n -> p k n", p=128), writes=["pssm"])
                for k in range(8):
                    P.dma("pool", pattn_sb[:, k, :], p_attn[l, k * 128:(k + 1) * 128, :], writes=["pattn"])
                for k in range(8):
                    b = k % 2
                    P.dma("sp", wstg[b][:], w_out[l, k * 128:(k + 1) * 128, :], writes=[("wstg", b)])
                    P.op("dve", lambda e: e.tensor_tensor(out=wout_sb[:, k, :], in0=wstg[b][:], in1=G12[:, 0, :], op=ALU.mult),
                         reads=[("wstg", b), "G12"], writes=["wout"])
                cnt = 0
                for T in range(NS):
                    tsl = slice(T * 512, (T + 1) * 512)
                    pb = T % 2
                    ys_sb, ya_sb, gs_sb, ga_sb, xs = ys_b[pb], ya_b[pb], gs_b[pb], ga_b[pb], xsb[pb]
                    YS, YA, GS, GA, XT = ("ys", pb), ("ya", pb), ("gs", pb), ("ga", pb), ("xs", pb)
                    P.dma("sp", ys_sb[:], ysT[:, tsl].rearrange("(k p) t -> p k t", p=128), reads=[("ysT", T)], writes=[YS])
                    P.dma("sp", ya_sb[:], yaT[:, tsl].rearrange("(k p) t -> p k t", p=128), reads=[("yaT", T)], writes=[YA])
                    P.dma("sp", gs_sb[:], gsT[:, tsl].rearrange("(k p) t -> p k t", p=128), reads=[("gsT", T)], writes=[GS])
                    P.dma("sp", ga_sb[:], gaT[:, tsl].rearrange("(k p) t -> p k t", p=128), reads=[("gaT", T)], writes=[GA])
                    P.dma("sp", xs[:], xsrc[tsl, :].rearrange("(s p) d -> p s d", p=128), writes=[XT])
                    for m in range(4):
                        b = m % 2
                        for k in range(4):
                            P.op("pe", lambda e: e.matmul(pg[b][:], lhsT=wglu_sb[:, k, m * 128:(m + 1) * 128], rhs=ys_sb[:, k, :], start=(k == 0), stop=(k == 3)),
                                 reads=["wglu", YS], writes=[("pg", b)])
                        P.op("act", lambda e: e.activation(out=sg[:], in_=pg[b][:], func=AF.Sigmoid, bias=bglu_sb[:, l * 4 + m:l * 4 + m + 1], scale=1.0),
                             reads=[("pg", b), "bglu"], writes=["sg"])
                        P.op("dve", lambda e: e.tensor_tensor(out=yssm[:, m, :], in0=ys_sb[:, m, :], in1=sg[:], op=ALU.mult),
                             reads=[YS, "sg"], writes=["yssm"])
                    for n in range(8):
                        b = n % 2
                        for k in range(4):
                            P.op("pe", lambda e: e.matmul(pA[b][:], lhsT=pssm_sb[:, k, n * 128:(n + 1) * 128], rhs=yssm[:, k, :], start=(k == 0), stop=(k == 3)),
                                 reads=["pssm", "yssm"], writes=[("pA", b)])
                        for k in range(8):
                            P.op("pe", lambda e: e.matmul(pB[b][:], lhsT=pattn_sb[:, k, n * 128:(n + 1) * 128], rhs=ya_sb[:, k, :], start=(k == 0), stop=(k == 7)),
                                 reads=["pattn", YA], writes=[("pB", b)])
                        P.op("dve", lambda e: e.tensor_tensor(out=tA[:], in0=pA[b][:], in1=gs_sb[:, n, :], op=ALU.mult), reads=[("pA", b), GS], writes=["tA"])
                        P.op("dve", lambda e: e.tensor_tensor(out=tB[:], in0=pB[b][:], in1=ga_sb[:, n, :], op=ALU.mult), reads=[("pB", b), GA], writes=["tB"])
                        P.op("pool", lambda e: e.tensor_tensor(out=merged[:, n, :], in0=tA[:], in1=tB[:], op=ALU.add), reads=["tA", "tB"], writes=["merged"])
                    for s in range(4):
                        for hf in range(2):
                            for k in range(8):
                                P.op("pe", lambda e: e.matmul(phh[:, hf * 512:(hf + 1) * 512], lhsT=merged[:, k, s * 128:(s + 1) * 128],
                                                              rhs=wout_sb[:, k, hf * 512:(hf + 1) * 512], start=(k == 0), stop=(k == 7)),
                                     reads=["merged", "wout"], writes=["phh"])
                        for hf in range(2):
                            P.op("dve", lambda e: e.scalar_tensor_tensor(out=vt[:, hf * 512:(hf + 1) * 512], in0=xs[:, s, hf * 512:(hf + 1) * 512], scalar=ALPHA,
                                                                         in1=phh[:, hf * 512:(hf + 1) * 512], op0=ALU.mult, op1=ALU.add),
                                 reads=[XT, "phh"], writes=["vt"])
                        layer_norm_tile(vt, xs[:, s, :], lnb, stats, mv, rstd, XT)
                    P.dma("pool", x1d[tsl, :].rearrange("(s p) d -> p s d", p=128), xs[:], reads=[XT], writes=[("x1d", T)])
                P.barrier()

        def phase_e(l, xdst):
            NT2 = L // 512
            HM = 11
            for hp in range(2):
              with ExitStack() as ph:
                wgu_sb = SB(ph, "wgu", [128, 8, 2 * HM * 128], BF16)
                wdn_sb = SB(ph, "wdn", [128, HM, D], BF16)
                wstg = [SB(ph, "wstg%d" % i, [128, D], F32) for i in range(2)]
                xsb = [SB(ph, "xs%d" % i_, [128, 4, D], F32) for i_ in range(2)]
                fpb = [SB(ph, "fp%d" % i_, [128, 4, D], F32) for i_ in range(2)]
                u2T = SB(ph, "u2T", [128, 8, 512], BF16)
                sa = SB(ph, "sa", [128, 512], F32)
                hT = SB(ph, "hT", [128, HM, 512], BF16)
                vt = SB(ph, "vt", [128, D], F32)
                stats = SB(ph, "stats", [128, 2, 6], F32)
                mv = SB(ph, "mv", [128, 2], F32)
                rstd = SB(ph, "rstd", [128, 1], F32)
                ptp = [PSU(ph, "ptp%d" % i, [128, 512]) for i in range(2)]
                pa = [PSU(ph, "pa%d" % i, [128, 512]) for i in range(2)]
                pb = [PSU(ph, "pb%d" % i, [128, 512]) for i in range(2)]
                pf = PSU(ph, "pf", [128, D])
                W = HM * 128
                lnb = SB(ph, "lnb", [128, 2, D], F32)
                for i_ in range(2):
                    P.dma("sp", lnb[:, i_, :], lnp[l, 2 + i_, :].partition_broadcast(128), writes=["lnb"])
                for k in range(8):
                    P.dma("pool", wgu_sb[:, k, 0:W], w_gu[l, k * 128:(k + 1) * 128, hp * W:(hp + 1) * W], writes=["wgu"])
                    P.dma("pool", wgu_sb[:, k, W:2 * W], w_gu[l, k * 128:(k + 1) * 128, DFF + hp * W:DFF + (hp + 1) * W], writes=["wgu"])
                for k in range(HM):
                    b = k % 2
                    r0 = (hp * HM + k) * 128
                    P.dma("sp", wstg[b][:], w_dn[l, r0:r0 + 128, :], writes=[("wstg", b)])
                    P.op("dve", lambda e: e.tensor_tensor(out=wdn_sb[:, k, :], in0=wstg[b][:], in1=G12[:, 1, :], op=ALU.mult),
                         reads=[("wstg", b), "G12"], writes=["wdn"])
                for T in range(NT2):
                    tsl = slice(T * 512, (T + 1) * 512)
                    xs, fp_sb = xsb[T % 2], fpb[T % 2]
                    XT, FT = ("xs", T % 2), ("fp", T % 2)
                    P.dma("sp", xs[:], x1d[tsl, :].rearrange("(s p) d -> p s d", p=128), reads=[("x1d", T)], writes=[XT])
                    if hp == 1:
                        P.dma("sp", fp_sb[:], fpart[tsl, :].rearrange("(s p) d -> p s d", p=128), reads=[("fpart", T)], writes=[FT])
                    for k in range(8):
                        b = k % 2
                        for s in range(4):
                            P.op("pe", lambda e: e.transpose(ptp[b][:, s * 128:(s + 1) * 128], xs[:, s, k * 128:(k + 1) * 128], ident[:]),
                                 reads=[XT, "ident"], writes=[("ptp", b)])
                        P.op("act", lambda e: e.activation(out=u2T[:, k, :], in_=ptp[b][:], func=AF.Identity,
                                                           scale=modT[:, 32 + k:33 + k], bias=modT[:, 24 + k:25 + k]),
                             reads=[("ptp", b), "modT"], writes=["u2T"])
                    for m in range(HM):
                        b = m % 2
                        for k in range(8):
                            P.op("pe", lambda e: e.matmul(pa[b][:], lhsT=wgu_sb[:, k, m * 128:(m + 1) * 128], rhs=u2T[:, k, :], start=(k == 0), stop=(k == 7)),
                                 reads=["wgu", "u2T"], writes=[("pa", b)])
                        for k in range(8):
                            P.op("pe", lambda e: e.matmul(pb[b][:], lhsT=wgu_sb[:, k, W + m * 128:W + (m + 1) * 128], rhs=u2T[:, k, :], start=(k == 0), stop=(k == 7)),
                                 reads=["wgu", "u2T"], writes=[("pb", b)])
                        P.op("act", lambda e: e.activation(out=sa[:], in_=pa[b][:], func=AF.Silu), reads=[("pa", b)], writes=["sa"])
                        P.op("dve", lambda e: e.tensor_tensor(out=hT[:, m, :], in0=sa[:], in1=pb[b][:], op=ALU.mult), reads=["sa", ("pb", b)], writes=["hT"])
                    for s in range(4):
                        for hf in range(2):
                            for k in range(HM):
                                P.op("pe", lambda e: e.matmul(pf[:, hf * 512:(hf + 1) * 512], lhsT=hT[:, k, s * 128:(s + 1) * 128],
                                                              rhs=wdn_sb[:, k, hf * 512:(hf + 1) * 512], start=(k == 0), stop=(k == HM - 1)),
                                     reads=["hT", "wdn"], writes=["pf"])
                        if hp == 0:
                            for hf in range(2):
                                P.op("dve", lambda e: e.tensor_copy(out=fp_sb[:, s, hf * 512:(hf + 1) * 512], in_=pf[:, hf * 512:(hf + 1) * 512]),
                                     reads=["pf"], writes=[FT])
                        else:
                            for hf in range(2):
                                P.op("dve", lambda e: e.tensor_tensor(out=vt[:, hf * 512:(hf + 1) * 512], in0=fp_sb[:, s, hf * 512:(hf + 1) * 512],
                                                                      in1=pf[:, hf * 512:(hf + 1) * 512], op=ALU.add), reads=[FT, "pf"], writes=["vt"])
                            P.op("dve", lambda e: e.scalar_tensor_tensor(out=vt[:], in0=xs[:, s, :], scalar=ALPHA, in1=vt[:], op0=ALU.mult, op1=ALU.add),
                                 reads=[XT, "vt"], writes=["vt"])
                            layer_norm_tile(vt, xs[:, s, :], lnb, stats, mv, rstd, XT)
                    if hp == 0:
                        P.dma("pool", fpart[tsl, :].rearrange("(s p) d -> p s d", p=128), fp_sb[:], reads=[FT], writes=[("fpart", T)])
                    else:
                        P.dma("pool", xdst[tsl, :].rearrange("(s p) d -> p s d", p=128), xs[:], reads=[XT], writes=[("xdst", T)])
                P.barrier()


        def phase_dsa(l):
            SC = 1.0 / math.sqrt(128.0)
            with ExitStack() as ph:
                kT_sb = SB(ph, "kTsb", [128, 2, L], BF16)
                v_sb = SB(ph, "vsb", [128, NT, 2, 132], BF16)
                ki2 = SB(ph, "ki2", [128, L], BF16)
                score_b = [SB(ph, "score%d" % i, [128, L], BF16) for i in range(2)]
                mbs = [SB(ph, "mb%d" % i, [128, L], BF16) for i in range(2)]
                junk = SB(ph, "junk", [128, L], mybir.dt.uint8)
                R = SB(ph, "R", [128, 8, 512], BF16)
                qT_b = [SB(ph, "qTi%d" % i, [128, 8, 128], BF16) for i in range(3)]
                qiT_b = [SB(ph, "qiTi%d" % i, [128, 8, 128], BF16) for i in range(2)]
                w_b = [SB(ph, "wi_%d" % i, [128, 8], F32) for i in range(2)]
                diag_b = [SB(ph, "diag%d" % i, [128, 8, 128], BF16) for i in range(2)]
                PT = [SB(ph, "PT%d" % i, [128, 512], BF16) for i in range(3)]
                o_b = [SB(ph, "osb%d" % i, [128, D], BF16) for i in range(2)]
                oT_b = [SB(ph, "oTsb%d" % i, [128, 8, 128], BF16) for i in range(2)]
                sm = SB(ph, "sm", [128, 16], F32)
                nrm = SB(ph, "nrm", [128, 8], F32)
                tneg = SB(ph, "tneg", [128, 1], F32)
                psx = [PSU(ph, "psx%d" % i, [128, 512]) for i in range(2)]
                pss = PSU(ph, "pss", [128, 512])
                psST = [PSU(ph, "psST%d" % i, [128, 512]) for i in range(2)]
                psO = [PSU(ph, "psO%d" % i, [128, 512]) for i in range(2)]
                psT = PSU(ph, "psT", [128, 8, 128], BF16)
                P.dma("sp", kT_sb[:], kT.rearrange("(g p) t -> p g t", p=128), writes=["kTsb"])
                for g in range(2):
                    P.dma("sp", v_sb[:, :, g, 0:128], vd[:, g * 128:(g + 1) * 128].rearrange("(j p) d -> p j d", p=128), writes=["vsb"])
                P.op("pool", lambda e: e.memset(v_sb[:, :, :, 128:129], 1.0), writes=["vsb"])
                P.dma("sp", ki2[0:64, :], kiT, writes=["ki2"])
                P.dma("sp", ki2[64:128, :], kiT, writes=["ki2"])
                P.op("pool", lambda e: e.memset(tneg[:], -1.0e29), writes=["tneg"])
                for pb_ in range(2):
                    P.op("pool", lambda e: e.memset(qiT_b[pb_][:], 0.0), writes=[("qiTi", pb_)])
                cnts = {"pt": 0, "st": 0}

                def emit_load(i):
                    pb = i % 2
                    qs = slice(i * 128, (i + 1) * 128)
                    qiT_i, w_i, diag = qiT_b[pb], w_b[pb], diag_b[pb]
                    qsrc = qiT[:, qs].rearrange("(m two d) t -> two d m t", two=2, d=64)
                    qdst = qiT_i[:].rearrange("p (m two) t -> p m two t", two=2)
                    P.dma("sp", qdst[0:64, :, 0, :], qsrc[0], writes=[("qiTi", pb)])
                    P.dma("sp", qdst[64:128, :, 1, :], qsrc[1], writes=[("qiTi", pb)])
                    P.dma("sp", w_i[:], wid[qs, :], writes=[("wi_", pb)])
                    P.dma("sp", qT_b[i % 3][:], qT[:, qs].rearrange("(h p) t -> p h t", p=128), writes=[("qTi", i % 3)])
                    for h in range(8):
                        P.op("pool", lambda e: e.tensor_scalar(out=diag[:, h, :], in0=identb[:], scalar1=w_i[:, h:h + 1], scalar2=None, op0=ALU.mult),
                             reads=["identb", ("wi_", pb)], writes=[("diag", pb)])

                def emit_idx(i):
                    pb = i % 2
                    n = 128 * (i + 1)
                    nch = (n + 511) // 512
                    qiT_i, diag = qiT_b[pb], diag_b[pb]
                    score, SCT = score_b[pb], ("score", pb)
                    for c in range(nch):
                        wc = min(512, n - c * 512)
                        ks = slice(c * 512, c * 512 + wc)

                        xring = [(psx[0], ("psx", 0)), (psx[1], ("psx", 1)), (psST[0], ("psST", 0)), (psST[1], ("psST", 1))]

                        def xmm(h):
                            px_, tk_ = xring[h % 4]
                            P.op("pe", lambda e: e.matmul(px_[:, 0:wc], lhsT=qiT_i[:, h, :], rhs=ki2[:, ks],
                                                          start=True, stop=True), reads=[("qiTi", pb), "ki2"], writes=[tk_])
                        for h in range(4):
                            xmm(h)
                        for h in range(8):
                            px_, tk_ = xring[h % 4]
                            P.op("act", lambda e: e.activation(out=R[:, h, 0:wc], in_=px_[:, 0:wc], func=AF.Relu),
                                 reads=[tk_], writes=[("R", h)])
                            P.op("pe", lambda e: e.matmul(pss[:, 0:wc], lhsT=diag[:, h, :], rhs=R[:, h, 0:wc], start=(h == 0), stop=(h == 7)),
                                 reads=[("diag", pb), ("R", h)], writes=["pss"])
                            if h + 4 < 8:
                                xmm(h + 4)
                        last = (c == nch - 1)
                        wcopy = wc - 128 if last else wc
                        if wcopy > 0:
                            P.op("act", lambda e: e.activation(out=score[:, c * 512:c * 512 + wcopy], in_=pss[:, 0:wcopy], func=AF.Copy),
                                 reads=["pss"], writes=[SCT])
                        if last:
                            P.op("dve", lambda e: e.tensor_tensor(out=score[:, n - 128:n], in0=pss[:, wc - 128:wc], in1=causal[:], op=ALU.add),
                                 reads=["pss", "causal"], writes=[SCT])

                def emit_thr(i):
                    pb = i % 2
                    n = 128 * (i + 1)
                    mb = mbs[pb]
                    score, SCT = score_b[pb], ("score", pb)
                    if i >= 2:
                        P.op("dve", lambda e: e.tensor_reduce(out=sm[:, 0:1], in_=score[:, 0:n], axis=AX.X, op=ALU.max), reads=[SCT], writes=["sm"])
                        P.op("dve", lambda e: e.tensor_reduce(out=sm[:, 1:2], in_=score[:, 0:n - 128], axis=AX.X, op=ALU.min), reads=[SCT], writes=["sm"])
                        P.op("dve", lambda e: e.tensor_scalar(out=sm[:, 2:3], in0=sm[:, 1:2], scalar1=-1.0, scalar2=None, op0=ALU.add), reads=["sm"], writes=["sm"])
                        P.op("dve", lambda e: e.tensor_tensor(out=sm[:, 3:4], in0=sm[:, 0:1], in1=sm[:, 2:3], op=ALU.subtract), reads=["sm"], writes=["sm"])
                        for k in range(NBIS):
                            ck = 2.0 ** (-(k + 1))
                            P.op("dve", lambda e: e.scalar_tensor_tensor(out=sm[:, 4:5], in0=sm[:, 3:4], scalar=ck, in1=sm[:, 2:3], op0=ALU.mult, op1=ALU.add),
                                 reads=["sm"], writes=["sm"])
                            P.op("dve", lambda e: e.tensor_scalar(out=junk[:, 0:n], in0=score[:, 0:n], scalar1=sm[:, 4:5], scalar2=0.0, op0=ALU.is_gt, op1=ALU.add,
                                                                  accum_out=sm[:, 5:6]), reads=[SCT, "sm"], writes=["sm", "junk"])
                            P.op("dve", lambda e: e.tensor_scalar(out=sm[:, 6:7], in0=sm[:, 5:6], scalar1=255.5, scalar2=sm[:, 3:4], op0=ALU.is_ge, op1=ALU.mult),
                                 reads=["sm"], writes=["sm"])
                            P.op("dve", lambda e: e.scalar_tensor_tensor(out=sm[:, 2:3], in0=sm[:, 6:7], scalar=ck, in1=sm[:, 2:3], op0=ALU.mult, op1=ALU.add),
                                 reads=["sm"], writes=["sm"])
                        thr = sm[:, 2:3]
                    else:
                        thr = tneg[:, 0:1]
                    P.op("dve", lambda e: e.tensor_scalar(out=mb[:, 0:n], in0=score[:, 0:n], scalar1=thr, scalar2=-30000.0, op0=ALU.is_le, op1=ALU.mult),
                         reads=[SCT, "sm", "tneg"], writes=[("mb", pb)])

                def emit_att(i):
                    pb = i % 2
                    qs = slice(i * 128, (i + 1) * 128)
                    qT_i, mb, o_sb, oT_sb = qT_b[i % 3], mbs[pb], o_b[pb], oT_b[pb]
                    ring = [(psST[0], ("psST", 0)), (psST[1], ("psST", 1)), (psx[0], ("psx", 0))]
                    for g in range(2):
                        def qk(j):
                            pst_, tk_ = ring[j % 3]
                            P.op("pe", lambda e: e.matmul(pst_[:], lhsT=kT_sb[:, g, j * 128:(j + 1) * 128], rhs=qT_i[:, 4 * g:4 * g + 4, :],
                                                          start=True, stop=False), reads=["kTsb", ("qTi", i % 3)], writes=[tk_])
                            P.op("pe", lambda e: e.matmul(pst_[:], lhsT=mb[:, j * 128:(j + 1) * 128], rhs=ident4b[:],
                                                          start=False, stop=True), reads=[("mb", pb), "ident4b"], writes=[tk_])
                        qk(0)
                        if i >= 1:
                            qk(1)
                        for j in range(i + 1):
                            pst_, tk_ = ring[j % 3]
                            r3 = cnts["pt"] % 3
                            cnts["pt"] += 1
                            P.op("act", lambda e: e.activation(out=PT[r3][:], in_=pst_[:], func=AF.Exp, scale=SC),
                                 reads=[tk_], writes=[("PT", r3)])
                            if j + 2 <= i:
                                qk(j + 2)
                            for hh in range(4):
                                off = (hh % 2) * 256
                                P.op("pe", lambda e: e.matmul(psO[hh // 2][:, off:off + 129], lhsT=PT[r3][:, hh * 128:(hh + 1) * 128], rhs=v_sb[:, j, g, 0:129],
                                                              start=(j == 0 and hh % 2 == 0), stop=(j == i and hh % 2 == 1), skip_group_check=True),
                                     reads=[("PT", r3), "vsb"], writes=[("psO", hh // 2)])
                        for hh in range(4):
                            off = (hh % 2) * 256
                            P.op("act", lambda e: e.activation(out=nrm[:, hh:hh + 1], in_=psO[hh // 2][:, off + 128:off + 129], func=AF.Ln),
                                 reads=[("psO", hh // 2)], writes=["nrm"])
                            P.op("act", lambda e: e.activation(out=nrm[:, 4 + hh:5 + hh], in_=nrm[:, hh:hh + 1], func=AF.Exp, scale=-1.0),
                                 reads=["nrm"], writes=["nrm"])
                            P.op("act", lambda e: e.activation(out=o_sb[:, (4 * g + hh) * 128:(4 * g + hh + 1) * 128], in_=psO[hh // 2][:, off:off + 128],
                                                               func=AF.Identity, scale=nrm[:, 4 + hh:5 + hh]),
                                 reads=[("psO", hh // 2), "nrm"], writes=[("osb", pb)])
                    for h in range(8):
                        P.op("pe", lambda e: e.transpose(psT[:, h, :], o_sb[:, h * 128:(h + 1) * 128], identb[:]), reads=[("osb", pb), "identb"], writes=["psT"])
                    P.op("act", lambda e: e.activation(out=oT_sb[:], in_=psT[:], func=AF.Copy), reads=["psT"], writes=[("oTsb", pb)])
                    P.dma("pool", yaT[:, qs].rearrange("(h p) t -> p h t", p=128), oT_sb[:], reads=[("oTsb", pb)], writes=[("yaT", i)])

                emit_load(0)
                for i in range(NT + 1):
                    if i + 1 < NT:
                        emit_load(i + 1)
                    if i < NT:
                        emit_idx(i)
                        emit_thr(i)
                    if i >= 1:
                        emit_att(i - 1)
                P.barrier()

        def phase_ssm(l):
            TK = "ssmprep"

            def TT(o, a, b, op, eng="dve"):
                P.op(eng, lambda e: e.tensor_tensor(out=o, in0=a, in1=b, op=op), reads=[TK], writes=[TK])

            def TS(o, a, s1, op0, s2=None, op1=None):
                if op1 is None:
                    P.op("dve", lambda e: e.tensor_scalar(out=o, in0=a, scalar1=s1, scalar2=None, op0=op0), reads=[TK], writes=[TK])
                else:
                    P.op("dve", lambda e: e.tensor_scalar(out=o, in0=a, scalar1=s1, scalar2=s2, op0=op0, op1=op1), reads=[TK], writes=[TK])

            def ACT(o, a, func, scale=1.0):
                P.op("act", lambda e: e.activation(out=o, in_=a, func=func, scale=scale), reads=[TK], writes=[TK])

            def zoh(st, tmp, nm, lam_re_d, lam_im_d, ldt_d, F):
                pw_re = SB(st, nm + "pwre", [128, 9, F], F32)
                pw_im = SB(st, nm + "pwim", [128, 9, F], F32)
                fr = SB(st, nm + "fr", [128, F], F32)
                fi = SB(st, nm + "fi", [128, F], F32)
                t = [SB(tmp, nm + "t%d" % i, [128, F], F32) for i in range(10)]
                lr, li, dt, mag, er, ei, a, b, c_, d_ = t
                P.dma("sp", lr[:], lam_re_d, writes=[TK])
                P.dma("sp", li[:], lam_im_d, writes=[TK])
                P.dma("sp", dt[:], ldt_d, writes=[TK])
                ACT(dt[:], dt[:], AF.Exp)
                TT(a[:], lr[:], dt[:], ALU.mult)
                ACT(mag[:], a[:], AF.Exp)
                TT(a[:], li[:], dt[:], ALU.mult)
                ACT(ei[:], a[:], AF.Sin, scale=1.0 / 16.0)
                ACT(b[:], a[:], AF.Sin, scale=1.0 / 32.0)
                TT(b[:], b[:], b[:], ALU.mult)
                TS(er[:], b[:], -2.0, ALU.mult, 1.0, ALU.add)
                for _ in range(4):
                    TT(a[:], er[:], er[:], ALU.mult)
                    TT(b[:], ei[:], ei[:], ALU.mult)
                    TT(c_[:], er[:], ei[:], ALU.mult)
                    TT(er[:], a[:], b[:], ALU.subtract)
                    TS(ei[:], c_[:], 2.0, ALU.mult)
                ar, ai = pw_re[:, 1, :], pw_im[:, 1, :]
                TT(ar, mag[:], er[:], ALU.mult)
                TT(ai, mag[:], ei[:], ALU.mult)
                P.op("dve", lambda e: e.memset(pw_re[:, 0, :], 1.0), reads=[TK], writes=[TK])
                P.op("dve", lambda e: e.memset(pw_im[:, 0, :], 0.0), reads=[TK], writes=[TK])
                TT(a[:], lr[:], lr[:], ALU.mult)
                TT(b[:], li[:], li[:], ALU.mult)
                TT(a[:], a[:], b[:], ALU.add)
                P.op("dve", lambda e: e.reciprocal(out=a[:], in_=a[:]), reads=[TK], writes=[TK])
                TS(b[:], ar, -1.0, ALU.add)
                TT(c_[:], b[:], lr[:], ALU.mult)
                TT(d_[:], ai, li[:], ALU.mult)
                TT(c_[:], c_[:], d_[:], ALU.add)
                TT(fr[:], c_[:], a[:], ALU.mult)
                TT(c_[:], ai, lr[:], ALU.mult)
                TT(d_[:], b[:], li[:], ALU.mult)
                TT(c_[:], c_[:], d_[:], ALU.subtract)
                TT(fi[:], c_[:], a[:], ALU.mult)
                for k in range(2, 9):
                    cmul(pw_re[:, k, :], pw_im[:, k, :], pw_re[:, k - 1, :], pw_im[:, k - 1, :], ar, ai, a[:], b[:])
                return pw_re, pw_im, fr, fi

            def cmul(o_re, o_im, x_re, x_im, y_re, y_im, t1, t2):
                TT(t1, x_re, y_re, ALU.mult)
                TT(t2, x_im, y_im, ALU.mult)
                TT(o_re, t1, t2, ALU.subtract)
                TT(t1, x_re, y_im, ALU.mult)
                TT(t2, x_im, y_re, ALU.mult)
                TT(o_im, t1, t2, ALU.add)

            with ExitStack() as ph:
                B8 = SB(ph, "B8", [128, 4, 8, 2, 64], F32)
                Wr = SB(ph, "Wr", [128, 9, 16, 16], F32)
                Wi = SB(ph, "Wi", [128, 9, 16, 16], F32)
                BbS_re = SB(ph, "BbSre", [128, 16, 16], F32)
                BbS_im = SB(ph, "BbSim", [128, 16, 16], F32)
                lev_re = SB(ph, "levre", [128, NLEV, 16], F32)
                lev_im = SB(ph, "levim", [128, NLEV, 16], F32)
                lev_nim = SB(ph, "levnim", [128, NLEV, 16], F32)
                dcol_sb = SB(ph, "dcol", [128, DEPTH * 4], F32)
                P.dma("sp", dcol_sb[:], dcol, writes=[TK])
                with ExitStack() as tmp:
                    pwR_re, pwR_im, frR, fiR = zoh(tmp, tmp, "R", lamR_re[l], lamR_im[l], ldtR[l], 256)
                    bre = SB(tmp, "bre", [128, 256], F32)
                    bim = SB(tmp, "bim", [128, 256], F32)
                    Bb_re = SB(tmp, "Bbre", [128, 256], F32)
                    Bb_im = SB(tmp, "Bbim", [128, 256], F32)
                    u1 = SB(tmp, "u1", [128, 256], F32)
                    u2 = SB(tmp, "u2", [128, 256], F32)
                    P.dma("sp", bre[:], bR_re[l], writes=[TK])
                    P.dma("sp", bim[:], bR_im[l], writes=[TK])
                    cmul(Bb_re[:], Bb_im[:], frR[:], fiR[:], bre[:], bim[:], u1[:], u2[:])
                    v4 = lambda ap: ap.rearrange("p (f q) -> p f q", q=64)
                    for s_ in range(8):
                        k = 7 - s_
                        cmul(B8[:, :, s_, 0, :], B8[:, :, s_, 1, :], v4(pwR_re[:, k, :]), v4(pwR_im[:, k, :]), v4(Bb_re[:]), v4(Bb_im[:]), v4(u1[:]), v4(u2[:]))
                    pwS_re, pwS_im, frS, fiS = zoh(tmp, tmp, "S", lamS_re[l], lamS_im[l], ldtS[l], 16)
                    cre = SB(tmp, "cre", [128, 16, 16], F32)
                    cim = SB(tmp, "cim", [128, 16, 16], F32)
                    bsr = SB(tmp, "bsr", [128, 16, 16], F32)
                    bsi = SB(tmp, "bsi", [128, 16, 16], F32)
                    w1 = SB(tmp, "w1", [128, 16, 16], F32)
                    w2 = SB(tmp, "w2", [128, 16, 16], F32)
                    P.dma("sp", cre[:], cS_re[l].rearrange("p (q i) -> p q i", i=16), writes=[TK])
                    P.dma("sp", cim[:], cS_im[l].rearrange("p (q i) -> p q i", i=16), writes=[TK])
                    P.dma("sp", bsr[:], bS_re[l].rearrange("p (q i) -> p q i", i=16), writes=[TK])
                    P.dma("sp", bsi[:], bS_im[l].rearrange("p (q i) -> p q i", i=16), writes=[TK])
                    bc = lambda ap: ap.unsqueeze(2).to_broadcast([128, 16, 16])
                    cmul(BbS_re[:], BbS_im[:], bc(frS[:]), bc(fiS[:]), bsr[:], bsi[:], w1[:], w2[:])
                    for k in range(9):
                        er_k, ei_k = bc(pwS_re[:, k, :]), bc(pwS_im[:, k, :])
                        TT(w1[:], cre[:], er_k, ALU.mult)
                        TT(w2[:], cim[:], ei_k, ALU.mult)
                        TT(Wr[:, k, :, :], w1[:], w2[:], ALU.subtract)
                        TT(w1[:], cre[:], ei_k, ALU.mult)
                        TT(w2[:], cim[:], er_k, ALU.mult)
                        TT(w1[:], w1[:], w2[:], ALU.add)
                        TS(Wi[:, k, :, :], w1[:], -1.0, ALU.mult)
                    P.op("dve", lambda e: e.tensor_copy(out=lev_re[:, 0, :], in_=pwS_re[:, 8, :]), reads=[TK], writes=[TK])
                    P.op("dve", lambda e: e.tensor_copy(out=lev_im[:, 0, :], in_=pwS_im[:, 8, :]), reads=[TK], writes=[TK])
                    for d_ in range(1, NLEV):
                        cmul(lev_re[:, d_, :], lev_im[:, d_, :], lev_re[:, d_ - 1, :], lev_im[:, d_ - 1, :], lev_re[:, d_ - 1, :], lev_im[:, d_ - 1, :],
                             w1[:, 0, :], w2[:, 0, :])
                    TS(lev_nim[:], lev_im[:], -1.0, ALU.mult)
                    P.barrier()
                if ssm_stop < 2:
                    return
                uT_ft = SB(ph, "uTft", [128, L], BF16)
                ys_t = SB(ph, "yst", [128, L], BF16)
                uT_de = SB(ph, "uTde", [128, 8, NC8], BF16)
                B8pad = SB(ph, "B8pad", [128, 4, 8, 2, 2, 64], BF16)
                Cpad = SB(ph, "Cpad", [128, 4, 2, 9, 128], BF16)
                BbSpad = SB(ph, "BbSpad", [128, 4, 2, 128], BF16)
                BD_sb = SB(ph, "BDsb", [128, 8, 128], BF16)
                sc_ = [[SB(ph, "scan%d%d" % (a_, b_), [128, NC8], F32) for b_ in range(2)] for a_ in range(2)]
                Hb = [[SB(ph, "Hb%d%d" % (q_, r_), [128, NC8], BF16) for r_ in range(2)] for q_ in range(4)]
                g1 = SB(ph, "g1", [128, CW], F32)
                g2_ = SB(ph, "g2", [128, CW], F32)
                psBD = PSU(ph, "psBD", [128, 1024])
                psX = [PSU(ph, "psX%d" % i, [128, 512]) for i in range(2)]
                psY = [PSU(ph, "psY%d" % i, [128, 512]) for i in range(2)]
                P.op("pool", lambda e: e.memset(Cpad[:], 0.0), writes=["Cpad"])
                P.op("pool", lambda e: e.memset(BbSpad[:], 0.0), writes=["BbSpad"])
                uTv = uT_ft[:].rearrange("p (c s) -> p c s", s=8)
                ysv = ys_t[:].rearrange("p (c s) -> p c s", s=8)
                xc = 0
                yc = 0
                for ft in range(4):
                    P.dma("sp", uT_ft[:], uTs[ft * 128:(ft + 1) * 128, :], writes=["uTft"])
                    for s_ in range(8):
                        if s_ % 2 == 0:
                            P.op("act", lambda e: e.activation(out=uT_de[:, s_, :], in_=uTv[:, :, s_], func=AF.Copy), reads=["uTft"], writes=["uTde"])
                        else:
                            P.op("pool", lambda e: e.tensor_copy(out=uT_de[:, s_, :], in_=uTv[:, :, s_]), reads=["uTft"], writes=["uTde"])
                    for gl in range(8):
                        P.op("dve", lambda e: e.tensor_scalar(out=B8pad[:, gl // 2, :, :, gl % 2, :], in0=B8[:, ft],
                                                              scalar1=rowmask[:, gl:gl + 1], scalar2=None, op0=ALU.mult),
                             reads=[TK, "rowmask"], writes=["B8pad"])
                    for ql in range(4):
                        qq = ft * 4 + ql
                        for g2 in range(2):
                            rows = slice(g2 * 64, (g2 + 1) * 64)
                            cs = slice((2 * ql + g2) * 16, (2 * ql + g2) * 16 + 16)
                            for ri, Wsrc, Bsrc in ((0, Wr, BbS_re), (1, Wi, BbS_im)):
                                P.op("pool", lambda e: e.tensor_copy(out=Cpad[rows, ql, ri, :, cs], in_=Wsrc[rows, :, qq, :]), reads=[TK], writes=["Cpad"])
                                P.op("pool", lambda e: e.tensor_copy(out=BbSpad[rows, ql, ri, cs], in_=Bsrc[rows, qq, :]), reads=[TK], writes=["BbSpad"])
                    if ssm_stop < 3:
                        continue
                    for hf in range(2):
                        n_ = 0
                        for ql in range(4):
                            for ri in range(2):
                                P.op("pe", lambda e: e.matmul(psBD[:, hf * 512:(hf + 1) * 512], lhsT=BbSpad[:, ql, ri, :], rhs=Cpad[:, ql, ri, 4 * hf:4 * hf + 4, :],
                                                              start=(n_ == 0), stop=(n_ == 7)), reads=["BbSpad", "Cpad"], writes=["psBD"])
                                n_ += 1
                    P.op("dve", lambda e: e.scalar_tensor_tensor(out=BD_sb[:, 0, :], in0=ident[:], scalar=dcol_sb[:, l * 4 + ft:l * 4 + ft + 1], in1=psBD[:, 0:128],
                                                                 op0=ALU.mult, op1=ALU.add), reads=["psBD", "ident", TK], writes=["BDsb"])
                    P.op("act", lambda e: e.activation(out=BD_sb[:, 1:8, :], in_=psBD[:, 128:1024].rearrange("p (t q) -> p t q", q=128), func=AF.Copy),
                         reads=["psBD"], writes=["BDsb"])
                    if ssm_stop < 4:
                        continue
                    for ql in range(4):
                        qq = ft * 4 + ql
                        for ri in range(2):
                            for cb in range(NCH):
                                b = xc % 2
                                xc += 1
                                for s_ in range(8):
                                    P.op("pe", lambda e: e.matmul(psX[b][:, 0:CW], lhsT=B8pad[:, ql, s_, ri, :, :], rhs=uT_de[:, s_, cb * CW:(cb + 1) * CW],
                                                                  start=(s_ == 0), stop=(s_ == 7)), reads=["B8pad", "uTde"], writes=[("psX", b)])
                                P.op("act", lambda e: e.activation(out=sc_[0][ri][:, cb * CW:(cb + 1) * CW], in_=psX[b][:, 0:CW], func=AF.Copy),
                                     reads=[("psX", b)], writes=[("scan", 0)])
                        cur = 0
                        for d_ in range(NLEV):
                            sh = 1 << d_
                            src, dst = sc_[cur], sc_[1 - cur]
                            lre = lev_re[:, d_, qq:qq + 1]
                            lim = lev_im[:, d_, qq:qq + 1]
                            lnim = lev_nim[:, d_, qq:qq + 1]
                            tk_s, tk_d = ("scan", cur), ("scan", 1 - cur)
                            m_ = NC8 - sh
                            P.op("dve", lambda e: e.scalar_tensor_tensor(out=dst[0][:, sh:], in0=src[0][:, 0:m_], scalar=lre, in1=src[0][:, sh:], op0=ALU.mult, op1=ALU.add),
                                 reads=[tk_s, TK], writes=[tk_d])
                            P.op("dve", lambda e: e.scalar_tensor_tensor(out=dst[0][:, sh:], in0=src[1][:, 0:m_], scalar=lnim, in1=dst[0][:, sh:], op0=ALU.mult, op1=ALU.add),
                                 reads=[tk_s, TK], writes=[tk_d])
                            P.op("dve", lambda e: e.scalar_tensor_tensor(out=dst[1][:, sh:], in0=src[1][:, 0:m_], scalar=lre, in1=src[1][:, sh:], op0=ALU.mult, op1=ALU.add),
                                 reads=[tk_s, TK], writes=[tk_d])
                            P.op("dve", lambda e: e.scalar_tensor_tensor(out=dst[1][:, sh:], in0=src[0][:, 0:m_], scalar=lim, in1=dst[1][:, sh:], op0=ALU.mult, op1=ALU.add),
                                 reads=[tk_s, TK], writes=[tk_d])
                            for ri in range(2):
                                P.op("pool", lambda e: e.tensor_copy(out=dst[ri][:, 0:sh], in_=src[ri][:, 0:sh]), reads=[tk_s], writes=[tk_d])
                            cur = 1 - cur
                        for ri in range(2):
                            P.op("pool", lambda e: e.memset(Hb[ql][ri][:, 0:1], 0.0), writes=[("Hb", ql)])
                            P.op("act", lambda e: e.activation(out=Hb[ql][ri][:, 1:NC8], in_=sc_[cur][ri][:, 0:NC8 - 1], func=AF.Copy),
                                 reads=[("scan", cur)], writes=[("Hb", ql)])
                    if ssm_stop < 5:
                        continue
                    for t_ in range(8):
                        for cb in range(NCH):
                            b = yc % 2
                            yc += 1
                            cs = slice(cb * CW, (cb + 1) * CW)
                            nmm = (t_ + 1) + 8
                            n_ = 0
                            for tau in range(t_ + 1):
                                P.op("pe", lambda e: e.matmul(psY[b][:, 0:CW], lhsT=BD_sb[:, tau, :], rhs=uT_de[:, t_ - tau, cs], start=(n_ == 0), stop=(n_ == nmm - 1)),
                                     reads=["BDsb", "uTde"], writes=[("psY", b)])
                                n_ += 1
                            for ql in range(4):
                                for ri in range(2):
                                    P.op("pe", lambda e: e.matmul(psY[b][:, 0:CW], lhsT=Cpad[:, ql, ri, t_ + 1, :], rhs=Hb[ql][ri][:, cs], start=(n_ == 0), stop=(n_ == nmm - 1)),
                                         reads=["Cpad", ("Hb", ql)], writes=[("psY", b)])
                                    n_ += 1
                            y_ = psY[b][:, 0:CW]
                            P.op("act", lambda e: e.activation(out=g1[:], in_=y_, func=AF.Square), reads=[("psY", b)], writes=["g1"])
                            P.op("dve", lambda e: e.tensor_scalar(out=g1[:], in0=g1[:], scalar1=0.044715, scalar2=1.0, op0=ALU.mult, op1=ALU.add), reads=["g1"], writes=["g1"])
                            P.op("dve", lambda e: e.tensor_tensor(out=g1[:], in0=g1[:], in1=y_, op=ALU.mult), reads=["g1", ("psY", b)], writes=["g1"])
                            P.op("act", lambda e: e.activation(out=g2_[:], in_=g1[:], func=AF.Sigmoid, scale=1.5957691216057308), reads=["g1"], writes=["g2"])
                            P.op("dve", lambda e: e.tensor_tensor(out=ysv[:, cs, t_], in0=g2_[:], in1=y_, op=ALU.mult), reads=["g2", ("psY", b)], writes=["yst"])
                    P.dma("sp", ysT[ft * 128:(ft + 1) * 128, :], ys_t[:], reads=["yst"], writes=[("ysT", ft)])
                P.barrier()

        for l in range(nlayers):
            xsrc = x_in if l == 0 else xres
            xdst = y_out if l == nlayers - 1 else xres

            if "P" in phases:
              with ExitStack() as ph:
                condcol = SB(ph, "condcol", [128, 8], F32)
                condrep = SB(ph, "condrep", [128, 8, 128], F32)
                wc = [SB(ph, "wc%d" % i, [128, 8, 512], F32) for i in range(2)]
                bcb = [SB(ph, "bcb%d" % i, [128, 512], F32) for i in range(2)]
                modB = SB(ph, "modB", [128, 6 * D], F32)
                psm = [PSU(ph, "psm%d" % i, [128, 512]) for i in range(2)]
                pst = PSU(ph, "pst", [128, 48])
                P.dma("sp", condcol[:], ccol, writes=["condcol"])
                P.op("act", lambda e: e.activation(out=condcol[:], in_=condcol[:], func=AF.Silu), reads=["condcol"], writes=["condcol"])
                for k in range(8):
                    P.op("dve", lambda e: e.tensor_copy(out=condrep[:, k, :], in_=condcol[:, k:k + 1].to_broadcast([128, 128])),
                         reads=["condcol"], writes=["condrep"])
                for j in range(12):
                    b = j % 2
                    P.dma("sp", wc[b][:], w_cond[l, :, j * 512:(j + 1) * 512].rearrange("(k p) n -> p k n", p=128), writes=[("wc", b)])
                    P.dma("sp", bcb[b][:], b_cond[l, j * 512:(j + 1) * 512].partition_broadcast(128), writes=[("bcb", b)])
                    for k in range(8):
                        P.op("pe", lambda e: e.matmul(psm[b][:], lhsT=condrep[:, k, :], rhs=wc[b][:, k, :], start=(k == 0), stop=(k == 7)),
                             reads=["condrep", ("wc", b)], writes=[("psm", b)])
                    P.op("dve", lambda e: e.tensor_tensor(out=modB[:, j * 512:(j + 1) * 512], in0=psm[b][:], in1=bcb[b][:], op=ALU.add),
                         reads=[("psm", b), ("bcb", b)], writes=["modB"])
                for j in range(48):
                    P.op("pe", lambda e: e.matmul(pst[:, j:j + 1], lhsT=modB[:, j * 128:(j + 1) * 128], rhs=ident[:, 0:1], start=True, stop=True),
                         reads=["modB", "ident"], writes=["pst"])
                P.op("dve", lambda e: e.tensor_copy(out=modT[:], in_=pst[:]), reads=["pst"], writes=["modT"])
                for a in (8, 32):
                    P.op("dve", lambda e: e.tensor_scalar(out=modT[:, a:a + 8], in0=modT[:, a:a + 8], scalar1=1.0, scalar2=None, op0=ALU.add),
                         reads=["modT"], writes=["modT"])
                P.op("dve", lambda e: e.tensor_scalar(out=G12[:, 0, :], in0=modB[:, 2048:3072], scalar1=1.0, scalar2=None, op0=ALU.add),
                     reads=["modB"], writes=["G12"])
                P.op("dve", lambda e: e.tensor_scalar(out=G12[:, 1, :], in0=modB[:, 5120:6144], scalar1=1.0, scalar2=None, op0=ALU.add),
                     reads=["modB"], writes=["G12"])
                if "dbg_mod" in dbg and l == 0:
                    P.dma("sp", dbg_mod, modB[:], reads=["modB"])
                P.barrier()

            if "A" in phases:
              with ExitStack() as ph:
                wi = SB(ph, "wi", [128, 8, DIN], BF16)
                wr = SB(ph, "wr", [128, 8, 1856], BF16)
                xs = SB(ph, "xs", [128, 4, D], F32)
                uT = SB(ph, "uT", [128, 8, 512], BF16)
                rc = [SB(ph, "rc%d" % i, [128, 512], F32) for i in range(4)]
                t1 = SB(ph, "t1", [128, 512], F32)
                t2 = SB(ph, "t2", [128, 512], F32)
                stg = {}
                for nm, nt_ in (("ssm", 4), ("q", 8), ("k", 2), ("qi", 4), ("ki", 1), ("gs", 8), ("ga", 8)):
                    stg[nm] = SB(ph, "stg_" + nm, [128, nt_, 512], BF16)
                vst = SB(ph, "vst", [128, 4, 256], BF16)
                wst = SB(ph, "wst", [128, 4, 8], F32)
                ptp = [PSU(ph, "ptp%d" % i, [128, 512]) for i in range(2)]
                pz = [PSU(ph, "pz%d" % i, [128, 512]) for i in range(2)]
                pzr = [PSU(ph, "pzr%d" % i, [128, 512]) for i in range(2)]
                pv = PSU(ph, "pv", [128, 512])
                for k in range(8):
                    P.dma("pool", wi[:, k, :], w_in[l, k * 128:(k + 1) * 128, :], writes=["wi"])
                ro = 0
                rinfo = {}
                for nm, off, ncol, half in (("q", OFF_Q, 1024, 64), ("k", OFF_K, 256, 64), ("qi", OFF_QI, 512, 32), ("ki", OFF_KI, 64, 32)):
                    rinfo[nm] = ro
                    for k in range(8):
                        src = wi[:, k, off:off + ncol].rearrange("p (h two f) -> p h two f", two=2, f=half)
                        dst = wr[:, k, ro:ro + ncol].rearrange("p (h two f) -> p h two f", two=2, f=half)
                        P.op("pool", lambda e: e.tensor_copy(out=dst[:, :, 0, :], in_=src[:, :, 1, :]), reads=["wi"], writes=["wr"])
                        P.op("pool", lambda e: e.tensor_copy(out=dst[:, :, 1, :], in_=src[:, :, 0, :]), reads=["wi"], writes=["wr"])
                    ro += ncol
                zc = [0]

                def zbank():
                    zc[0] += 1
                    return zc[0] % 2

                for T in range(NS):
                    tsl = slice(T * 512, (T + 1) * 512)
                    P.dma("sp", xs[:], xsrc[tsl, :].rearrange("(s p) d -> p s d", p=128), writes=["xs"])
                    for i, tab in enumerate((ropeA_c, ropeA_s, ropeB_c, ropeB_s)):
                        P.dma("sp", rc[i][:], tab[:, tsl], writes=[("rc", i)])
                    for k in range(8):
                        b = k % 2
                        for s in range(4):
                            P.op("pe", lambda e: e.transpose(ptp[b][:, s * 128:(s + 1) * 128], xs[:, s, k * 128:(k + 1) * 128], ident[:]),
                                 reads=["xs", "ident"], writes=[("ptp", b)])
                        P.op("act", lambda e: e.activation(out=uT[:, k, :], in_=ptp[b][:], func=AF.Identity,
                                                           scale=modT[:, 8 + k:9 + k], bias=modT[:, k:k + 1]),
                             reads=[("ptp", b), "modT"], writes=["uT"])
                    for nm, off, ntl, rows, kind, dst in (("ssm", OFF_SSM, 4, 128, "copy", uTs), ("q", OFF_Q, 8, 128, "ropeA", qT),
                                                         ("k", OFF_K, 2, 128, "ropeA", kT), ("qi", OFF_QI, 4, 128, "ropeB", qiT),
                                                         ("ki", OFF_KI, 1, 64, "ropeB", kiT), ("gs", OFF_GS, 8, 128, "sig", gsT),
                                                         ("ga", OFF_GA, 8, 128, "sig", gaT)):
                        st = stg[nm]
                        for m in range(ntl):
                            b = zbank()
                            for k in range(8):
                                P.op("pe", lambda e: e.matmul(pz[b][0:rows, :], lhsT=wi[:, k, off + m * 128: off + m * 128 + rows], rhs=uT[:, k, :],
                                                              start=(k == 0), stop=(k == 7)),
                                     reads=["wi", "uT"], writes=[("pz", b)])
                            if kind.startswith("rope"):
                                ro = rinfo[nm]
                                ci, si = (0, 1) if kind == "ropeA" else (2, 3)
                                for k in range(8):
                                    P.op("pe", lambda e: e.matmul(pzr[b][0:rows, :], lhsT=wr[:, k, ro + m * 128: ro + m * 128 + rows], rhs=uT[:, k, :],
                                                                  start=(k == 0), stop=(k == 7)),
                                         reads=["wr", "uT"], writes=[("pzr", b)])
                                P.op("dve", lambda e: e.tensor_tensor(out=t1[0:rows, :], in0=pz[b][0:rows, :], in1=rc[ci][0:rows, :], op=ALU.mult),
                                     reads=[("pz", b), ("rc", ci)], writes=["t1"])
                                P.op("dve", lambda e: e.tensor_tensor(out=t2[0:rows, :], in0=pzr[b][0:rows, :], in1=rc[si][0:rows, :], op=ALU.mult),
                                     reads=[("pzr", b), ("rc", si)], writes=["t2"])
                                P.op("dve", lambda e: e.tensor_tensor(out=st[0:rows, m, :], in0=t1[0:rows, :], in1=t2[0:rows, :], op=ALU.add),
                                     reads=["t1", "t2"], writes=[("stg", nm)])
                            elif kind == "sig":
                                P.op("act", lambda e: e.activation(out=st[:, m, :], in_=pz[b][:], func=AF.Sigmoid),
                                     reads=[("pz", b)], writes=[("stg", nm)])
                            else:
                                P.op("act", lambda e: e.activation(out=st[:, m, :], in_=pz[b][:], func=AF.Copy),
                                     reads=[("pz", b)], writes=[("stg", nm)])
                        if rows == 128:
                            P.dma("pool", dst[:, tsl].rearrange("(m p) t -> p m t", p=128), st[:], reads=[("stg", nm)], writes=[(nm + "T", T)])
                        else:
                            P.dma("pool", dst[:, tsl], st[0:rows, 0, :], reads=[("stg", nm)], writes=[(nm + "T", T)])
                    for s in range(4):
                        for k in range(8):
                            P.op("pe", lambda e: e.matmul(pv[:, 0:256], lhsT=uT[:, k, s * 128:(s + 1) * 128], rhs=wi[:, k, OFF_V:OFF_V + 256],
                                                          start=(k == 0), stop=(k == 7)), reads=["uT", "wi"], writes=["pv"])
                        P.op("act", lambda e: e.activation(out=vst[:, s, :], in_=pv[:, 0:256], func=AF.Copy), reads=["pv"], writes=["vst"])
                        for k in range(8):
                            P.op("pe", lambda e: e.matmul(pv[:, 256:264], lhsT=uT[:, k, s * 128:(s + 1) * 128], rhs=wi[:, k, OFF_W:OFF_W + 8],
                                                          start=(k == 0), stop=(k == 7)), reads=["uT", "wi"], writes=["pv"])
                        P.op("dve", lambda e: e.tensor_copy(out=wst[:, s, :], in_=pv[:, 256:264]), reads=["pv"], writes=["wst"])
                    P.dma("pool", vd[tsl, :].rearrange("(s p) d -> p s d", p=128), vst[:], reads=["vst"], writes=[("vd", T)])
                    P.dma("pool", wid[tsl, :].rearrange("(s p) d -> p s d", p=128), wst[:], reads=["wst"], writes=[("wid", T)])
                P.barrier()

            if "B" in phases:
                phase_ssm(l)

            if "C" in phases:
                phase_dsa(l)

            if "D" in phases:
                phase_d(l, xsrc)
            if "E" in phases:
                phase_e(l, xdst)

        P.barrier(engines=("sp",))
        build.ninst = P.ninst
    return nc


def _rope_tables(L):
    pos = np.arange(L).astype(np.float32)
    out = []
    for half in (64, 32):
        inv = (np.float32(10000.0) ** (-(np.arange(half, dtype=np.float32)) / np.float32(half))).astype(np.float32)
        ang = (pos[None, :] * inv[:, None]).astype(np.float32)
        cos = np.cos(ang).astype(np.float32)
        sin = np.sin(ang).astype(np.float32)
        reps = 128 // half
        c = np.concatenate([cos] * reps, axis=0)
        s = np.concatenate([(-sin if (r % 2 == 0) else sin) for r in range(reps)], axis=0)
        out += [np.ascontiguousarray(c), np.ascontiguousarray(s)]
    return out


def _shared_inputs(inp, L):
    f = lambda a: np.ascontiguousarray(np.asarray(a, dtype=np.float32))
    sh = {}
    for k in ("w_cond", "b_cond", "w_in", "ssm_w_glu", "p_ssm", "p_attn", "w_out", "w_gate_up", "w_down"):
        sh[k] = f(inp[k])
    sh["bglu"] = f(np.asarray(inp["ssm_b_glu"]).reshape(DEPTH, 4, 128).transpose(2, 0, 1).reshape(128, DEPTH * 4))
    sh["dcol"] = f(np.asarray(inp["ssm_d"]).reshape(DEPTH, 4, 128).transpose(2, 0, 1).reshape(128, DEPTH * 4))
    sh["lnp"] = f(np.stack([inp["ln1_g"], inp["ln1_b"], inp["ln2_g"], inp["ln2_b"]], axis=1))
    lam_re, lam_im, ldt = np.asarray(inp["ssm_lam_re"]), np.asarray(inp["ssm_lam_im"]), np.asarray(inp["ssm_log_dt"])
    b_re, b_im = np.asarray(inp["ssm_b_re"]), np.asarray(inp["ssm_b_im"])
    c_re, c_im = np.asarray(inp["ssm_c_re"]), np.asarray(inp["ssm_c_im"])

    def Rl(a):
        a = a.reshape(DEPTH, 4, 8, 1, 64).transpose(0, 2, 3, 1, 4)
        return f(np.broadcast_to(a, (DEPTH, 8, 16, 4, 64)).reshape(DEPTH, 128, 256))

    def Sl(a):
        return f(a.reshape(DEPTH, 16, 2, 64).transpose(0, 2, 3, 1).reshape(DEPTH, 128, 16))

    sh["lamR_re"], sh["lamR_im"] = Rl(lam_re), Rl(lam_im)
    sh["ldtR"] = Rl(np.broadcast_to(ldt[:, :, None], (DEPTH, 32, 64)))
    sh["lamS_re"], sh["lamS_im"] = Sl(lam_re), Sl(lam_im)
    sh["ldtS"] = Sl(np.broadcast_to(ldt[:, :, None], (DEPTH, 32, 64)))
    for nm, a in (("bR_re", b_re), ("bR_im", b_im)):
        sh[nm] = f(a.reshape(DEPTH, 4, 8, 64, 16).transpose(0, 2, 4, 1, 3).reshape(DEPTH, 128, 256))
    for nm, a in (("bS_re", b_re), ("bS_im", b_im)):
        sh[nm] = f(a.reshape(DEPTH, 16, 2, 64, 16).transpose(0, 2, 3, 1, 4).reshape(DEPTH, 128, 256))
    for nm, a in (("cS_re", c_re), ("cS_im", c_im)):
        sh[nm] = f(a.reshape(DEPTH, 16, 2, 16, 64).transpose(0, 2, 4, 1, 3).reshape(DEPTH, 128, 256))
    sh["ident"] = np.eye(128, dtype=np.float32)
    qq = np.arange(128)[:, None]
    ss = np.arange(128)[None, :]
    sh["causal"] = np.where(ss <= qq, 0.0, NEG).astype(np.float32)
    sh["rowmask"] = (np.arange(128)[:, None] // 16 == np.arange(8)[None, :]).astype(np.float32)
    ra_c, ra_s, rb_c, rb_s = _rope_tables(L)
    sh["ropeA_c"], sh["ropeA_s"], sh["ropeB_c"], sh["ropeB_s"] = ra_c, ra_s, rb_c, rb_s
    return sh


def make_in_maps(inp, L, nb):
    sh = _shared_inputs(inp, L)
    x = np.asarray(inp["x"], dtype=np.float32)
    c = np.asarray(inp["c"], dtype=np.float32)
    maps = []
    for b in range(nb):
        m = dict(sh)
        m["x"] = np.ascontiguousarray(x[b])
        m["ccol"] = np.ascontiguousarray(c[b].reshape(8, 128).T)
        maps.append(m)
    return maps


_NC_CACHE = {}


def kernel(**inputs):
    x = np.asarray(inputs["x"])
    B, L, _ = x.shape
    if L not in _NC_CACHE:
        _NC_CACHE[L] = build(L)
    nc = _NC_CACHE[L]
    maps = make_in_maps(inputs, L, B)
    res = run_bass_kernel_spmd(nc, maps, core_ids=list(range(B)))
    return np.stack([np.asarray(r["y"]) for r in res.results], axis=0).astype(np.float32)
```
